# Optimizing a Trainium2 kernel written in Bass

```python
import math
import jax
import jax.numpy as jnp
from jax import lax
import numpy as np

D_MODEL = 1024
BATCH = 4
SEQ = 4096
DEPTH = 4
DEC_BATCH = 32
DEC_SEQ = 1
PAST_LEN = 8192
PAGE_SIZE = 128

N_MIXERS = 3
N_A_LAYERS = len(range(0, DEPTH, N_MIXERS))
N_B_LAYERS = len(range(1, DEPTH, N_MIXERS))
N_C_LAYERS = len(range(2, DEPTH, N_MIXERS))

A_WINDOWS = (128, 512, 2048)
A_DILATIONS = (1, 4, 16)
A_GROUPS = 3
A_SLOTS = 8
A_HEAD_DIM = 64
A_OUT = A_SLOTS * A_HEAD_DIM
A_QKV = 3 * A_GROUPS * A_SLOTS * A_HEAD_DIM
A_QBLOCK = 64

B_HEADS = 4
B_DK = D_MODEL // (2 * B_HEADS)
B_DV = D_MODEL // B_HEADS
B_GATE_RANK = 16
B_GATE_TAU = 16.0
B_CHUNK = 64
B_SIZES = (B_HEADS * B_DK, B_HEADS * B_DK, B_HEADS * B_DV, B_GATE_RANK, B_HEADS * B_DV)

C_HEADS = 16
C_KV_HEADS = 4
C_HEAD_DIM = 64
C_CMP_BLOCK = 32
C_CMP_STRIDE = 16
C_SEL_BLOCK = 64
C_TOPK = 16
C_WINDOW = 512
C_CMP_HIDDEN = 128
C_QBLOCK = 64
C_SIZES = (C_HEADS * C_HEAD_DIM,) + (C_KV_HEADS * C_HEAD_DIM,) * 6 + (3 * C_HEADS,)

MEM_LEN = 256
MEM_HEADS = 4
MEM_HEAD_DIM = D_MODEL // MEM_HEADS

FFN_HIDDEN = ((8 * D_MODEL + 3 * 256 - 1) // (3 * 256)) * 256

RMS_EPS = 1e-6
NEG_INF = -1e30
FORCE = 1e30
TINY = 1e-30

kernel_name = 'hybrid_dilated_gla_nsa_decoder_step'


def rmsnorm(x, g):
    xf = x.astype(jnp.float32)
    y = xf * lax.rsqrt(jnp.mean(xf * xf, axis=-1, keepdims=True) + RMS_EPS)
    return (y * g.astype(jnp.float32)).astype(x.dtype)


def split_sizes(x, sizes):
    return jnp.split(x, [int(c) for c in np.cumsum(sizes)[:-1]], axis=-1)


def alibi_slopes(n):
    return jnp.asarray(2.0 ** (-8.0 * np.arange(1, n + 1) / n), dtype=jnp.float32)


def masked_softmax(s, mask):
    s = jnp.where(mask, s, NEG_INF)
    m = jnp.max(s, axis=-1, keepdims=True)
    e = jnp.where(mask, jnp.exp(s - m), 0.0)
    den = jnp.maximum(jnp.sum(e, axis=-1, keepdims=True), TINY)
    return e / den, (m + jnp.log(den))[..., 0]


def swiglu(h, w_in, w_out):
    a, b = jnp.split(h @ w_in, 2, axis=-1)
    return (jax.nn.silu(a) * b) @ w_out


def cross_attention(h, mem_kv, w_q, w_o):
    B, L, _ = h.shape
    q = (h @ w_q).reshape(B, L, MEM_HEADS, MEM_HEAD_DIM)
    s = jnp.einsum('blhd,bmhd->bhlm', q, mem_kv[:, :, 0]).astype(jnp.float32) * MEM_HEAD_DIM ** -0.5
    p = jax.nn.softmax(s, axis=-1)
    o = jnp.einsum('bhlm,bmhd->blhd', p.astype(h.dtype), mem_kv[:, :, 1])
    return o.reshape(B, L, MEM_HEADS * MEM_HEAD_DIM) @ w_o


def dilated_attention(q, ks, vs, starts, slopes):
    B, Lq = q.shape[:2]
    qb = math.gcd(Lq, A_QBLOCK)
    scale = A_HEAD_DIM ** -0.5

    def block(i):
        q_blk = lax.dynamic_slice_in_dim(q, i * qb, qb, axis=1)
        outs, lses = [], []
        for g in range(A_GROUPS):
            d = A_DILATIONS[g]
            dist = d * jnp.arange(A_WINDOWS[g] // d + 1)
            t = starts[g] + i * qb + jnp.arange(qb)
            kpos = t[:, None] - dist[None, :]
            kidx = jnp.clip(kpos, 0, ks[g].shape[1] - 1)
            kg = jnp.take(ks[g], kidx, axis=1)
            vg = jnp.take(vs[g], kidx, axis=1)
            s = jnp.einsum('bqsd,bqksd->bsqk', q_blk[:, :, g], kg).astype(jnp.float32) * scale
            s = s - slopes[g][None, :, None, None] * dist.astype(jnp.float32)
            p, lse = masked_softmax(s, (kpos >= 0)[None, None])
            outs.append(jnp.einsum('bsqk,bqksd->bqsd', p.astype(vg.dtype), vg))
            lses.append(lse)
        w = jax.nn.softmax(jnp.stack(lses), axis=0)
        o = jnp.einsum('gbsq,gbqsd->bqsd', w.astype(q.dtype), jnp.stack(outs))
        return o.reshape(B, qb, A_OUT)

    o = lax.map(block, jnp.arange(Lq // qb))
    return jnp.moveaxis(o, 0, 1).reshape(B, Lq, A_OUT)


def mixer_dilated(h, w_qkv, w_o, slopes, bufs):
    B, L, _ = h.shape
    qkv = (h @ w_qkv).reshape(B, L, 3, A_GROUPS, A_SLOTS, A_HEAD_DIM)
    q, k, v = qkv[:, :, 0], qkv[:, :, 1], qkv[:, :, 2]
    ks, vs, starts, new_state = [], [], [], []
    for g in range(A_GROUPS):
        kg, vg = k[:, :, g], v[:, :, g]
        kv_new = jnp.stack([kg, vg], axis=2)
        if bufs is None:
            ks.append(kg)
            vs.append(vg)
            starts.append(0)
            new_state.append(kv_new[:, -min(A_WINDOWS[g], L):])
        else:
            buf = bufs[g]
            ks.append(jnp.concatenate([buf[:, :, 0], kg], axis=1))
            vs.append(jnp.concatenate([buf[:, :, 1], vg], axis=1))
            starts.append(buf.shape[1])
            new_state.append(kv_new)
    o = dilated_attention(q, ks, vs, starts, slopes)
    return o @ w_o, new_state


def gla_recurrence(q, k, v, log_a, s0):
    B, L, H, Dk = q.shape
    Dv = v.shape[-1]
    c = math.gcd(L, B_CHUNK)
    n = L // c
    f32 = jnp.float32

    def chunks(a):
        return jnp.moveaxis(a.astype(f32).reshape(B, n, c, H, a.shape[-1]), 1, 0)

    causal = jnp.tril(jnp.ones((c, c), dtype=bool))[None, :, :, None, None]

    def step(S, inp):
        qc, kc, vc, la = inp
        b = jnp.cumsum(la, axis=1)
        o_inter = jnp.einsum('bihk,bhkv->bihv', qc * jnp.exp(b), S)
        decay = jnp.exp(jnp.where(causal, b[:, :, None] - b[:, None, :], -jnp.inf))
        attn = jnp.einsum('bihk,bjhk,bijhk->bhij', qc, kc, decay)
        o = o_inter + jnp.einsum('bhij,bjhv->bihv', attn, vc)
        b_last = b[:, -1]
        S = jnp.exp(b_last)[..., None] * S + jnp.einsum('bjhk,bjhv->bhkv', kc * jnp.exp(b_last[:, None] - b), vc)
        return S, o

    S, o = lax.scan(step, s0.astype(f32), (chunks(q), chunks(k), chunks(v), chunks(log_a)))
    return jnp.moveaxis(o, 0, 1).reshape(B, L, H, Dv), S


def mixer_gla(h, w_in, w_gate2, b_gate, g_head, w_o, s0):
    B, L, _ = h.shape
    q, k, v, g_low, r = split_sizes(h @ w_in, B_SIZES)
    q = q.reshape(B, L, B_HEADS, B_DK) * B_DK ** -0.5
    k = k.reshape(B, L, B_HEADS, B_DK)
    v = v.reshape(B, L, B_HEADS, B_DV)
    log_a = jax.nn.log_sigmoid((g_low @ w_gate2 + b_gate).astype(jnp.float32)) / B_GATE_TAU
    o, S = gla_recurrence(q, k, v, log_a.reshape(B, L, B_HEADS, B_DK), s0)
    o = rmsnorm(o.astype(h.dtype), g_head).reshape(B, L, B_HEADS * B_DV) * jax.nn.silu(r)
    return o @ w_o, S


def compress(x, pos_emb, w1, w2):
    B, Lp, G, Dh = x.shape
    r = C_CMP_BLOCK // C_CMP_STRIDE
    n_chunks = Lp // C_CMP_STRIDE
    n_cmp = n_chunks - r + 1
    xs = x.reshape(B, n_chunks, C_CMP_STRIDE, G, Dh)
    blocks = jnp.concatenate([xs[:, j:j + n_cmp] for j in range(r)], axis=2)
    blocks = blocks + pos_emb[None, None, :, None, :]
    flat = jnp.moveaxis(blocks, 3, 2).reshape(B, n_cmp, G, C_CMP_BLOCK * Dh)
    return jax.nn.gelu(flat @ w1) @ w2


def nsa_attention(q, q_pos0, kc, vc, ks, vs, kw, vw, win_start, gates, slopes):
    B, Lq = q.shape[:2]
    G, hpg, Dh, SEL = C_KV_HEADS, C_HEADS // C_KV_HEADS, C_HEAD_DIM, C_SEL_BLOCK
    n_cmp, n_slc = kc.shape[1], ks.shape[1] // SEL
    topk = min(C_TOPK, n_slc)
    qb = math.gcd(Lq, C_QBLOCK)
    scale = Dh ** -0.5
    f32 = jnp.float32
    qg = q.reshape(B, Lq, G, hpg, Dh)
    gg = gates.reshape(B, Lq, G, hpg, 3).astype(q.dtype)
    sl = slopes.reshape(G, hpg)[None, :, :, None, None]
    cmp_end = jnp.arange(n_cmp) * C_CMP_STRIDE + (C_CMP_BLOCK - 1)
    blk = jnp.arange(n_slc)
    c_start = jnp.arange(n_cmp)[:, None] * C_CMP_STRIDE
    cover = ((c_start < (blk[None, :] + 1) * SEL) & (c_start + C_CMP_BLOCK > blk[None, :] * SEL)).astype(f32)
    ksb = jnp.moveaxis(ks.reshape(B, n_slc, SEL, G, Dh), 3, 1)
    vsb = jnp.moveaxis(vs.reshape(B, n_slc, SEL, G, Dh), 3, 1)
    kwp = jnp.pad(kw, ((0, 0), (C_WINDOW, 0), (0, 0), (0, 0)))
    vwp = jnp.pad(vw, ((0, 0), (C_WINDOW, 0), (0, 0), (0, 0)))
    bi = jnp.arange(B)[:, None, None]
    gi = jnp.arange(G)[None, :, None]

    def block(i):
        q0 = q_pos0 + i * qb
        qi = lax.dynamic_slice_in_dim(qg, i * qb, qb, axis=1)
        gt = lax.dynamic_slice_in_dim(gg, i * qb, qb, axis=1)
        t = q0 + jnp.arange(qb)
        tf = t.astype(f32)
        s = jnp.einsum('bqghd,bcgd->bghqc', qi, kc).astype(f32) * scale - sl * (tf[:, None] - cmp_end.astype(f32))
        p_c, _ = masked_softmax(s, cmp_end[None, :] <= t[:, None])
        o_c = jnp.einsum('bghqc,bcgd->bqghd', p_c.astype(vc.dtype), vc)
        imp = jnp.einsum('bghqc,cj->bgqj', p_c, cover)
        cur = (t // SEL)[:, None]
        forced = (blk == 0) | (blk == cur) | (blk == cur - 1)
        score = jnp.where(blk * SEL <= t[:, None], jnp.where(forced, FORCE, imp), NEG_INF)
        top_s, top_j = lax.top_k(score, topk)
        idx = top_j.reshape(B, G, qb * topk)
        kb = ksb[bi, gi, idx].reshape(B, G, qb, topk * SEL, Dh)
        vb = vsb[bi, gi, idx].reshape(B, G, qb, topk * SEL, Dh)
        kpos = (top_j[..., None] * SEL + jnp.arange(SEL)).reshape(B, G, qb, topk * SEL)
        ok = jnp.repeat(top_s > 0.5 * NEG_INF, SEL, axis=-1) & (kpos <= t[:, None])
        s = jnp.einsum('bqghd,bgqkd->bghqk', qi, kb).astype(f32) * scale - sl * (tf[:, None] - kpos.astype(f32))[:, :, None]
        p_s, _ = masked_softmax(s, ok[:, :, None])
        o_s = jnp.einsum('bghqk,bgqkd->bqghd', p_s.astype(vb.dtype), vb)
        kwin = lax.dynamic_slice_in_dim(kwp, q0 - win_start, C_WINDOW + qb, axis=1)
        vwin = lax.dynamic_slice_in_dim(vwp, q0 - win_start, C_WINDOW + qb, axis=1)
        wpos = q0 - C_WINDOW + jnp.arange(C_WINDOW + qb)
        okw = (wpos[None, :] <= t[:, None]) & (wpos[None, :] >= t[:, None] - C_WINDOW) & (wpos[None, :] >= win_start)
        s = jnp.einsum('bqghd,bkgd->bghqk', qi, kwin).astype(f32) * scale - sl * (tf[:, None] - wpos.astype(f32))
        p_w, _ = masked_softmax(s, okw)
        o_w = jnp.einsum('bghqk,bkgd->bqghd', p_w.astype(vwin.dtype), vwin)
        o = gt[..., 0:1] * o_c + gt[..., 1:2] * o_s + gt[..., 2:3] * o_w
        return o.reshape(B, qb, C_HEADS * Dh)

    o = lax.map(block, jnp.arange(Lq // qb))
    return jnp.moveaxis(o, 0, 1).reshape(B, Lq, C_HEADS * Dh)


def mixer_nsa(h, w_in, b_gate, pos_k, pos_v, wk1, wk2, wv1, wv2, w_o, slopes, past, win_buf, q_pos0):
    B, L, _ = h.shape
    q, kc, vc, ks, vs, kw, vw, g = split_sizes(h @ w_in, C_SIZES)
    rows = lambda a: a.reshape(B, L, C_KV_HEADS, C_HEAD_DIM)
    new_rows = jnp.stack([rows(kc), rows(vc), rows(ks), rows(vs)], axis=2)
    win_rows = jnp.stack([rows(kw), rows(vw)], axis=2)
    gates = jax.nn.sigmoid((g + b_gate).astype(jnp.float32)).reshape(B, L, C_HEADS, 3)
    if past is None:
        full, win, win_start = new_rows, win_rows, 0
        win_state = win_rows[:, -min(C_WINDOW, L):]
    else:
        full = jnp.concatenate([past, new_rows], axis=1)
        win = jnp.concatenate([win_buf, win_rows], axis=1)
        win_start = q_pos0 - win_buf.shape[1]
        win_state = win_rows
    n_tot = full.shape[1]
    n_pad = -(-n_tot // C_SEL_BLOCK) * C_SEL_BLOCK - n_tot
    full = jnp.pad(full, ((0, 0), (0, n_pad), (0, 0), (0, 0), (0, 0)))
    k_cmp = compress(full[:, :, 0], pos_k, wk1, wk2)
    v_cmp = compress(full[:, :, 1], pos_v, wv1, wv2)
    o = nsa_attention(q.reshape(B, L, C_HEADS, C_HEAD_DIM), q_pos0, k_cmp, v_cmp, full[:, :, 2], full[:, :, 3],
                      win[:, :, 0], win[:, :, 1], win_start, gates, slopes)
    return o @ w_o, new_rows, win_state


def setup_inputs(seed: int = 0) -> dict:
    key = jax.random.key(seed)
    keys = iter(jax.random.split(key, 64))

    def nrm(shape, scale=1.0):
        return scale * jax.random.normal(next(keys), shape, jnp.float32)

    def dense(shape):
        return nrm(shape, shape[-2] ** -0.5)

    def gain(shape):
        return 1.0 + nrm(shape, 0.05)

    n_pages = PAST_LEN // PAGE_SIZE
    n_used = DEC_BATCH * n_pages
    n_pool = n_used + max(1, n_used // 4)
    kvh, dh = C_KV_HEADS, C_HEAD_DIM
    return {
        'x_prompt': nrm((BATCH, SEQ, D_MODEL)),
        'x_sample': nrm((DEC_BATCH, DEC_SEQ, D_MODEL)),
        'cache_dil_w128': nrm((N_A_LAYERS, DEC_BATCH, min(A_WINDOWS[0], PAST_LEN), 2, A_SLOTS, A_HEAD_DIM)),
        'cache_dil_w512': nrm((N_A_LAYERS, DEC_BATCH, min(A_WINDOWS[1], PAST_LEN), 2, A_SLOTS, A_HEAD_DIM)),
        'cache_dil_w2048': nrm((N_A_LAYERS, DEC_BATCH, min(A_WINDOWS[2], PAST_LEN), 2, A_SLOTS, A_HEAD_DIM)),
        'state_gla': nrm((N_B_LAYERS, DEC_BATCH, B_HEADS, B_DK, B_DV)),
        'cache_nsa_win': nrm((N_C_LAYERS, DEC_BATCH, min(C_WINDOW, PAST_LEN), 2, kvh, dh)),
        'cache_nsa_kv': nrm((N_C_LAYERS, n_pool, PAGE_SIZE, 4, kvh, dh)),
        'cache_mem_kv': nrm((DEPTH, DEC_BATCH, MEM_LEN, 2, MEM_HEADS, MEM_HEAD_DIM)),
        'page_table': jax.random.permutation(next(keys), n_pool)[:n_used].reshape(DEC_BATCH, n_pages).astype(jnp.int32),
        'mem_prompt': nrm((BATCH, MEM_LEN, D_MODEL)),
        'g_mix': gain((DEPTH, D_MODEL)),
        'g_cross': gain((DEPTH, D_MODEL)),
        'g_mem': gain((DEPTH, D_MODEL)),
        'g_ffn': gain((DEPTH, D_MODEL)),
        'g_final': gain((D_MODEL,)),
        'w_a_qkv': dense((N_A_LAYERS, D_MODEL, A_QKV)),
        'w_a_o': dense((N_A_LAYERS, A_OUT, D_MODEL)),
        'w_b_in': dense((N_B_LAYERS, D_MODEL, sum(B_SIZES))),
        'w_b_gate2': dense((N_B_LAYERS, B_GATE_RANK, B_HEADS * B_DK)),
        'b_b_gate': nrm((N_B_LAYERS, B_HEADS * B_DK), 0.1),
        'g_b_head': gain((N_B_LAYERS, B_HEADS, B_DV)),
        'w_b_o': dense((N_B_LAYERS, B_HEADS * B_DV, D_MODEL)),
        'w_c_in': dense((N_C_LAYERS, D_MODEL, sum(C_SIZES))),
        'b_c_gate': nrm((N_C_LAYERS, 3 * C_HEADS), 0.1),
        'c_pos_k': nrm((N_C_LAYERS, C_CMP_BLOCK, dh), 0.5),
        'c_pos_v': nrm((N_C_LAYERS, C_CMP_BLOCK, dh), 0.5),
        'w_c_k1': dense((N_C_LAYERS, C_CMP_BLOCK * dh, C_CMP_HIDDEN)),
        'w_c_k2': dense((N_C_LAYERS, C_CMP_HIDDEN, dh)),
        'w_c_v1': dense((N_C_LAYERS, C_CMP_BLOCK * dh, C_CMP_HIDDEN)),
        'w_c_v2': dense((N_C_LAYERS, C_CMP_HIDDEN, dh)),
        'w_c_o': dense((N_C_LAYERS, C_HEADS * dh, D_MODEL)),
        'w_x_q': dense((DEPTH, D_MODEL, MEM_HEADS * MEM_HEAD_DIM)),
        'w_x_kv': dense((DEPTH, D_MODEL, 2 * MEM_HEADS * MEM_HEAD_DIM)),
        'w_x_o': dense((DEPTH, MEM_HEADS * MEM_HEAD_DIM, D_MODEL)),
        'w_ffn_in': dense((DEPTH, D_MODEL, 2 * FFN_HIDDEN)),
        'w_ffn_out': dense((DEPTH, FFN_HIDDEN, D_MODEL)),
    }


def reference(x_prompt, x_sample, cache_dil_w128, cache_dil_w512, cache_dil_w2048, state_gla, cache_nsa_win,
              cache_nsa_kv, cache_mem_kv, page_table, mem_prompt, g_mix, g_cross, g_mem, g_ffn, g_final,
              w_a_qkv, w_a_o, w_b_in, w_b_gate2, b_b_gate, g_b_head, w_b_o, w_c_in, b_c_gate, c_pos_k, c_pos_v,
              w_c_k1, w_c_k2, w_c_v1, w_c_v2, w_c_o, w_x_q, w_x_kv, w_x_o, w_ffn_in, w_ffn_out):
    slopes_a = alibi_slopes(A_GROUPS * A_SLOTS).reshape(A_GROUPS, A_SLOTS)
    slopes_c = alibi_slopes(C_HEADS)

    def trunk(x, q_pos0, mem_kvs, dil_bufs, gla_s0s, nsa_pasts, nsa_wins):
        new_dil, new_gla, new_rows, new_win = [], [], [], []
        for l in range(DEPTH):
            kind, j = l % N_MIXERS, l // N_MIXERS
            h = rmsnorm(x, g_mix[l])
            if kind == 0:
                o, st = mixer_dilated(h, w_a_qkv[j], w_a_o[j], slopes_a, dil_bufs[j])
                new_dil.append(st)
            elif kind == 1:
                o, st = mixer_gla(h, w_b_in[j], w_b_gate2[j], b_b_gate[j], g_b_head[j], w_b_o[j], gla_s0s[j])
                new_gla.append(st)
            else:
                o, rows, win = mixer_nsa(h, w_c_in[j], b_c_gate[j], c_pos_k[j], c_pos_v[j], w_c_k1[j], w_c_k2[j],
                                         w_c_v1[j], w_c_v2[j], w_c_o[j], slopes_c, nsa_pasts[j], nsa_wins[j], q_pos0)
                new_rows.append(rows)
                new_win.append(win)
            x = x + o
            x = x + cross_attention(rmsnorm(x, g_cross[l]), mem_kvs[l], w_x_q[l], w_x_o[l])
            x = x + swiglu(rmsnorm(x, g_ffn[l]), w_ffn_in[l], w_ffn_out[l])
        return rmsnorm(x, g_final), new_dil, new_gla, new_rows, new_win

    bp = x_prompt.shape[0]
    mem_kv_p = [(rmsnorm(mem_prompt, g_mem[l]) @ w_x_kv[l]).reshape(bp, mem_prompt.shape[1], 2, MEM_HEADS, MEM_HEAD_DIM)
                for l in range(DEPTH)]
    zero_state = jnp.zeros((bp, B_HEADS, B_DK, B_DV), jnp.float32)
    y_prompt, dil_p, gla_p, rows_p, win_p = trunk(
        x_prompt, 0, mem_kv_p, [None] * N_A_LAYERS, [zero_state] * N_B_LAYERS, [None] * N_C_LAYERS, [None] * N_C_LAYERS)

    bs = x_sample.shape[0]
    nsa_pasts = [cache_nsa_kv[j][page_table].reshape(bs, -1, 4, C_KV_HEADS, C_HEAD_DIM) for j in range(N_C_LAYERS)]
    y_sample, dil_s, gla_s, rows_s, win_s = trunk(
        x_sample, PAST_LEN, [cache_mem_kv[l] for l in range(DEPTH)],
        [[cache_dil_w128[j], cache_dil_w512[j], cache_dil_w2048[j]] for j in range(N_A_LAYERS)],
        [state_gla[j] for j in range(N_B_LAYERS)], nsa_pasts, [cache_nsa_win[j] for j in range(N_C_LAYERS)])

    dil128_prompt = jnp.stack([st[0] for st in dil_p])
    dil128_sample = jnp.stack([st[0] for st in dil_s])
    dil512_prompt = jnp.stack([st[1] for st in dil_p])
    dil512_sample = jnp.stack([st[1] for st in dil_s])
    dil2048_prompt = jnp.stack([st[2] for st in dil_p])
    dil2048_sample = jnp.stack([st[2] for st in dil_s])
    gla_prompt = jnp.stack(gla_p)
    gla_sample = jnp.stack(gla_s)
    nsa_win_prompt = jnp.stack(win_p)
    nsa_win_sample = jnp.stack(win_s)
    nsa_kv_prompt = jnp.stack(rows_p)
    nsa_kv_sample = jnp.stack(rows_s)
    mem_kv_prompt = jnp.stack(mem_kv_p)
    return (y_prompt, y_sample, dil128_prompt, dil128_sample, dil512_prompt, dil512_sample, dil2048_prompt,
            dil2048_sample, gla_prompt, gla_sample, nsa_win_prompt, nsa_win_sample, nsa_kv_prompt, nsa_kv_sample,
            mem_kv_prompt)
```

```python
import numpy as np
from contextlib import ExitStack
import concourse.bass as bass
import concourse.mybir as mybir
from concourse.bass_utils import run_bass_kernel_spmd

F32 = mybir.dt.float32
BF16 = mybir.dt.bfloat16
I32 = mybir.dt.int32
AF = mybir.ActivationFunctionType
ALU = mybir.AluOpType
AX = mybir.AxisListType

ENGS = ("pe", "act", "dve", "pool", "sp")
SEM_EPOCH = 20000
N_DMA_SEMS = 12

D = 1024
DEPTH = 4
FH = 2816
MEM = 256
NCORES = 8


class Prog:
    def __init__(self, nc, stack, same_engine_sync=True):
        self.nc = nc
        self.stack = stack
        self.same_engine_sync = same_engine_sync
        self.streams = {e: [] for e in ENGS}
        self.count = {e: 0 for e in ENGS}
        self.eng_sems = {e: [] for e in ENGS}
        self.dma_sems = []
        self.dma_count = []
        self.dma_rr = 0
        self.clock = {e: {} for e in ENGS}
        self.last_write = {}
        self.readers = {}
        self.n_sem = 0
        self.n_ops = 0
        for _ in range(N_DMA_SEMS):
            self.dma_sems.append(self._sem())
            self.dma_count.append(0)

    def _sem(self):
        self.n_sem += 1
        return self.stack.enter_context(self.nc.semaphore("s%d" % self.n_sem))

    def _deps(self, reads, writes):
        deps = {}

        def add(t):
            tr, v = t
            if deps.get(tr, 0) < v:
                deps[tr] = v
        for k in reads:
            if k in self.last_write:
                add(self.last_write[k])
        for k in writes:
            if k in self.last_write:
                add(self.last_write[k])
            for r in self.readers.get(k, ()):
                add(r)
        return deps

    def _commit(self, token, reads, writes):
        for k in writes:
            self.last_write[k] = token
            self.readers[k] = []
        for k in reads:
            self.readers.setdefault(k, []).append(token)

    def _waits(self, eng, deps):
        waits = []
        clk = self.clock[eng]
        for tr, v in deps.items():
            if tr == eng and (eng == "pe" or not self.same_engine_sync):
                continue
            if clk.get(tr, 0) >= v:
                continue
            clk[tr] = v
            waits.append((tr, v))
        return waits

    def _sem_for(self, tr, v):
        if isinstance(tr, tuple):
            return self.dma_sems[tr[1]], v * 16
        ep = (v - 1) // SEM_EPOCH
        while len(self.eng_sems[tr]) <= ep:
            self.eng_sems[tr].append(self._sem())
        return self.eng_sems[tr][ep], v - ep * SEM_EPOCH

    def op(self, eng, fn, reads=(), writes=(), **kw):
        fn = (fn, kw)
        deps = self._deps(reads, writes)
        waits = self._waits(eng, deps)
        self.count[eng] += 1
        seq = self.count[eng]
        sem, _ = self._sem_for(eng, seq)
        self.streams[eng].append(("op", waits, fn, sem))
        self._commit((eng, seq), reads, writes)
        self.n_ops += 1

    def dma(self, out, in_, reads=(), writes=(), eng="sp", **kw):
        eng = "sp"
        i = self.dma_rr
        self.dma_rr = (self.dma_rr + 1) % N_DMA_SEMS
        tr = ("dma", i)
        deps = self._deps(reads, writes)
        if self.dma_count[i] > 0:
            v = self.dma_count[i]
            if deps.get(tr, 0) < v:
                deps[tr] = v
        waits = self._waits(eng, deps)
        self.dma_count[i] += 1
        self.streams[eng].append(("dma", waits, (out, in_, kw), self.dma_sems[i]))
        self._commit((tr, self.dma_count[i]), reads, writes)
        self.n_ops += 1

    def barrier(self):
        deps = {}
        for i in range(N_DMA_SEMS):
            if self.dma_count[i] > 0:
                deps[("dma", i)] = self.dma_count[i]
        for e in ENGS:
            if self.count[e] > 0:
                deps[e] = self.count[e]
        for eng in ENGS:
            d = {k: v for k, v in deps.items() if k != eng}
            waits = self._waits(eng, d)
            if waits:
                self.streams[eng].append(("wait", waits, None, None))

    def emit(self):
        nc = self.nc
        for e in ENGS:
            for item in self.streams[e]:
                for tr, v in item[1]:
                    self._sem_for(tr, v)
        streams = self.streams
        self.streams = {e: [] for e in ENGS}
        with nc.Block() as block:
            def run(engine, items):
                for kind, waits, fn, sem in items:
                    for tr, v in waits:
                        s, val = self._sem_for(tr, v)
                        engine.wait_ge(s, val)
                    if kind == "op":
                        getattr(engine, fn[0])(**fn[1]).then_inc(sem, 1)
                    elif kind == "dma":
                        out, in_, kw = fn
                        engine.dma_start(out=out, in_=in_, **kw).then_inc(sem, 16)
                    elif kind == "idma":
                        engine.indirect_dma_start(**fn).then_inc(sem, 16)
                    elif kind == "cc":
                        engine.collective_compute(**fn).then_inc(sem, 16)

            @block.tensor
            def _(e):
                run(e, streams["pe"])

            @block.scalar
            def _(e):
                run(e, streams["act"])

            @block.vector
            def _(e):
                run(e, streams["dve"])

            @block.gpsimd
            def _(e):
                run(e, streams["pool"])

            @block.sync
            def _(e):
                run(e, streams["sp"])


class Phase:
    def __init__(self, B, name):
        self.B = B
        self.P = B.P
        self.nc = B.nc
        self.name = name
        self.stack = ExitStack()
        self.n = 0

    def __enter__(self):
        self.stack.__enter__()
        return self

    def __exit__(self, *a):
        self.P.barrier()
        self.P.emit()
        return self.stack.__exit__(*a)

    def sbuf(self, name, shape, dtype):
        self.n += 1
        nm = "%s_%s" % (self.name, name)
        t = self.stack.enter_context(self.nc.sbuf_tensor(nm, list(shape), dtype))
        return t

    def psum(self, name, shape, dtype=F32):
        nm = "%s_%s" % (self.name, name)
        return self.stack.enter_context(self.nc.psum_tensor(nm, list(shape), dtype))


class Rot:
    def __init__(self, tiles, name):
        self.tiles = tiles
        self.name = name
        self.i = 0

    def next(self):
        t = self.tiles[self.i % len(self.tiles)]
        k = "%s#%d" % (self.name, self.i % len(self.tiles))
        self.i += 1
        return t, k


class Builder:
    def __init__(self, L=4096, NS=4, layers=(0, 1, 2, 3), dbg=False, parts=("mix", "cross", "ffn")):
        self.L = L
        self.NS = NS
        self.layers = layers
        self.dbg = dbg
        self.parts = parts
        self.nc = bass.Bass("TRN2", target_bir_lowering=False)
        self.ins = {}
        self.outs = {}
        self._rr = 0
        self.dbg_skip = set()
        self.nsa_sample = False
        self.n_pool = 2560

    def inp(self, name, shape, dtype=F32):
        t = self.nc.dram_tensor(name, list(shape), dtype, kind="ExternalInput").ap()
        self.ins[name] = t
        return t

    def out(self, name, shape, dtype=F32):
        t = self.nc.dram_tensor(name, list(shape), dtype, kind="ExternalOutput").ap()
        self.outs[name] = t
        return t

    def scratch(self, name, shape, dtype=F32):
        if self.dbg and dtype == F32:
            return self.out("dbg_" + name, shape, dtype)
        return self.nc.dram_tensor(name, list(shape), dtype, kind="Internal").ap()

    def phase(self, name):
        return Phase(self, name)

    def setup_consts(self, st):
        nc, P = self.nc, self.P
        self.identf = st.enter_context(nc.sbuf_tensor("identf", [128, 128], F32))
        self.ident = st.enter_context(nc.sbuf_tensor("ident", [128, 128], BF16))
        self.ones = st.enter_context(nc.sbuf_tensor("onesb", [128, 128], BF16))
        self.epsb = st.enter_context(nc.sbuf_tensor("epsb", [128, 1], F32))
        self.A_qks = st.enter_context(nc.sbuf_tensor("A_qks", [128, 24, self.NS], BF16))
        P.op("pool", "memset", writes=["identf"], ap=self.identf[:], constant=0.0)
        P.op("pool", "affine_select", reads=["identf"], writes=["identf"], out=self.identf[:], in_=self.identf[:],
             pattern=[[-1, 128]], compare_op=ALU.not_equal, fill=1.0, base=0, channel_multiplier=1)
        P.op("dve", "tensor_copy", reads=["identf"], writes=["ident"], out=self.ident[:], in_=self.identf[:])
        P.op("pool", "memset", writes=["epsb"], ap=self.epsb[:], constant=1e-6)
        P.op("pool", "memset", writes=["ones"], ap=self.ones[:], constant=1.0)

    def cast(self, dst, src, reads, writes, eng=None):
        P = self.P
        if eng is None:
            eng = ("act", "dve", "pool")[self._rr % 3]
            self._rr += 1
        if eng == "act":
            P.op("act", "activation", reads=reads, writes=writes, out=dst, in_=src, func=AF.Copy)
        else:
            P.op(eng, "tensor_copy", reads=reads, writes=writes, out=dst, in_=src)

    def load_weight(self, ph, name, w, K, stg, col0=0, ncols=None, CH=512, KCH=8):
        P = self.P
        KC = (K + 127) // 128
        wb = ph.sbuf(name, [128, KC, ncols], BF16)
        key = ph.name + "_" + name
        if K % 128 == 0:
            wv = w.rearrange("(k p) n -> p k n", p=128)
            for k0 in range(0, KC, KCH):
                kn = min(KCH, KC - k0)
                for n0 in range(0, ncols, CH):
                    cw = min(CH, ncols - n0)
                    s, sk = stg.next()
                    P.dma(s[:, :kn, :cw], wv[:, k0:k0 + kn, col0 + n0:col0 + n0 + cw], writes=[sk])
                    self.cast(wb[:, k0:k0 + kn, n0:n0 + cw], s[:, :kn, :cw], [sk], [key])
        else:
            assert K < 128
            for n0 in range(0, ncols, CH):
                cw = min(CH, ncols - n0)
                s, sk = stg.next()
                P.dma(s[:K, 0, :cw], w[:, col0 + n0:col0 + n0 + cw], writes=[sk])
                self.cast(wb[:K, 0, n0:n0 + cw], s[:K, 0, :cw], [sk], [key])
        return wb, key

    def load_gain(self, ph, name, g_row):
        t = ph.sbuf(name, [128, D], F32)
        key = ph.name + "_" + name
        self.P.dma(t[:], g_row.partition_broadcast(128), writes=[key])
        return t, key

    def make_norm_bufs(self, ph, nbuf=2):
        nb = {}
        nb["sq"] = Rot([ph.sbuf("nsq%d" % i, [128, D], F32) for i in range(1)], ph.name + "nsq")
        nb["ss"] = Rot([ph.sbuf("nss%d" % i, [128, 1], F32) for i in range(nbuf)], ph.name + "nss")
        nb["rs"] = Rot([ph.sbuf("nrs%d" % i, [128, 1], F32) for i in range(nbuf)], ph.name + "nrs")
        nb["hb"] = Rot([ph.sbuf("nhb%d" % i, [128, D], BF16) for i in range(nbuf)], ph.name + "nhb")
        nb["pT"] = Rot([ph.psum("npT%d" % i, [128, 8, 128], BF16) for i in range(1)], ph.name + "npT")
        return nb

    def rstd(self, nb, x_ap, nt, xkey, width=D):
        P = self.P
        sq, sqk = nb["sq"].next()
        ss, ssk = nb["ss"].next()
        rs, rsk = nb["rs"].next()
        P.op("act", "activation", reads=[xkey], writes=[sqk, ssk], out=sq[:nt, :width], in_=x_ap, func=AF.Square, accum_out=ss[:nt])
        P.op("act", "activation", reads=[ssk, "epsb"], writes=[rsk], out=rs[:nt], in_=ss[:nt], func=AF.Sqrt,
             scale=1.0 / width, bias=self.epsb[:nt])
        P.op("dve", "reciprocal", reads=[rsk], writes=[rsk], out=rs[:nt], in_=rs[:nt])
        return rs, rsk

    def norm_T(self, nb, x_ap, nt, xkey, g, gkey, hT_ap, hTkey):
        P = self.P
        rs, rsk = self.rstd(nb, x_ap, nt, xkey)
        hb, hbk = nb["hb"].next()
        P.op("dve", "scalar_tensor_tensor", reads=[xkey, rsk, gkey], writes=[hbk], out=hb[:nt], in0=x_ap,
             scalar=rs[:nt, 0:1], in1=g[:nt], op0=ALU.mult, op1=ALU.mult)
        self.transpose_to(nb, hb[:nt], hbk, 8, nt, hT_ap, hTkey)

    def transpose_to(self, nb, src_ap, srckey, nchunks, nt, dst_ap, dstkey, eng="act"):
        P = self.P
        for c0 in range(0, nchunks, 8):
            cn = min(8, nchunks - c0)
            pT, pTk = nb["pT"].next()
            for k in range(cn):
                P.op("pe", "transpose", reads=[srckey, "ident"], writes=[pTk], out=pT[:, k, :nt],
                     in_=src_ap[:, (c0 + k) * 128:(c0 + k + 1) * 128], identity=self.ident[:nt, :nt])
            self.cast(dst_ap[:, c0:c0 + cn, :], pT[:, :cn, :nt], [pTk], [dstkey], eng=eng)

    def row_groups(self, TG):
        groups = []
        for r0 in range(0, self.L, TG):
            groups.append(("p", r0, min(TG, self.L - r0)))
        groups.append(("s", 0, self.NS))
        return groups

    def xrows(self, kind, r0, n):
        if kind == "p":
            return self.X[r0:r0 + n, :]
        return self.XS[r0:r0 + n, :]

    def xkeys(self, kind, r0, n):
        if kind == "p":
            return ["X_%d" % t for t in range(r0 // 128, (r0 + n + 127) // 128)]
        return ["XS"]

    def load_x(self, xt, xtk, kind, r0, n):
        P = self.P
        src = self.xrows(kind, r0, n)
        keys = self.xkeys(kind, r0, n)
        if n >= 128:
            assert n % 128 == 0
            P.dma(xt[:, :n // 128, :], src.rearrange("(t p) d -> p t d", p=128), reads=keys, writes=[xtk])
            return [(t, 128) for t in range(n // 128)]
        P.dma(xt[:n, 0, :], src, reads=keys, writes=[xtk])
        return [(0, n)]

    def store_x(self, xt, xtk, kind, r0, n):
        P = self.P
        dst = self.xrows(kind, r0, n)
        keys = self.xkeys(kind, r0, n)
        if n >= 128:
            P.dma(dst.rearrange("(t p) d -> p t d", p=128), xt[:, :n // 128, :], reads=[xtk], writes=keys, eng="pool")
        else:
            P.dma(dst, xt[:n, 0, :], reads=[xtk], writes=keys, eng="pool")

    def out_proj_add(self, aT, aTk, KC, wo, wok, xt, xtk, tiles, pos):
        P = self.P
        for t, nt in tiles:
            for nh in range(2):
                po, pok = pos.next()
                for k in range(KC):
                    P.op("pe", "matmul", reads=[aTk, wok], writes=[pok], out=po[:nt, :], lhsT=aT[:, k, t * 128:t * 128 + nt],
                         rhs=wo[:, k, nh * 512:(nh + 1) * 512], start=(k == 0), stop=(k == KC - 1))
                P.op("dve", "tensor_tensor", reads=[pok, xtk], writes=[xtk], out=xt[:nt, t, nh * 512:(nh + 1) * 512],
                     in0=xt[:nt, t, nh * 512:(nh + 1) * 512], in1=po[:nt, :], op=ALU.add)

    def phase_init(self, x_prompt, x_sample):
        P = self.P
        for r0 in range(0, self.L, 512):
            n = min(512, self.L - r0)
            P.dma(self.X[r0:r0 + n, :], x_prompt[r0:r0 + n, :], writes=self.xkeys("p", r0, n))
        P.dma(self.XS[:, :], x_sample, writes=["XS"])

    def phase_ffn(self, l):
        P = self.P
        TG = 256
        with self.phase("ffn%d" % l) as ph:
            stg = Rot([ph.sbuf("stg%d" % i, [128, 8, 256], F32) for i in range(2)], ph.name + "stg")
            win, wink = self.load_weight(ph, "win", self.w["w_ffn_in"][l], D, stg, ncols=2 * FH, CH=256)
            wout, woutk = self.load_weight(ph, "wout", self.w["w_ffn_out"][l], FH, stg, ncols=D, CH=256)
            g, gk = self.load_gain(ph, "g", self.w["g_ffn"][l])
            nb = self.make_norm_bufs(ph)
            xts = Rot([ph.sbuf("xt%d" % i, [128, TG // 128, D], F32) for i in range(2)], ph.name + "xt")
            hTs = Rot([ph.sbuf("hT%d" % i, [128, 8, TG], BF16) for i in range(1)], ph.name + "hT")
            gTs = Rot([ph.sbuf("gT%d" % i, [128, 22, TG], BF16) for i in range(1)], ph.name + "gT")
            sas = Rot([ph.sbuf("sa%d" % i, [128, TG], F32) for i in range(2)], ph.name + "sa")
            pas = Rot([ph.psum("pa%d" % i, [128, 512], F32) for i in range(2)], ph.name + "pa")
            pbs = Rot([ph.psum("pb%d" % i, [128, 512], F32) for i in range(2)], ph.name + "pb")
            pos = Rot([ph.psum("po%d" % i, [128, 512], F32) for i in range(2)], ph.name + "po")
            for kind, r0, n in self.row_groups(TG):
                xt, xtk = xts.next()
                tiles = self.load_x(xt, xtk, kind, r0, n)
                hT, hTk = hTs.next()
                for t, nt in tiles:
                    self.norm_T(nb, xt[:nt, t, :], nt, xtk, g, gk, hT[:, :, t * 128:t * 128 + nt], hTk)
                gT, gTk = gTs.next()
                for hc in range(22):
                    pa, pak = pas.next()
                    pb, pbk = pbs.next()
                    for k in range(8):
                        P.op("pe", "matmul", reads=[wink, hTk], writes=[pak], out=pa[:, :n], lhsT=win[:, k, hc * 128:(hc + 1) * 128],
                             rhs=hT[:, k, :n], start=(k == 0), stop=(k == 7))
                    for k in range(8):
                        P.op("pe", "matmul", reads=[wink, hTk], writes=[pbk], out=pb[:, :n],
                             lhsT=win[:, k, FH + hc * 128:FH + (hc + 1) * 128], rhs=hT[:, k, :n], start=(k == 0), stop=(k == 7))
                    sa, sak = sas.next()
                    P.op("act", "activation", reads=[pak], writes=[sak], out=sa[:, :n], in_=pa[:, :n], func=AF.Silu)
                    P.op("dve", "tensor_tensor", reads=[sak, pbk], writes=[gTk], out=gT[:, hc, :n], in0=sa[:, :n], in1=pb[:, :n], op=ALU.mult)
                self.out_proj_add(gT, gTk, 22, wout, woutk, xt, xtk, tiles, pos)
                self.store_x(xt, xtk, kind, r0, n)

    def phase_final(self, y_prompt, y_sample):
        P = self.P
        with self.phase("fin") as ph:
            g, gk = self.load_gain(ph, "g", self.w["g_final"])
            nb = self.make_norm_bufs(ph)
            xts = Rot([ph.sbuf("xt%d" % i, [128, 4, D], F32) for i in range(2)], ph.name + "xt")
            yts = Rot([ph.sbuf("yt%d" % i, [128, 4, D], F32) for i in range(2)], ph.name + "yt")
            for kind, r0, n in self.row_groups(512):
                xt, xtk = xts.next()
                yt, ytk = yts.next()
                tiles = self.load_x(xt, xtk, kind, r0, n)
                for t, nt in tiles:
                    rs, rsk = self.rstd(nb, xt[:nt, t, :], nt, xtk)
                    P.op("dve", "scalar_tensor_tensor", reads=[xtk, rsk, gk], writes=[ytk], out=yt[:nt, t, :], in0=xt[:nt, t, :],
                         scalar=rs[:nt, 0:1], in1=g[:nt], op0=ALU.mult, op1=ALU.mult)
                if kind == "p":
                    P.dma(y_prompt[r0:r0 + n, :].rearrange("(t p) d -> p t d", p=128), yt[:, :n // 128, :], reads=[ytk],
                          writes=["y%d" % r0], eng="pool")
                else:
                    P.dma(y_sample, yt[:n, 0, :], reads=[ytk], writes=["ys"], eng="pool")

    def attn_units(self, units, PTs, pss, LA=2):
        P = self.P
        staged = {}
        n = len(units)
        for i in range(n + LA):
            if i < n:
                u = units[i]
                ps, psk = pss.next()
                nq = u["nq"]
                nc_ = len(u["k"])
                nk = None
                for c in range(nc_):
                    ka, kk = u["k"][c]
                    qa, qk = u["q"][c]
                    nk = ka.shape[-1]
                    P.op("pe", "matmul", reads=[kk, qk], writes=[psk], out=ps[:nk, :nq], lhsT=ka, rhs=qa, start=(c == 0), stop=(c == nc_ - 1))
                PT, PTk = PTs.next()
                P.op("act", "activation", reads=[psk], writes=[PTk], out=PT[:nk, :nq], in_=ps[:nk, :nq], func=AF.Exp)
                staged[i] = (PT, PTk, nk)
            jb = i - LA
            if jb >= 0:
                u = units[jb]
                PT, PTk, nk = staged.pop(jb)
                nq = u["nq"]
                for va, vk, acc, acck in u["v"]:
                    P.op("pe", "matmul", reads=[vk, PTk], writes=[acck], out=acc, lhsT=va, rhs=PT[:nk, :nq], start=u["first"], stop=u["last"])
                if u["last"] and u.get("fin") is not None:
                    u["fin"]()

    def phase_cross(self, l):
        P = self.P
        NS = self.NS
        TG = 512
        with self.phase("cr%d" % l) as ph:
            stg = Rot([ph.sbuf("stg%d" % i, [128, 8, 256], F32) for i in range(2)], ph.name + "stg")
            wq, wqk = self.load_weight(ph, "wq", self.w["w_x_q"][l], D, stg, ncols=D, CH=256)
            wo, wok = self.load_weight(ph, "wo", self.w["w_x_o"][l], D, stg, ncols=D, CH=256)
            wkv, wkvk = self.load_weight(ph, "wkv", self.w["w_x_kv"][l], D, stg, ncols=2 * D, CH=256)
            g, gk = self.load_gain(ph, "g", self.w["g_cross"][l])
            gm, gmk = self.load_gain(ph, "gm", self.w["g_mem"][l])
            nb = self.make_norm_bufs(ph)
            kTs = [ph.sbuf("kT%d" % i, [128, 8, MEM], BF16) for i in range(NS + 1)]
            Vs = [ph.sbuf("V%d" % i, [128, 2, D], BF16) for i in range(NS + 1)]
            pss = Rot([ph.psum("ps%d" % i, [128, 512], F32) for i in range(2)], ph.name + "ps")
            pos = Rot([ph.psum("po%d" % i, [128, 512], F32) for i in range(2)], ph.name + "po")
            paccs = [ph.psum("pacc%d" % i, [128, 512], F32) for i in range(3)]
            kvf = ph.sbuf("kvf", [128, 2, 2 * D], F32)
            kvfk = ph.name + "kvf"
            mfull = kvf[:, :, D:2 * D]
            mfk = kvfk
            P.dma(mfull, self.mem_prompt.rearrange("(t p) d -> p t d", p=128), writes=[mfk])
            hmT = ph.sbuf("hmT", [128, 8, MEM], BF16)
            hmTk = ph.name + "hmT"
            for t in range(2):
                self.norm_T(nb, kvf[:, t, D:2 * D], 128, mfk, gm, gmk, hmT[:, :, t * 128:(t + 1) * 128], hmTk)
            for t in range(2):
                for nbk in range(4):
                    po, pok = pos.next()
                    for k in range(8):
                        P.op("pe", "matmul", reads=[hmTk, wkvk], writes=[pok], out=po[:, :], lhsT=hmT[:, k, t * 128:(t + 1) * 128],
                             rhs=wkv[:, k, nbk * 512:(nbk + 1) * 512], start=(k == 0), stop=(k == 7))
                    self.cast(kvf[:, t, nbk * 512:(nbk + 1) * 512], po[:, :], [pok], [kvfk], eng=("act", "dve")[nbk % 2])
            P.dma(self.o_mem[l].rearrange("(t p) d -> p t d", p=128), kvf[:], reads=[kvfk], writes=["omem%d" % l], eng="pool")
            self.cast(Vs[0][:, :, :], kvf[:, :, D:2 * D], [kvfk], [ph.name + "V0"])
            for hc in range(8):
                po, pok = pos.next()
                for k in range(8):
                    P.op("pe", "matmul", reads=[hmTk, wkvk], writes=[pok], out=po[:, :MEM], lhsT=wkv[:, k, hc * 128:(hc + 1) * 128],
                         rhs=hmT[:, k, :], start=(k == 0), stop=(k == 7))
                self.cast(kTs[0][:, hc, :], po[:, :MEM], [pok], [ph.name + "kT0"], eng=("act", "dve")[hc % 2])
            kvb = ph.sbuf("kvb", [128, 2, D], BF16)
            kvbk = ph.name + "kvb"
            for s in range(NS):
                P.dma(kvf[:], self.cache_mem[l, s].rearrange("(t p) d -> p t d", p=128), reads=[], writes=[kvfk])
                self.cast(kvb[:], kvf[:, :, 0:D], [kvfk], [kvbk])
                self.cast(Vs[s + 1][:, :, :], kvf[:, :, D:2 * D], [kvfk], [ph.name + "V%d" % (s + 1)])
                for t in range(2):
                    self.transpose_to(nb, kvb[:, t, 0:D], kvbk, 8, 128, kTs[s + 1][:, :, t * 128:(t + 1) * 128], ph.name + "kT%d" % (s + 1))
            xts = Rot([ph.sbuf("xt%d" % i, [128, TG // 128, D], F32) for i in range(1)], ph.name + "xt")
            hT = ph.sbuf("hT", [128, 8, TG], BF16)
            hTk = ph.name + "hT"
            qT = ph.sbuf("qT", [128, 8, TG], BF16)
            qTk = ph.name + "qT"
            oT = ph.sbuf("oT", [128, 8, TG], BF16)
            oTk = ph.name + "oT"
            PTs = Rot([ph.sbuf("PT%d" % i, [128, TG], BF16) for i in range(4)], ph.name + "PT")
            rden = ph.sbuf("rden", [128, TG], F32)
            rdenk = ph.name + "rden"
            for kind, r0, n in self.row_groups(TG):
                xt, xtk = xts.next()
                tiles = self.load_x(xt, xtk, kind, r0, n)
                for t, nt in tiles:
                    self.norm_T(nb, xt[:nt, t, :], nt, xtk, g, gk, hT[:, :, t * 128:t * 128 + nt], hTk)
                for hc in range(8):
                    po, pok = pos.next()
                    for k in range(8):
                        P.op("pe", "matmul", reads=[hTk, wqk], writes=[pok], out=po[:, :n], lhsT=wq[:, k, hc * 128:(hc + 1) * 128],
                             rhs=hT[:, k, :n], start=(k == 0), stop=(k == 7))
                    P.op("act", "activation", reads=[pok], writes=[qTk], out=qT[:, hc, :n], in_=po[:, :n], func=AF.Copy, scale=1.0 / 16)
                seqs = [(0, 0, n)] if kind == "p" else [(s + 1, s, 1) for s in range(n)]
                units = []
                for mi, q0, nq in seqs:
                    kTm, Vm = kTs[mi], Vs[mi]
                    kTmk, Vmk = ph.name + "kT%d" % mi, ph.name + "V%d" % mi
                    for h in range(4):
                        accs = [(paccs[j][:, :nq], ph.name + "pacc%d" % j) for j in range(3)]

                        def fin(h=h, q0=q0, nq=nq, accs=accs):
                            P.op("dve", "reciprocal", reads=[accs[2][1]], writes=[rdenk], out=rden[:, :nq], in_=accs[2][0])
                            for j in range(2):
                                P.op("dve", "tensor_tensor", reads=[accs[j][1], rdenk], writes=[oTk], out=oT[:, h * 2 + j, q0:q0 + nq],
                                     in0=accs[j][0], in1=rden[:, :nq], op=ALU.mult)
                        for t in range(2):
                            units.append(dict(
                                k=[(kTm[:, h * 2 + c, t * 128:(t + 1) * 128], kTmk) for c in range(2)],
                                q=[(qT[:, h * 2 + c, q0:q0 + nq], qTk) for c in range(2)], nq=nq,
                                v=[(Vm[:, t, h * 256 + j * 128:h * 256 + (j + 1) * 128], Vmk, accs[j][0], accs[j][1]) for j in range(2)]
                                  + [(self.ones[:, :], "ones", accs[2][0], accs[2][1])],
                                first=(t == 0), last=(t == 1), fin=fin))
                self.attn_units(units, PTs, pss, LA=2)
                self.out_proj_add(oT, oTk, 8, wo, wok, xt, xtk, tiles, pos)
                self.store_x(xt, xtk, kind, r0, n)

    def phase_mixA(self, l, j):
        P = self.P
        L, NS = self.L, self.NS
        TG = 512
        QT_d, KT_d, V_d, VS_d = self.QT_d, self.KT_d, self.V_d, self.VS_d
        WIN = (128, 512, 2048)
        DIL = (1, 4, 16)
        outs_p = (self.o_dil128_p, self.o_dil512_p, self.o_dil2048_p)
        outs_s = (self.o_dil128_s, self.o_dil512_s, self.o_dil2048_s)
        with self.phase("a1_%d" % l) as ph:
            stg = Rot([ph.sbuf("stg%d" % i, [128, 8, 512], F32) for i in range(2)], ph.name + "stg")
            wqkv, wk = self.load_weight(ph, "wqkv", self.w["w_a_qkv"][j], D, stg, ncols=4608)
            g, gk = self.load_gain(ph, "g", self.w["g_mix"][l])
            nb = self.make_norm_bufs(ph)
            xts = Rot([ph.sbuf("xt%d" % i, [128, TG // 128, D], F32) for i in range(1)], ph.name + "xt")
            hT = ph.sbuf("hT", [128, 8, TG], BF16)
            hTk = ph.name + "hT"
            qks = Rot([ph.sbuf("qk%d" % i, [128, 24, TG], BF16) for i in range(1)], ph.name + "qk")
            vbs = Rot([ph.sbuf("vb%d" % i, [128, 1536], BF16) for i in range(2)], ph.name + "vb")
            sfs = Rot([ph.sbuf("sf%d" % i, [128, 512], F32) for i in range(3)], ph.name + "sf")
            pos = Rot([ph.psum("po%d" % i, [128, 512], F32) for i in range(4)], ph.name + "po")
            for kind, r0, n in self.row_groups(TG):
                if kind in self.dbg_skip:
                    continue
                xt, xtk = xts.next()
                tiles = self.load_x(xt, xtk, kind, r0, n)
                for t, nt in tiles:
                    self.norm_T(nb, xt[:nt, t, :], nt, xtk, g, gk, hT[:, :, t * 128:t * 128 + nt], hTk)
                qk, qkk = qks.next()
                for c in range(24):
                    po, pok = pos.next()
                    for k in range(8):
                        P.op("pe", "matmul", reads=[hTk, wk], writes=[pok], out=po[:, :n], lhsT=wqkv[:, k, c * 128:(c + 1) * 128],
                             rhs=hT[:, k, :n], start=(k == 0), stop=(k == 7))
                    if c < 12:
                        P.op("act", "activation", reads=[pok], writes=[qkk], out=qk[:, c, :n], in_=po[:, :n], func=AF.Copy, scale=0.125)
                    else:
                        P.op("dve", "tensor_copy", reads=[pok], writes=[qkk], out=qk[:, c, :n], in_=po[:, :n])
                if "qk" in self.dbg_skip:
                    pass
                elif kind == "p":
                    P.dma(QT_d[:, r0:r0 + n].rearrange("(c p) t -> p c t", p=128), qk[:, 0:12, :n], reads=[qkk], writes=["QT_d%d" % r0], eng="sp")
                    P.dma(KT_d[:, r0:r0 + n].rearrange("(c p) t -> p c t", p=128), qk[:, 12:24, :n], reads=[qkk], writes=["KT_d%d" % r0], eng="sp")
                else:
                    P.op("pool", "tensor_copy", reads=[qkk], writes=["A_qks"], out=self.A_qks[:, :, :n], in_=qk[:, :, :n])
                for t, nt in tiles:
                    if "v" in self.dbg_skip:
                        break
                    row0 = r0 + t * 128
                    vb, vbk = vbs.next()
                    for gI in range(3):
                        Wc = min(WIN[gI], L)
                        need_state = (kind == "s") or (row0 + nt > L - Wc)
                        po, pok = pos.next()
                        for k in range(8):
                            P.op("pe", "matmul", reads=[hTk, wk], writes=[pok], out=po[:nt, :], lhsT=hT[:, k, t * 128:t * 128 + nt],
                                 rhs=wqkv[:, k, 3072 + gI * 512:3072 + (gI + 1) * 512], start=(k == 0), stop=(k == 7))
                        if not need_state:
                            P.op("act", "activation", reads=[pok], writes=[vbk], out=vb[:nt, gI * 512:(gI + 1) * 512], in_=po[:nt, :], func=AF.Copy)
                        if need_state:
                            sf, sfk = sfs.next()
                            P.op("dve", "tensor_copy", reads=[pok], writes=[sfk], out=sf[:nt, :], in_=po[:nt, :])
                            P.op("act", "activation", reads=[sfk], writes=[vbk], out=vb[:nt, gI * 512:(gI + 1) * 512], in_=sf[:nt, :], func=AF.Copy)
                            if kind == "p":
                                o0 = row0 - (L - Wc)
                                P.dma(outs_p[gI][j, o0:o0 + nt, 512:1024], sf[:nt, :], reads=[sfk], writes=["odil"], eng="pool")
                            else:
                                P.dma(outs_s[gI][j, :, 512:1024], sf[:nt, :], reads=[sfk], writes=["odil"], eng="pool")
                            po, pok = pos.next()
                            for k in range(8):
                                P.op("pe", "matmul", reads=[hTk, wk], writes=[pok], out=po[:nt, :], lhsT=hT[:, k, t * 128:t * 128 + nt],
                                     rhs=wqkv[:, k, 1536 + gI * 512:1536 + (gI + 1) * 512], start=(k == 0), stop=(k == 7))
                            sf, sfk = sfs.next()
                            P.op("dve", "tensor_copy", reads=[pok], writes=[sfk], out=sf[:nt, :], in_=po[:nt, :])
                            if kind == "p":
                                P.dma(outs_p[gI][j, o0:o0 + nt, 0:512], sf[:nt, :], reads=[sfk], writes=["odil"], eng="pool")
                            else:
                                P.dma(outs_s[gI][j, :, 0:512], sf[:nt, :], reads=[sfk], writes=["odil"], eng="pool")
                    if kind == "p":
                        P.dma(V_d[row0:row0 + nt, :], vb[:nt, :], reads=[vbk], writes=["V_d%d" % row0], eng="sp")
                    else:
                        P.dma(VS_d[:, :], vb[:nt, :], reads=[vbk], writes=["VS_d"], eng="sp")
        NR = 17
        LA = 2
        if getattr(self, "skip_a2", False):
            return
        gidx0 = (0, 2, 7)
        with self.phase("a2_%d" % l) as ph:
            stg = Rot([ph.sbuf("stg%d" % i, [128, 4, 512], F32) for i in range(2)], ph.name + "stg")
            wo, wok = self.load_weight(ph, "wo", self.w["w_a_o"][j], 512, stg, ncols=D, KCH=4)
            RA = ph.sbuf("RA", [128, 24, 128], F32)
            P.dma(RA[:].rearrange("p a q -> p (a q)"), self.c_RA, writes=["RA"])
            CBA = ph.sbuf("CBA", [128, 24 * 17], F32)
            P.dma(CBA[:], self.c_CBA, writes=["CBA"])
            MA = ph.sbuf("MA", [128, 24, 128], BF16)
            for i in range(2):
                sg, sk = stg.next()
                P.dma(sg[:, 0:3, :].rearrange("p a q -> p (a q)"), self.c_MA[:, i * 1536:(i + 1) * 1536], writes=[sk])
                self.cast(MA[:, i * 12:(i + 1) * 12, :].rearrange("p a q -> p (a q)"), sg[:, 0:3, :].rearrange("p a q -> p (a q)"), [sk], ["MA"])
            R0 = ph.sbuf("R0", [128, 24, 128], F32)
            P.dma(R0[:].rearrange("p a q -> p (a q)"), self.c_R0, writes=["R0"])
            KR = ph.sbuf("KR", [128, 12, NR, 128], BF16)
            VR = ph.sbuf("VR", [128, NR, 1536], BF16)
            QTs = Rot([ph.sbuf("QT%d" % i, [128, 12, 128], BF16) for i in range(2)], ph.name + "QT")
            xts = Rot([ph.sbuf("xt%d" % i, [128, 1, D], F32) for i in range(2)], ph.name + "xt")
            t1s = Rot([ph.sbuf("t1%d" % i, [128, 4, 128], F32) for i in range(3)], ph.name + "t1")
            PTs = Rot([ph.sbuf("PT%d" % i, [128, 4, 128], BF16) for i in range(LA + 2)], ph.name + "PT")
            oTs = Rot([ph.sbuf("oT%d" % i, [128, 4, 128], BF16) for i in range(2)], ph.name + "oT")
            rden = ph.sbuf("rden", [128, 128], F32)
            rdenk = ph.name + "rden"
            pss = Rot([ph.psum("ps%d" % i, [128, 4, 128], F32) for i in range(3)], ph.name + "ps")
            paccs = Rot([ph.psum("pacc%d" % i, [128, 512], F32) for i in range(2)], ph.name + "pacc")
            pdens = Rot([ph.psum("pden%d" % i, [128, 512], F32) for i in range(2)], ph.name + "pden")
            pos = Rot([ph.psum("po%d" % i, [128, 512], F32) for i in range(1)], ph.name + "po")
            for qt in range(L // 128):
                slot = qt % NR
                P.dma(KR[:, :, slot, :], KT_d[:, qt * 128:(qt + 1) * 128].rearrange("(c p) t -> p c t", p=128),
                      reads=["KT_d%d" % (qt * 128 // TG * TG)], writes=["KR%d" % slot])
                P.dma(VR[:, slot, :], V_d[qt * 128:(qt + 1) * 128, :], reads=["V_d%d" % (qt * 128)], writes=["VR%d" % slot])
                QT, QTk = QTs.next()
                P.dma(QT[:], QT_d[:, qt * 128:(qt + 1) * 128].rearrange("(c p) t -> p c t", p=128),
                      reads=["QT_d%d" % (qt * 128 // TG * TG)], writes=[QTk])
                xt, xtk = xts.next()
                self.load_x(xt, xtk, "p", qt * 128, 128)
                oT, oTk = oTs.next()
                batches = []
                for s in range(8):
                    bl = []
                    for gI in range(3):
                        dmax = min(qt, WIN[gI] // 128)
                        for d0 in range(0, dmax + 1, 4):
                            bl.append((gI, d0, min(4, dmax + 1 - d0)))
                    for bi, (gI, d0, nbt) in enumerate(bl):
                        batches.append((s, gI, d0, nbt, bi == 0, bi == len(bl) - 1))
                staged = {}
                accs = {}
                for i in range(len(batches) + LA):
                    if i < len(batches):
                        s, gI, d0, nbt, first, last = batches[i]
                        half = slice(0, 64) if s % 2 == 0 else slice(64, 128)
                        h = gI * 8 + s
                        c = gI * 4 + s // 2
                        ps, psk = pss.next()
                        for u in range(nbt):
                            ks = (qt - d0 - u) % NR
                            P.op("pe", "matmul", reads=["KR%d" % ks, QTk], writes=[psk], out=ps[:, u, :], lhsT=KR[half, c, ks, :],
                                 rhs=QT[half, c, :], start=True, stop=True)
                        t1, t1k = t1s.next()
                        u0 = 0
                        if d0 == 0:
                            P.op("dve", "tensor_tensor", reads=[psk, "R0"], writes=[t1k], out=t1[:, 0, :], in0=ps[:, 0, :], in1=R0[:, h, :], op=ALU.add)
                            u0 = 1
                        if nbt > u0:
                            P.op("dve", "tensor_tensor", reads=[psk, "RA"], writes=[t1k], out=t1[:, u0:nbt, :], in0=ps[:, u0:nbt, :],
                                 in1=RA[:, h:h + 1, :].to_broadcast([128, nbt - u0, 128]), op=ALU.add)
                        PT, PTk = PTs.next()
                        pks = ["%s_%d" % (PTk, u) for u in range(nbt)]
                        for u in range(nbt):
                            dl = d0 + u
                            P.op("act", "activation", reads=[t1k, "CBA"], writes=[pks[u]], out=PT[:, u, :], in_=t1[:, u, :], func=AF.Exp,
                                 bias=CBA[:, h * 17 + dl:h * 17 + dl + 1])
                        m0 = gidx0[gI] + d0
                        P.op("pool", "tensor_tensor", reads=pks + ["MA"], writes=pks, out=PT[:, 0:nbt, :], in0=PT[:, 0:nbt, :],
                             in1=MA[:, m0:m0 + nbt, :], op=ALU.mult)
                        pall = ["%s_%d" % (PTk, u) for u in range(4)]
                        if nbt < 4:
                            P.op("pool", "memset", writes=pall[nbt:], ap=PT[:, nbt:4, :], constant=0.0)
                        staged[i] = (PT, pks, pall)
                    jb = i - LA
                    if jb >= 0:
                        s, gI, d0, nbt, first, last = batches[jb]
                        PT, pks, pall = staged.pop(jb)
                        if first:
                            accs[s] = (paccs.next(), pdens.next())
                        (acc, acck), (den, denk) = accs[s]
                        for u in range(nbt):
                            ks = (qt - d0 - u) % NR
                            fl = dict(start=(first and u == 0), stop=(last and u == nbt - 1))
                            P.op("pe", "matmul", reads=["VR%d" % ks, pks[u]], writes=[acck], out=acc[:, 0:128],
                                 lhsT=VR[:, ks, gI * 512 + (s // 2) * 128:gI * 512 + (s // 2 + 1) * 128], rhs=PT[:, u, :], **fl)
                        P.op("pe", "matmul", reads=["ones"] + pall, writes=[denk], out=den[:, :], lhsT=self.ones[:, :],
                             rhs=PT[:, :, :].rearrange("p u q -> p (u q)"), start=first, stop=last)
                        if last:
                            half = slice(0, 64) if s % 2 == 0 else slice(64, 128)
                            P.op("dve", "tensor_reduce", reads=[denk], writes=[rdenk], out=rden[half, :],
                                 in_=den[half, :].rearrange("p (u q) -> p q u", u=4), axis=AX.X, op=ALU.add)
                            P.op("dve", "reciprocal", reads=[rdenk], writes=[rdenk], out=rden[half, :], in_=rden[half, :])
                            P.op("dve", "tensor_tensor", reads=[acck, rdenk], writes=[oTk], out=oT[half, s // 2, :], in0=acc[half, 0:128],
                                 in1=rden[half, :], op=ALU.mult)
                self.out_proj_add(oT, oTk, 4, wo, wok, xt, xtk, [(0, 128)], pos)
                self.store_x(xt, xtk, "p", qt * 128, 128)
        with self.phase("a2s_%d" % l) as ph:
            stg = Rot([ph.sbuf("stg%d" % i, [128, 4, 512], F32) for i in range(2)], ph.name + "stg")
            wo, wok = self.load_weight(ph, "wo", self.w["w_a_o"][j], 512, stg, ncols=D, KCH=4)
            CBS = ph.sbuf("CBS", [128, 24], F32)
            P.dma(CBS[:], self.c_CBS, writes=["CBS"])
            xts = Rot([ph.sbuf("xt%d" % i, [128, 1, D], F32) for i in range(1)], ph.name + "xt")
            PTs = Rot([ph.sbuf("PT%d" % i, [128, 128], BF16) for i in range(3)], ph.name + "PT")
            rden = ph.sbuf("rden", [128, 128], F32)
            rdenk = ph.name + "rden"
            pss = Rot([ph.psum("ps%d" % i, [128, 512], F32) for i in range(2)], ph.name + "ps")
            paccs = Rot([ph.psum("pacc%d" % i, [128, 512], F32) for i in range(2)], ph.name + "pacc")
            pdens = Rot([ph.psum("pden%d" % i, [128, 512], F32) for i in range(2)], ph.name + "pden")
            pos = Rot([ph.psum("po%d" % i, [128, 512], F32) for i in range(1)], ph.name + "po")
            vrow = ph.sbuf("vrow", [1, 1536], BF16)
            kvc = Rot([ph.sbuf("kvc%d" % i, [128, 1024], F32) for i in range(2)], ph.name + "kvc")
            kvbs = Rot([ph.sbuf("kvb%d" % i, [128, 1024], BF16) for i in range(3)], ph.name + "kvb")
            kTcs = Rot([ph.sbuf("kTc%d" % i, [128, 4, 128], BF16) for i in range(3)], ph.name + "kTc")
            nb = {"pT": Rot([ph.psum("npT0", [128, 8, 128], BF16)], ph.name + "npT")}
            oTS = ph.sbuf("oTS", [128, 4, NS], BF16)
            oTSk = ph.name + "oTS"
            caches = (self.cache_dil128, self.cache_dil512, self.cache_dil2048)
            for sI in range(NS):
                P.dma(vrow[:], VS_d[sI:sI + 1, :], reads=["VS_d"], writes=["vrow"])
                grp = []
                for gI in range(3):
                    kv, kvk = kvc.next()
                    W, dd = WIN[gI], DIL[gI]
                    src = caches[gI][j, sI].rearrange("(i d) f -> i d f", d=dd)[:, 0, :]
                    P.dma(kv[:], src, writes=[kvk])
                    kvb, kvbk = kvbs.next()
                    self.cast(kvb[:], kv[:], [kvk], [kvbk])
                    kTc, kTck = kTcs.next()
                    self.transpose_to(nb, kvb[:, 0:512], kvbk, 4, 128, kTc[:, :, :], kTck)
                    grp.append((kvb, kvbk, kTc, kTck))
                for s in range(8):
                    half = slice(0, 64) if s % 2 == 0 else slice(64, 128)
                    acc, acck = paccs.next()
                    den, denk = pdens.next()
                    for gI in range(3):
                        h = gI * 8 + s
                        c = gI * 4 + s // 2
                        kvb, kvbk, kTc, kTck = grp[gI]
                        ps, psk = pss.next()
                        P.op("pe", "matmul", reads=[kTck, "A_qks"], writes=[psk], out=ps[:, 0:1], lhsT=kTc[half, s // 2, :],
                             rhs=self.A_qks[half, c, sI:sI + 1], start=True, stop=True)
                        PT, PTk = PTs.next()
                        P.op("act", "activation", reads=[psk, "CBS"], writes=[PTk], out=PT[:, 0:1], in_=ps[:, 0:1], func=AF.Exp,
                             bias=CBS[:, h:h + 1])
                        P.op("pe", "matmul", reads=[kvbk, PTk], writes=[acck], out=acc[:, 0:1],
                             lhsT=kvb[:, 512 + (s // 2) * 128:512 + (s // 2 + 1) * 128], rhs=PT[:, 0:1], start=(gI == 0), stop=False)
                        P.op("pe", "matmul", reads=["ones", PTk], writes=[denk], out=den[:, 0:1], lhsT=self.ones[:, :], rhs=PT[:, 0:1],
                             start=(gI == 0), stop=False)
                        ps, psk = pss.next()
                        P.op("pe", "matmul", reads=["A_qks"], writes=[psk], out=ps[0:1, 0:1], lhsT=self.A_qks[half, 12 + c, sI:sI + 1],
                             rhs=self.A_qks[half, c, sI:sI + 1], start=True, stop=True)
                        PT, PTk = PTs.next()
                        P.op("act", "activation", reads=[psk], writes=[PTk], out=PT[0:1, 0:1], in_=ps[0:1, 0:1], func=AF.Exp)
                        v0 = gI * 512 + (s // 2) * 128
                        P.op("pe", "matmul", reads=["vrow", PTk], writes=[acck], out=acc[:, 0:1], lhsT=vrow[0:1, v0:v0 + 128], rhs=PT[0:1, 0:1],
                             start=False, stop=(gI == 2))
                        P.op("pe", "matmul", reads=["ones", PTk], writes=[denk], out=den[:, 0:1], lhsT=self.ones[0:1, :], rhs=PT[0:1, 0:1],
                             start=False, stop=(gI == 2))
                    hs = slice(0, 64) if s % 2 == 0 else slice(64, 128)
                    P.op("dve", "reciprocal", reads=[denk], writes=[rdenk], out=rden[hs, 0:1], in_=den[hs, 0:1])
                    P.op("dve", "tensor_tensor", reads=[acck, rdenk], writes=[oTSk], out=oTS[hs, s // 2, sI:sI + 1], in0=acc[hs, 0:1],
                         in1=rden[hs, 0:1], op=ALU.mult)
            xt, xtk = xts.next()
            tiles = self.load_x(xt, xtk, "s", 0, NS)
            self.out_proj_add(oTS, oTSk, 4, wo, wok, xt, xtk, tiles, pos)
            self.store_x(xt, xtk, "s", 0, NS)

    def gla_chunk(self, bufs, c, qkT_ap, la_ap, v_ap, S, Sk, Sb, Sbk, o_ap, ok, rk):
        P = self.P
        pb, pbk = bufs["pb"].next()
        for h in range(4):
            P.op("pe", "matmul", reads=rk + ["TRI"], writes=[pbk], out=pb[:, h, :c], lhsT=la_ap[:, h * 128:(h + 1) * 128],
                 rhs=self.TRI[:c, :c], start=True, stop=True)
        eb, ebk = bufs["eb"].next()
        enb, enbk = bufs["enb"].next()
        P.op("act", "activation", reads=[pbk], writes=[ebk], out=eb[:, :, :c], in_=pb[:, :, :c], func=AF.Exp)
        P.op("act", "activation", reads=[pbk], writes=[enbk], out=enb[:, :, :c], in_=pb[:, :, :c], func=AF.Exp, scale=-1.0)
        qt, qtk = bufs["qt"].next()
        kt_, ktk = bufs["kt"].next()
        q32, q32k = bufs["q32"].next()
        k32, k32k = bufs["k32"].next()
        P.op("dve", "tensor_tensor", reads=rk + [ebk], writes=[q32k], out=q32[:, :, :c], in0=qkT_ap[:, 0:4, :], in1=eb[:, :, :c], op=ALU.mult)
        P.op("pool", "tensor_tensor", reads=rk + [enbk], writes=[k32k], out=k32[:, :, :c], in0=qkT_ap[:, 4:8, :], in1=enb[:, :, :c], op=ALU.mult)
        P.op("act", "activation", reads=[q32k], writes=[qtk], out=qt[:, :, :c], in_=q32[:, :, :c], func=AF.Copy)
        P.op("pool", "tensor_copy", reads=[k32k], writes=[ktk], out=kt_[:, :, :c], in_=k32[:, :, :c])
        pa, pak = bufs["pa"].next()
        for h in range(4):
            P.op("pe", "matmul", reads=[q32k, k32k], writes=[pak], out=pa[:c, h, :c], lhsT=k32[:, h, :c], rhs=q32[:, h, :c], start=True, stop=True)
        at, atk = bufs["at"].next()
        P.op("dve", "tensor_tensor", reads=[pak, "TRI"], writes=[atk], out=at[:c, :, :c], in0=pa[:c, :, :c],
             in1=self.TRI4[:c, :].rearrange("p (h i) -> p h i", h=4)[:, :, :c], op=ALU.mult)
        pt, ptk = bufs["pt"].next()
        for h in range(4):
            P.op("pe", "transpose", reads=[ktk, "ident"], writes=[ptk], out=pt[:c, h, :], in_=kt_[:, h, :c], identity=self.ident[:, :])
        ktm, ktmk = bufs["ktm"].next()
        P.op("act", "activation", reads=[ptk], writes=[ktmk], out=ktm[:c, :, :], in_=pt[:c, :, :], func=AF.Copy)
        for hp in range(2):
            po, pok = bufs["po"].next()
            for hh in range(2):
                h = hp * 2 + hh
                P.op("pe", "matmul", reads=[atk] + rk, writes=[pok], out=po[:c, hh, :], lhsT=at[:c, h, :c], rhs=v_ap[:, h * 256:(h + 1) * 256],
                     start=True, stop=False)
                P.op("pe", "matmul", reads=[qtk, Sbk], writes=[pok], out=po[:c, hh, :], lhsT=qt[:, h, :c], rhs=Sb[:, h, :], start=False, stop=True)
            P.op("dve", "tensor_copy", reads=[pok], writes=[ok], out=o_ap[:, hp * 512:(hp + 1) * 512], in_=po[:c, :, :].rearrange("p a v -> p (a v)"))
        for hp in range(2):
            pk, pkk = bufs["pk"].next()
            for hh in range(2):
                h = hp * 2 + hh
                P.op("pe", "matmul", reads=[ktmk] + rk, writes=[pkk], out=pk[:, hh, :], lhsT=ktm[:c, h, :], rhs=v_ap[:, h * 256:(h + 1) * 256],
                     start=True, stop=True)
            T, Tk = bufs["T"].next()
            P.op("dve", "tensor_tensor", reads=[pkk, Sk], writes=[Tk], out=T[:, :, :], in0=S[:, hp * 2:hp * 2 + 2, :], in1=pk[:, :, :], op=ALU.add)
            for hh in range(2):
                h = hp * 2 + hh
                P.op("pool", "tensor_scalar", reads=[Tk, ebk], writes=[Sk], out=S[:, h, :], in0=T[:, hh, :], scalar1=eb[:, h, c - 1:c],
                     scalar2=None, op0=ALU.mult)
                P.op("act", "activation", reads=[Tk, ebk], writes=[Sbk], out=Sb[:, h, :], in_=T[:, hh, :], func=AF.Copy, scale=eb[:, h, c - 1:c])

    def phase_mixB(self, l, j):
        P = self.P
        L, NS = self.L, self.NS
        TG = 512
        QK_d, LA_d, VB_d, SR_d, O_d = self.QK_d, self.LA_d, self.VB_d, self.SR_d, self.O_d
        QKS_d, LAS_d, VBS_d, SRS_d, OS_d = self.QKS_d, self.LAS_d, self.VBS_d, self.SRS_d, self.OS_d
        with self.phase("b1_%d" % l) as ph:
            stg = Rot([ph.sbuf("stg%d" % i, [128, 8, 512], F32) for i in range(2)], ph.name + "stg")
            win, wk = self.load_weight(ph, "win", self.w["w_b_in"][j], D, stg, col0=1024, ncols=2064)
            wqk = ph.sbuf("wqk", [128, 8, 1024], F32)
            P.dma(wqk[:], self.w["w_b_in"][j].rearrange("(k p) n -> p k n", p=128)[:, :, 0:1024], writes=["wqk"])
            g, gk = self.load_gain(ph, "g", self.w["g_mix"][l])
            hT32 = ph.sbuf("hT32", [128, 8, TG], F32)
            hf32 = Rot([ph.sbuf("hf32_%d" % i, [128, D], F32) for i in range(1)], ph.name + "hf32")
            pT32 = Rot([ph.psum("pT32", [128, 4, 128], F32)], ph.name + "pT32")
            wgs = ph.sbuf("wgs", [17, 512], F32)
            P.dma(wgs[0:16, :], self.w["w_b_gate2"][j], writes=["wgs"])
            P.dma(wgs[16:17, :], self.w["b_b_gate"][j:j + 1, :], writes=["wgs"])
            wg = ph.sbuf("wg", [17, 512], BF16)
            P.op("dve", "tensor_copy", reads=["wgs"], writes=["wg"], out=wg[:], in_=wgs[:])
            onesf = ph.sbuf("onesf", [128, 1], F32)
            P.op("pool", "memset", writes=["onesf"], ap=onesf[:], constant=1.0)
            nb = self.make_norm_bufs(ph)
            xts = Rot([ph.sbuf("xt%d" % i, [128, TG // 128, D], F32) for i in range(1)], ph.name + "xt")
            hT = ph.sbuf("hT", [128, 8, TG], BF16)
            hTk = ph.name + "hT"
            qkf = ph.sbuf("qkf", [128, 8, TG], F32)
            qkfk = ph.name + "qkf"
            glT = ph.sbuf("glT", [17, TG], BF16)
            glTk = ph.name + "glT"
            P.op("pool", "memset", writes=[glTk], ap=glT[:], constant=1.0)
            ezs = Rot([ph.sbuf("ez%d" % i, [128, 512], F32) for i in range(2)], ph.name + "ez")
            las = Rot([ph.sbuf("la%d" % i, [128, 512], F32) for i in range(2)], ph.name + "la")
            vbs = Rot([ph.sbuf("vb%d" % i, [128, 1024], BF16) for i in range(2)], ph.name + "vb")
            srs = Rot([ph.sbuf("sr%d" % i, [128, 1024], F32) for i in range(2)], ph.name + "sr")
            pos = Rot([ph.psum("po%d" % i, [128, 512], F32) for i in range(4)], ph.name + "po")
            for kind, r0, n in self.row_groups(TG):
                xt, xtk = xts.next()
                tiles = self.load_x(xt, xtk, kind, r0, n)
                for t, nt in tiles:
                    self.norm_T(nb, xt[:nt, t, :], nt, xtk, g, gk, hT[:, :, t * 128:t * 128 + nt], hTk)
                    rs, rsk = self.rstd(nb, xt[:nt, t, :], nt, xtk)
                    hf, hfk = hf32.next()
                    P.op("dve", "scalar_tensor_tensor", reads=[xtk, rsk, gk], writes=[hfk], out=hf[:nt], in0=xt[:nt, t, :],
                         scalar=rs[:nt, 0:1], in1=g[:nt], op0=ALU.mult, op1=ALU.mult)
                    for c0 in range(0, 8, 4):
                        pt3, pt3k = pT32.next()
                        for k in range(4):
                            P.op("pe", "transpose", reads=[hfk, "identf"], writes=[pt3k], out=pt3[:, k, :nt],
                                 in_=hf[:nt, (c0 + k) * 128:(c0 + k + 1) * 128], identity=self.identf[:nt, :nt])
                        P.op("act", "activation", reads=[pt3k], writes=["hT32"], out=hT32[:, c0:c0 + 4, t * 128:t * 128 + nt],
                             in_=pt3[:, :, :nt], func=AF.Copy)
                for c in range(8):
                    po, pok = pos.next()
                    for k in range(8):
                        P.op("pe", "matmul", reads=["hT32", "wqk"], writes=[pok], out=po[:, :n], lhsT=wqk[:, k, c * 128:(c + 1) * 128],
                             rhs=hT32[:, k, :n], start=(k == 0), stop=(k == 7))
                    if c < 4:
                        P.op("act", "activation", reads=[pok], writes=[qkfk], out=qkf[:, c, :n], in_=po[:, :n], func=AF.Copy, scale=128.0 ** -0.5)
                    else:
                        P.op("dve", "tensor_copy", reads=[pok], writes=[qkfk], out=qkf[:, c, :n], in_=po[:, :n])
                if kind == "p":
                    P.dma(QK_d[:, r0:r0 + n].rearrange("(c p) t -> p c t", p=128), qkf[:, :, :n], reads=[qkfk], writes=["QK_d%d" % r0])
                else:
                    P.dma(QKS_d.rearrange("(c p) t -> p c t", p=128), qkf[:, :, :n], reads=[qkfk], writes=["QKS_d"])
                po, pok = pos.next()
                for k in range(8):
                    P.op("pe", "matmul", reads=[hTk, wk], writes=[pok], out=po[:16, :n], lhsT=win[:, k, 1024:1040], rhs=hT[:, k, :n],
                         start=(k == 0), stop=(k == 7))
                P.op("act", "activation", reads=[pok], writes=[glTk], out=glT[0:16, :n], in_=po[:16, :n], func=AF.Copy)
                for t, nt in tiles:
                    row0 = r0 + t * 128
                    po, pok = pos.next()
                    P.op("pe", "matmul", reads=[glTk, "wg"], writes=[pok], out=po[:nt, :], lhsT=glT[:, t * 128:t * 128 + nt], rhs=wg[:, :],
                         start=True, stop=True)
                    ez, ezk = ezs.next()
                    P.op("act", "activation", reads=[pok], writes=[ezk], out=ez[:nt, :], in_=po[:nt, :], func=AF.Exp, scale=-1.0)
                    la, lak = las.next()
                    P.op("act", "activation", reads=[ezk, "onesf"], writes=[lak], out=la[:nt, :], in_=ez[:nt, :], func=AF.Ln, bias=onesf[:nt])
                    P.op("dve", "tensor_scalar", reads=[lak], writes=[lak], out=la[:nt, :], in0=la[:nt, :], scalar1=-1.0 / 16, scalar2=None, op0=ALU.mult)
                    if kind == "p":
                        P.dma(LA_d[row0:row0 + nt, :], la[:nt, :], reads=[lak], writes=["LA_d%d" % row0])
                    else:
                        P.dma(LAS_d[:, :], la[:nt, :], reads=[lak], writes=["LAS_d"])
                    vb, vbk = vbs.next()
                    sr, srk = srs.next()
                    for hf in range(2):
                        po, pok = pos.next()
                        for k in range(8):
                            P.op("pe", "matmul", reads=[hTk, wk], writes=[pok], out=po[:nt, :], lhsT=hT[:, k, t * 128:t * 128 + nt],
                                 rhs=win[:, k, hf * 512:(hf + 1) * 512], start=(k == 0), stop=(k == 7))
                        P.op("dve", "tensor_copy", reads=[pok], writes=[vbk], out=vb[:nt, hf * 512:(hf + 1) * 512], in_=po[:nt, :])
                        po, pok = pos.next()
                        for k in range(8):
                            P.op("pe", "matmul", reads=[hTk, wk], writes=[pok], out=po[:nt, :], lhsT=hT[:, k, t * 128:t * 128 + nt],
                                 rhs=win[:, k, 1040 + hf * 512:1040 + (hf + 1) * 512], start=(k == 0), stop=(k == 7))
                        P.op("act", "activation", reads=[pok], writes=[srk], out=sr[:nt, hf * 512:(hf + 1) * 512], in_=po[:nt, :], func=AF.Silu)
                    if kind == "p":
                        P.dma(VB_d[row0:row0 + nt, :], vb[:nt, :], reads=[vbk], writes=["VB_d%d" % row0])
                        P.dma(SR_d[row0:row0 + nt, :], sr[:nt, :], reads=[srk], writes=["SR_d%d" % row0])
                    else:
                        P.dma(VBS_d[:, :], vb[:nt, :], reads=[vbk], writes=["VBS_d"])
                        P.dma(SRS_d[:, :], sr[:nt, :], reads=[srk], writes=["SRS_d"])
        with self.phase("b2_%d" % l) as ph:
            self.TRI4 = ph.sbuf("TRI4", [64, 256], F32)
            P.dma(self.TRI4[:], self.c_TRI4, writes=["TRI"])
            self.TRI = self.TRI4[:, 0:64]
            bufs = {
                "pb": Rot([ph.psum("pb0", [128, 4, 64], F32)], ph.name + "pb"),
                "pa": Rot([ph.psum("pa0", [64, 4, 64], F32)], ph.name + "pa"),
                "pt": Rot([ph.psum("pt0", [64, 4, 128], BF16)], ph.name + "pt"),
                "po": Rot([ph.psum("po%d" % i, [64, 2, 256], F32) for i in range(2)], ph.name + "po"),
                "pk": Rot([ph.psum("pk%d" % i, [128, 2, 256], F32) for i in range(2)], ph.name + "pk"),
                "eb": Rot([ph.sbuf("eb%d" % i, [128, 4, 64], F32) for i in range(2)], ph.name + "eb"),
                "enb": Rot([ph.sbuf("enb%d" % i, [128, 4, 64], F32) for i in range(2)], ph.name + "enb"),
                "qt": Rot([ph.sbuf("qt%d" % i, [128, 4, 64], BF16) for i in range(2)], ph.name + "qt"),
                "q32": Rot([ph.sbuf("q32_%d" % i, [128, 4, 64], F32) for i in range(2)], ph.name + "q32"),
                "k32": Rot([ph.sbuf("k32_%d" % i, [128, 4, 64], F32) for i in range(2)], ph.name + "k32"),
                "kt": Rot([ph.sbuf("kt%d" % i, [128, 4, 64], BF16) for i in range(2)], ph.name + "kt"),
                "at": Rot([ph.sbuf("at%d" % i, [64, 4, 64], BF16) for i in range(2)], ph.name + "at"),
                "ktm": Rot([ph.sbuf("ktm%d" % i, [64, 4, 128], BF16) for i in range(2)], ph.name + "ktm"),
                "T": Rot([ph.sbuf("T%d" % i, [128, 2, 256], F32) for i in range(2)], ph.name + "T"),
            }
            S = ph.sbuf("S", [128, 4, 256], F32)
            Sb = ph.sbuf("Sb", [128, 4, 256], BF16)
            Sk, Sbk = ph.name + "S", ph.name + "Sb"
            qks = Rot([ph.sbuf("qk%d" % i, [128, 8, TG], F32) for i in range(2)], ph.name + "qk")
            lag = Rot([ph.sbuf("lag%d" % i, [64, TG // 64, 512], F32) for i in range(2)], ph.name + "lag")
            vg = Rot([ph.sbuf("vg%d" % i, [64, TG // 64, 1024], BF16) for i in range(2)], ph.name + "vg")
            og = Rot([ph.sbuf("og%d" % i, [64, TG // 64, 1024], F32) for i in range(2)], ph.name + "og")
            P.op("pool", "memset", writes=[Sk], ap=S[:], constant=0.0)
            P.op("pool", "memset", writes=[Sbk], ap=Sb[:], constant=0.0)
            for r0 in range(0, L, TG):
                n = min(TG, L - r0)
                qk, qkk = qks.next()
                P.dma(qk[:, :, :n], QK_d[:, r0:r0 + n].rearrange("(c p) t -> p c t", p=128), reads=["QK_d%d" % r0], writes=[qkk])
                la, lak = lag.next()
                P.dma(la[:, :n // 64, :], LA_d[r0:r0 + n, :].rearrange("(c p) f -> p c f", p=64),
                      reads=["LA_d%d" % (r0 + i) for i in range(0, n, 128)], writes=[lak])
                v, vk = vg.next()
                P.dma(v[:, :n // 64, :], VB_d[r0:r0 + n, :].rearrange("(c p) f -> p c f", p=64),
                      reads=["VB_d%d" % (r0 + i) for i in range(0, n, 128)], writes=[vk])
                o, ok = og.next()
                for ci in range(n // 64):
                    self.gla_chunk(bufs, 64, qk[:, :, ci * 64:(ci + 1) * 64], la[:, ci, :], v[:, ci, :], S, Sk, Sb, Sbk, o[:, ci, :], ok,
                                   [qkk, lak, vk])
                P.dma(O_d[r0:r0 + n, :].rearrange("(c p) f -> p c f", p=64), o[:, :n // 64, :], reads=[ok],
                      writes=["O_d%d" % (r0 + i) for i in range(0, n, 128)])
            P.dma(self.o_gla_p.rearrange("h k v -> k h v"), S[:], reads=[Sk], writes=["ogla_p"])
            qk, qkk = qks.next()
            P.dma(qk[:, :, :NS], QKS_d.rearrange("(c p) t -> p c t", p=128), reads=["QKS_d"], writes=[qkk])
            for s in range(NS):
                P.dma(S[:], self.state_gla[s].rearrange("h k v -> k h v"), reads=["ogla_p", "ogla_s"], writes=[Sk])
                P.op("act", "activation", reads=[Sk], writes=[Sbk], out=Sb[:], in_=S[:], func=AF.Copy)
                la, lak = lag.next()
                P.dma(la[0:1, 0, :], LAS_d[s:s + 1, :], reads=["LAS_d"], writes=[lak])
                v, vk = vg.next()
                P.dma(v[0:1, 0, :], VBS_d[s:s + 1, :], reads=["VBS_d"], writes=[vk])
                o, ok = og.next()
                self.gla_chunk(bufs, 1, qk[:, :, s:s + 1], la[0:1, 0, :], v[0:1, 0, :], S, Sk, Sb, Sbk, o[0:1, 0, :], ok, [qkk, lak, vk])
                P.dma(OS_d[s:s + 1, :], o[0:1, 0, :], reads=[ok], writes=["OS_d"])
                P.dma(self.o_gla_s[s].rearrange("h k v -> k h v"), S[:], reads=[Sk], writes=["ogla_s"])
        with self.phase("b3_%d" % l) as ph:
            stg = Rot([ph.sbuf("stg%d" % i, [128, 8, 512], F32) for i in range(2)], ph.name + "stg")
            wo, wok = self.load_weight(ph, "wo", self.w["w_b_o"][j], D, stg, ncols=D)
            gh, ghk = self.load_gain(ph, "gh", self.w["g_b_head"][j].rearrange("h v -> (h v)"))
            nb = self.make_norm_bufs(ph, nbuf=4)
            xts = Rot([ph.sbuf("xt%d" % i, [128, TG // 128, D], F32) for i in range(1)], ph.name + "xt")
            ots = Rot([ph.sbuf("ot%d" % i, [128, TG // 128, D], F32) for i in range(1)], ph.name + "ot")
            sts = Rot([ph.sbuf("st%d" % i, [128, TG // 128, D], F32) for i in range(1)], ph.name + "st")
            ons = Rot([ph.sbuf("on%d" % i, [128, D], F32) for i in range(2)], ph.name + "on")
            o2s = Rot([ph.sbuf("o2%d" % i, [128, D], BF16) for i in range(2)], ph.name + "o2")
            oT = ph.sbuf("oT", [128, 8, TG], BF16)
            oTk = ph.name + "oT"
            pos = Rot([ph.psum("po%d" % i, [128, 512], F32) for i in range(2)], ph.name + "po")
            for kind, r0, n in self.row_groups(TG):
                xt, xtk = xts.next()
                tiles = self.load_x(xt, xtk, kind, r0, n)
                ot, otk = ots.next()
                st_, stk = sts.next()
                if kind == "p":
                    rd = ["O_d%d" % (r0 + i) for i in range(0, n, 128)]
                    P.dma(ot[:, :n // 128, :], O_d[r0:r0 + n, :].rearrange("(t p) d -> p t d", p=128), reads=rd, writes=[otk])
                    rd = ["SR_d%d" % (r0 + i) for i in range(0, n, 128)]
                    P.dma(st_[:, :n // 128, :], SR_d[r0:r0 + n, :].rearrange("(t p) d -> p t d", p=128), reads=rd, writes=[stk])
                else:
                    P.dma(ot[:n, 0, :], OS_d[:, :], reads=["OS_d"], writes=[otk])
                    P.dma(st_[:n, 0, :], SRS_d[:, :], reads=["SRS_d"], writes=[stk])
                for t, nt in tiles:
                    on, onk = ons.next()
                    for h in range(4):
                        rs, rsk = self.rstd(nb, ot[:nt, t, h * 256:(h + 1) * 256], nt, otk, width=256)
                        P.op("dve", "scalar_tensor_tensor", reads=[otk, rsk, ghk], writes=[onk], out=on[:nt, h * 256:(h + 1) * 256],
                             in0=ot[:nt, t, h * 256:(h + 1) * 256], scalar=rs[:nt, 0:1], in1=gh[:nt, h * 256:(h + 1) * 256],
                             op0=ALU.mult, op1=ALU.mult)
                    o2, o2k = o2s.next()
                    P.op("pool", "tensor_tensor", reads=[onk, stk], writes=[o2k], out=o2[:nt, :], in0=on[:nt, :], in1=st_[:nt, t, :], op=ALU.mult)
                    self.transpose_to(nb, o2[:nt], o2k, 8, nt, oT[:, :, t * 128:t * 128 + nt], oTk)
                self.out_proj_add(oT, oTk, 8, wo, wok, xt, xtk, tiles, pos)
                self.store_x(xt, xtk, kind, r0, n)

    def gelu_tanh(self, ph, tmp, x_ps, xk, hb, n, out_ap, outk):
        P = self.P
        x, xk2 = tmp["x"].next()
        u, uk = tmp["u"].next()
        P.op("act", "activation", reads=[xk, "hb"], writes=[xk2], out=x[:, :n], in_=x_ps, func=AF.Identity, bias=hb)
        P.op("dve", "tensor_tensor", reads=[xk2], writes=[uk], out=u[:, :n], in0=x[:, :n], in1=x[:, :n], op=ALU.mult)
        P.op("dve", "tensor_scalar", reads=[uk], writes=[uk], out=u[:, :n], in0=u[:, :n], scalar1=0.044715, scalar2=1.0, op0=ALU.mult, op1=ALU.add)
        P.op("dve", "tensor_tensor", reads=[uk, xk2], writes=[uk], out=u[:, :n], in0=u[:, :n], in1=x[:, :n], op=ALU.mult)
        P.op("act", "activation", reads=[uk], writes=[uk], out=u[:, :n], in_=u[:, :n], func=AF.Sigmoid, scale=1.5957691216057308)
        P.op("dve", "tensor_tensor", reads=[uk, xk2], writes=[outk], out=out_ap, in0=u[:, :n], in1=x[:, :n], op=ALU.mult)

    def phase_mixC(self, l, j):
        P = self.P
        L, NS = self.L, self.NS
        TG = 512
        NQT = L // 128
        NCMP = L // 16 - 1
        NCT = (NCMP + 127) // 128
        NSLC = L // 64
        QN_d, KN_d, C2_d, VN_d, GT_d, KC_d, VC_d, OT_d = self.QN_d, self.KN_d, self.C2_d, self.VN_d, self.GT_d, self.KC_d, self.VC_d, self.OT_d
        OFF = {"q": 0, "kc": 1024, "vc": 1280, "ks": 1536, "vs": 1792, "kw": 2048, "vw": 2304, "g": 2560}
        with self.phase("c1_%d" % l) as ph:
            stg = Rot([ph.sbuf("stg%d" % i, [128, 8, 512], F32) for i in range(2)], ph.name + "stg")
            win, wk = self.load_weight(ph, "win", self.w["w_c_in"][j], D, stg, ncols=2608)
            g, gk = self.load_gain(ph, "g", self.w["g_mix"][l])
            wdup = ph.sbuf("wdup", [128, 8, 8, 128], BF16)
            for ti, ty in enumerate(("kc", "vc")):
                for G in range(4):
                    src = win[:, :, OFF[ty] + G * 64:OFF[ty] + (G + 1) * 64]
                    P.op("dve", "tensor_copy", reads=[wk], writes=["wdup"], out=wdup[:, ti * 4 + G, :, 0:64], in_=src)
                    P.op("pool", "tensor_copy", reads=[wk], writes=["wdup"], out=wdup[:, ti * 4 + G, :, 64:128], in_=src)
            bg = ph.sbuf("bg", [48, 1], F32)
            P.dma(bg[:], self.w["b_c_gate"][j].rearrange("(p o) -> p o", o=1), writes=["bg"])
            nb = self.make_norm_bufs(ph)
            xts = Rot([ph.sbuf("xt%d" % i, [128, TG // 128, D], F32) for i in range(1)], ph.name + "xt")
            hT = ph.sbuf("hT", [128, 8, TG], BF16)
            hTk = ph.name + "hT"
            qn = ph.sbuf("qn", [64, 16, TG], BF16)
            kn = ph.sbuf("kn", [64, 8, TG], BF16)
            c2 = ph.sbuf("c2", [128, 8, TG], BF16)
            gt = ph.sbuf("gt", [48, TG], BF16)
            sfs = Rot([ph.sbuf("sf%d" % i, [128, 512], F32) for i in range(3)], ph.name + "sf")
            vns = Rot([ph.sbuf("vn%d" % i, [128, 512], BF16) for i in range(2)], ph.name + "vn")
            pos = Rot([ph.psum("po%d" % i, [128, 512], F32) for i in range(4)], ph.name + "po")
            Wn = min(512, L)
            for kind, r0, n in self.row_groups(TG):
                xt, xtk = xts.next()
                tiles = self.load_x(xt, xtk, kind, r0, n)
                for t, nt in tiles:
                    self.norm_T(nb, xt[:nt, t, :], nt, xtk, g, gk, hT[:, :, t * 128:t * 128 + nt], hTk)
                for h in range(16):
                    po, pok = pos.next()
                    for k in range(8):
                        P.op("pe", "matmul", reads=[hTk, wk], writes=[pok], out=po[:64, :n], lhsT=win[:, k, h * 64:(h + 1) * 64], rhs=hT[:, k, :n],
                             start=(k == 0), stop=(k == 7))
                    P.op("act", "activation", reads=[pok], writes=["qn"], out=qn[:, h, :n], in_=po[:64, :n], func=AF.Copy, scale=0.125)
                for ti, ty in enumerate(("ks", "kw")):
                    for G in range(4):
                        po, pok = pos.next()
                        for k in range(8):
                            P.op("pe", "matmul", reads=[hTk, wk], writes=[pok], out=po[:64, :n],
                                 lhsT=win[:, k, OFF[ty] + G * 64:OFF[ty] + (G + 1) * 64], rhs=hT[:, k, :n], start=(k == 0), stop=(k == 7))
                        P.op("dve", "tensor_copy", reads=[pok], writes=["kn"], out=kn[:, ti * 4 + G, :n], in_=po[:64, :n])
                for b8 in range(8):
                    po, pok = pos.next()
                    for k in range(8):
                        P.op("pe", "matmul", reads=[hTk, "wdup"], writes=[pok], out=po[:, :n], lhsT=wdup[:, b8, k, :], rhs=hT[:, k, :n],
                             start=(k == 0), stop=(k == 7))
                    self.cast(c2[:, b8, :n], po[:, :n], [pok], ["c2"], eng=("act", "dve")[b8 % 2])
                po, pok = pos.next()
                for k in range(8):
                    P.op("pe", "matmul", reads=[hTk, wk], writes=[pok], out=po[:48, :n], lhsT=win[:, k, 2560:2608], rhs=hT[:, k, :n],
                         start=(k == 0), stop=(k == 7))
                P.op("act", "activation", reads=[pok, "bg"], writes=["gt"], out=gt[:, :n], in_=po[:48, :n], func=AF.Sigmoid, bias=bg[:, 0:1])
                if kind == "p":
                    P.dma(QN_d[:, :, r0:r0 + n], qn[:, :, :n], reads=["qn"], writes=["QN_d%d" % r0])
                    P.dma(KN_d[:, :, r0:r0 + n], kn[:, :, :n], reads=["kn"], writes=["KN_d%d" % r0])
                    P.dma(GT_d[:, r0:r0 + n], gt[:, :n], reads=["gt"], writes=["GT_d%d" % r0])
                    c2v = C2_d.rearrange("(b p) t -> p b t", p=128)
                    P.dma(c2v[0:64, :, r0:r0 + n], c2[0:64, :, :n], reads=["c2"], writes=["C2_d"])
                    if r0 == 0:
                        P.dma(c2v[64:128, :, 0:n - 1], c2[64:128, :, 1:n], reads=["c2"], writes=["C2_d"])
                    else:
                        P.dma(c2v[64:128, :, r0 - 1:r0 + n - 1], c2[64:128, :, :n], reads=["c2"], writes=["C2_d"])
                else:
                    P.dma(self.QNS_d[:, :, :], qn[:, :, :n], reads=["qn"], writes=["QNS_d"])
                    P.dma(self.KNS_d[:, :, :], kn[:, :, :n], reads=["kn"], writes=["KNS_d"])
                    P.dma(self.GTS_d[:, :], gt[:, :n], reads=["gt"], writes=["GTS_d"])
                for t, nt in tiles:
                    row0 = r0 + t * 128
                    vn, vnk = vns.next()
                    for blk in range(3):
                        need_f32 = blk < 2 or kind == "s" or (row0 + nt > L - Wn)
                        po, pok = pos.next()
                        for k in range(8):
                            P.op("pe", "matmul", reads=[hTk, wk], writes=[pok], out=po[:nt, :], lhsT=hT[:, k, t * 128:t * 128 + nt],
                                 rhs=win[:, k, 1024 + blk * 512:1024 + (blk + 1) * 512], start=(k == 0), stop=(k == 7))
                        sf, sfk = sfs.next()
                        P.op("dve", "tensor_copy", reads=[pok], writes=[sfk], out=sf[:nt, :], in_=po[:nt, :])
                        if blk == 1:
                            P.op("act", "activation", reads=[sfk], writes=[vnk], out=vn[:nt, 0:256], in_=sf[:nt, 256:512], func=AF.Copy)
                        if blk == 2:
                            P.op("act", "activation", reads=[sfk], writes=[vnk], out=vn[:nt, 256:512], in_=sf[:nt, 256:512], func=AF.Copy)
                        if kind == "p":
                            if blk < 2:
                                P.dma(self.o_nsa_kv_p[row0:row0 + nt, blk * 512:(blk + 1) * 512], sf[:nt, :], reads=[sfk], writes=["onsa"])
                            elif need_f32:
                                o0 = row0 - (L - Wn)
                                P.dma(self.o_nsa_win_p[o0:o0 + nt, :], sf[:nt, :], reads=[sfk], writes=["onsa"])
                        else:
                            if blk < 2:
                                P.dma(self.o_nsa_kv_s[:, blk * 512:(blk + 1) * 512], sf[:nt, :], reads=[sfk], writes=["onsa"])
                            else:
                                P.dma(self.o_nsa_win_s[:, :], sf[:nt, :], reads=[sfk], writes=["onsa"])
                    if kind == "p":
                        P.dma(VN_d[row0:row0 + nt, :], vn[:nt, :], reads=[vnk], writes=["VN_d%d" % row0])
                    else:
                        P.dma(self.VNS_d[:, :], vn[:nt, :], reads=[vnk], writes=["VNS_d"])
        with self.phase("c2_%d" % l) as ph:
            stg = Rot([ph.sbuf("stg%d" % i, [128, 16, 128], F32) for i in range(2)], ph.name + "stg")
            w1 = []
            for ti, nm in enumerate(("w_c_k1", "w_c_v1")):
                wb, wbk = self.load_weight(ph, "w1_%d" % ti, self.w[nm][j], 2048, stg, ncols=128, CH=128, KCH=16)
                w1.append((wb, wbk))
            w2 = []
            pw = Rot([ph.sbuf("pw%d" % i, [128, 16], F32) for i in range(2)], ph.name + "pw")
            posb = []
            for ti, (nm, pn) in enumerate((("w_c_k2", "c_pos_k"), ("w_c_v2", "c_pos_v"))):
                s, sk = stg.next()
                P.dma(s[:, 0, 0:64], self.w[nm][j], writes=[sk])
                wb2 = ph.sbuf("w2_%d" % ti, [128, 64], BF16)
                P.op("dve", "tensor_copy", reads=[sk], writes=["w2_%d" % ti], out=wb2[:], in_=s[:, 0, 0:64])
                w2.append((wb2, "w2_%d" % ti))
                pf, pfk = pw.next()
                P.dma(pf[:], self.w[pn][j].rearrange("(k a) d -> (a d) k", a=2), writes=[pfk], allow_slow_non_contiguous=True)
                pb_ = ph.sbuf("posb%d" % ti, [128, 16], BF16)
                P.op("dve", "tensor_copy", reads=[pfk], writes=["posb%d" % ti], out=pb_[:], in_=pf[:])
                posb.append((pb_, "posb%d" % ti))
            tmp = {"x": Rot([ph.sbuf("gx%d" % i, [128, 256], F32) for i in range(2)], ph.name + "gx"),
                   "u": Rot([ph.sbuf("gu%d" % i, [128, 256], F32) for i in range(2)], ph.name + "gu")}
            php = Rot([ph.psum("php%d" % i, [128, 512], F32) for i in range(2)], ph.name + "php")
            pco = Rot([ph.psum("pco%d" % i, [128, 512], F32) for i in range(2)], ph.name + "pco")
            hbs = []
            for ti in range(2):
                pc, pck = pco.next()
                for kc in range(16):
                    P.op("pe", "matmul", reads=[w1[ti][1], posb[ti][1]], writes=[pck], out=pc[:, 0:1], lhsT=w1[ti][0][:, kc, :],
                         rhs=posb[ti][0][:, kc:kc + 1], start=(kc == 0), stop=(kc == 15))
                hb = ph.sbuf("hb%d" % ti, [128, 1], F32)
                P.op("dve", "tensor_copy", reads=[pck], writes=["hb"], out=hb[:], in_=pc[:, 0:1])
                hbs.append(hb)
            X2s = Rot([ph.sbuf("X2_%d" % i, [128, L], BF16) for i in range(2)], ph.name + "X2")
            gels = Rot([ph.sbuf("gel%d" % i, [128, 256], BF16) for i in range(2)], ph.name + "gel")
            kcs = ph.sbuf("kcs", [64, 4, 256], BF16)
            vcs = ph.sbuf("vcs", [128, NCT, 4, 64], BF16)
            for ti in range(2):
                for G in range(4):
                    X2, X2k = X2s.next()
                    b8 = ti * 4 + G
                    P.dma(X2[:], C2_d[b8 * 128:(b8 + 1) * 128, :], reads=["C2_d"], writes=[X2k])
                    for ct in range(NCT):
                        c0 = ct * 128
                        nk = min(128, NCMP - c0)
                        hp, hpk = php.next()
                        for kc in range(16):
                            st0 = 16 * c0 + 2 * kc
                            P.op("pe", "matmul", reads=[w1[ti][1], X2k], writes=[hpk], out=hp[:, :nk], lhsT=w1[ti][0][:, kc, :],
                                 rhs=X2[:, st0:st0 + 16 * (nk - 1) + 1:16], start=(kc == 0), stop=(kc == 15))
                        gel, gelk = gels.next()
                        self.gelu_tanh(ph, tmp, hp[:, :nk], hpk, hbs[ti][:, 0:1], nk, gel[:, :nk], gelk)
                        pc, pck = pco.next()
                        if ti == 0:
                            P.op("pe", "matmul", reads=[gelk, w2[0][1]], writes=[pck], out=pc[:64, :nk], lhsT=w2[0][0][:, :], rhs=gel[:, :nk],
                                 start=True, stop=True)
                            P.op("act", "activation", reads=[pck], writes=["kcs"], out=kcs[:, G, c0:c0 + nk], in_=pc[:64, :nk], func=AF.Copy)
                        else:
                            P.op("pe", "matmul", reads=[gelk, w2[1][1]], writes=[pck], out=pc[:nk, :64], lhsT=gel[:, :nk], rhs=w2[1][0][:, :],
                                 start=True, stop=True)
                            P.op("act", "activation", reads=[pck], writes=["vcs"], out=vcs[:nk, ct, G, :], in_=pc[:nk, :64], func=AF.Copy)
            P.dma(KC_d[:, :, 0:NCMP], kcs[:, :, 0:NCMP], reads=["kcs"], writes=["KC_d"])
            for ct in range(NCT):
                nk = min(128, NCMP - ct * 128)
                P.dma(VC_d[ct * 128:ct * 128 + nk, :, :], vcs[:nk, ct, :, :], reads=["vcs"], writes=["VC_d"])
        with self.phase("c3_%d" % l) as ph:
            KN = ph.sbuf("KN", [64, 8, L], BF16)
            for r0 in range(0, L, TG):
                P.dma(KN[:, :, r0:r0 + TG], KN_d[:, :, r0:r0 + TG], reads=["KN_d%d" % r0], writes=["KN"])
            VN = ph.sbuf("VN", [128, NQT, 8, 65], BF16)
            P.op("pool", "memset", writes=["VN"], ap=VN[:], constant=1.0)
            for t in range(NQT):
                P.dma(VN[:, t, :, 0:64], VN_d[t * 128:(t + 1) * 128, :].rearrange("p (g d) -> p g d", d=64), reads=["VN_d%d" % (t * 128)], writes=["VN"])
            onesf = ph.sbuf("onesf", [65, 64], F32)
            P.op("pool", "memset", writes=["onesf"], ap=onesf[:], constant=1.0)
            drow = ph.sbuf("drow", [65, 512], F32)
            KC = ph.sbuf("KC", [64, 4, 256], BF16)
            P.dma(KC[:, :, 0:NCMP], KC_d[:, :, 0:NCMP], reads=["KC_d"], writes=["KC"])
            VC = ph.sbuf("VC", [128, NCT, 4, 65], BF16)
            P.op("pool", "memset", writes=["VC"], ap=VC[:], constant=1.0)
            for ct in range(NCT):
                nk = min(128, NCMP - ct * 128)
                P.dma(VC[:nk, ct, :, 0:64], VC_d[ct * 128:ct * 128 + nk, :, :], reads=["VC_d"], writes=["VC"])
            GT = ph.sbuf("GT", [48, L], BF16)
            P.dma(GT[:], GT_d, reads=["GT_d%d" % r0 for r0 in range(0, L, TG)], writes=["GT"])

            def const_bf16(name, src, shape2):
                s = ph.sbuf(name + "f", shape2, F32)
                P.dma(s[:], src, writes=[name + "f"])
                d = ph.sbuf(name, shape2, BF16)
                self.cast(d[:], s[:], [name + "f"], [name])
                return d
            COV = const_bf16("COV", self.c_COVER[:, 0:NCT * NSLC], [128, NCT * NSLC])
            Eb = ph.sbuf("Eb", [NSLC, L], BF16)
            sgs = Rot([ph.sbuf("sg%d" % i, [64, 2048], F32) for i in range(2)], ph.name + "sg")
            for c0 in range(0, L, 2048):
                cw = min(2048, L - c0)
                s, sk = sgs.next()
                P.dma(s[:NSLC, :cw], self.c_E[:, c0:c0 + cw], writes=[sk])
                self.cast(Eb[:, c0:c0 + cw], s[:NSLC, :cw], [sk], ["Eb"])
            SEL = ph.sbuf("SELG", [48, 48 * 64], BF16)
            for c0 in range(0, 48 * 64, 1536):
                s, sk = sgs.next()
                P.dma(s[:48, :1536], self.c_SELG[:, c0:c0 + 1536], writes=[sk])
                self.cast(SEL[:, c0:c0 + 1536], s[:48, :1536], [sk], ["SELG"])
            CAUS = ph.sbuf("CAUS", [128, 128], F32)
            P.dma(CAUS[:], self.c_CAUS, writes=["CAUS"])
            BK = ph.sbuf("BK", [128, 16 * 33], F32)
            P.dma(BK[:], self.c_BK, writes=["BK"])
            BKC = ph.sbuf("BKC", [128, 16 * NQT * NCT], F32)
            P.dma(BKC[:], self.c_BKC, writes=["BKC"])
            FT = ph.sbuf("FT", [128, NQT * NSLC], F32)
            P.dma(FT[:], self.c_FT, writes=["FT"])
            identf = self.identf
            Qs = Rot([ph.sbuf("Q%d" % i, [64, 16, 128], BF16) for i in range(2)], ph.name + "Q")
            PTs = Rot([ph.sbuf("PT%d" % i, [128, 512], BF16) for i in range(6)], ph.name + "PT")
            pss = Rot([ph.psum("ps%d" % i, [128, 512], F32) for i in range(4)], ph.name + "ps")
            pNs = Rot([ph.psum("pN%d" % i, [128, 512], F32) for i in range(1)], ph.name + "pN")
            pDs = Rot([ph.psum("pD%d" % i, [128, 512], F32) for i in range(1)], ph.name + "pD")
            pI = ph.psum("pI", [128, 512], F32)
            pM = Rot([ph.psum("pM0", [128, 512], F32)], ph.name + "pM")
            rdens = Rot([ph.sbuf("rden%d" % i, [64, 512], F32) for i in range(2)], ph.name + "rden")
            scs = Rot([ph.sbuf("sc%d" % i, [64, 512], F32) for i in range(2)], ph.name + "sc")
            oacc = ph.sbuf("oacc", [64, 512], F32)
            otmp = ph.sbuf("otmp", [64, 512], F32)
            oTb = ph.sbuf("oTb", [64, 16, 128], BF16)
            impn = ph.sbuf("impn", [64, 512], F32)
            impT = ph.sbuf("impT", [64, 128], F32)
            score = ph.sbuf("score", [128, 64], F32)
            sc2 = ph.sbuf("score2", [128, 64], F32)
            m8 = ph.sbuf("m8", [128, 16], F32)
            thr = ph.sbuf("thr", [128, 1], F32)
            vm = ph.sbuf("vm", [128, 64], F32)
            selb = ph.sbuf("selb", [128, 64], F32)
            selT = ph.sbuf("selT", [64, 128], BF16)
            pmc = ph.sbuf("pmc", [128, 128], F32)

            LA = 3
            tts = Rot([ph.sbuf("tt%d" % i, [128, 4, 128], F32) for i in range(4)], ph.name + "tt")
            msk = ph.sbuf("msk", [128, NQT + 3, 128], BF16)
            BK3 = BK[:, :].rearrange("p (h d) -> p h d", d=33)
            BKC3 = BKC[:, :].rearrange("p (h d) -> p h d", d=NQT * NCT)

            def stage(G, lhsT, lkey, nk, bias_ap, post):
                ps, psk = pss.next()
                P.op("pe", "matmul", reads=[lkey, Qk], writes=[psk], out=ps[:nk, :], lhsT=lhsT, rhs=qrhs, start=True, stop=True)
                tt, ttk = tts.next()
                P.op("dve", "tensor_tensor", reads=[psk, "BK", "BKC"], writes=[ttk], out=tt[:nk, :, :],
                     in0=ps[:nk, :].rearrange("p (a q) -> p a q", a=4), in1=bias_ap.unsqueeze(2).to_broadcast([nk, 4, 128]), op=ALU.add)
                PT, PTk = PTs.next()
                P.op("act", "activation", reads=[ttk], writes=[PTk], out=PT[:nk, :], in_=tt[:nk, :, :].rearrange("p a q -> p (a q)"), func=AF.Exp)
                post(PT, PTk, nk)
                return PT, PTk

            def run_branch(units, pN, pNk, pD, pDk, extra=None):
                staged = {}
                n = len(units)
                for i in range(n + LA):
                    if i < n:
                        u = units[i]
                        staged[i] = stage(G, u["lhsT"], u["lkey"], u["nk"], u["bias"], u["post"])
                    jb = i - LA
                    if jb >= 0:
                        u = units[jb]
                        PT, PTk = staged.pop(jb)
                        nk = u["nk"]
                        fl = dict(start=(jb == 0), stop=(jb == n - 1))
                        P.op("pe", "matmul", reads=[u["vkey"], PTk], writes=[pNk], out=pN[:65, :], lhsT=u["v"], rhs=PT[:nk, :], **fl)
                        if extra is not None:
                            extra(u, PT, PTk, nk, fl)

            def finish_branch(pN, pNk, pD, pDk, G, b, first):
                rd, rdk = rdens.next()
                P.op("act", "activation", reads=[pNk], writes=["drow"], out=drow[64:65, :], in_=pN[64:65, :], func=AF.Copy)
                P.op("pe", "matmul", reads=["onesf", "drow"], writes=[pDk], out=pD[:64, :], lhsT=onesf[64:65, :], rhs=drow[64:65, :], start=True, stop=True)
                P.op("dve", "tensor_scalar", reads=[pDk], writes=[rdk], out=rd[:], in0=pD[:64, :], scalar1=1e-30, scalar2=None, op0=ALU.max)
                P.op("dve", "reciprocal", reads=[rdk], writes=[rdk], out=rd[:], in_=rd[:])
                pm, pmk = pM.next()
                for i in range(4):
                    r = (4 * G + i) * 3 + b
                    P.op("pe", "matmul", reads=["SELG", "GT"], writes=[pmk], out=pm[:64, i * 128:(i + 1) * 128], lhsT=SEL[:, r * 64:(r + 1) * 64],
                         rhs=self._GTq, start=True, stop=True)
                sc, sck = scs.next()
                P.op("dve", "tensor_tensor", reads=[rdk, pmk], writes=[sck], out=sc[:], in0=rd[:], in1=pm[:64, :], op=ALU.mult)
                if first:
                    P.op("dve", "tensor_tensor", reads=[pNk, sck], writes=["oacc"], out=oacc[:], in0=pN[:64, :], in1=sc[:], op=ALU.mult)
                else:
                    P.op("dve", "tensor_tensor", reads=[pNk, sck], writes=["otmp"], out=otmp[:], in0=pN[:64, :], in1=sc[:], op=ALU.mult)
                    P.op("pool", "tensor_tensor", reads=["otmp", "oacc"], writes=["oacc"], out=oacc[:], in0=oacc[:], in1=otmp[:], op=ALU.add)
                return rd, rdk

            def no_post(PT, PTk, nk):
                pass

            for qt in range(NQT):
                tq0 = qt * 128
                Q, Qk = Qs.next()
                P.dma(Q[:], QN_d[:, :, tq0:tq0 + 128], reads=["QN_d%d" % (tq0 // TG * TG)], writes=[Qk])
                self._GTq = GT[:, tq0:tq0 + 128]
                for G in range(4):
                    qrhs = Q[:, 4 * G:4 * G + 4, :].rearrange("p a q -> p (a q)")
                    pN, pNk = pNs.next()
                    pD, pDk = pDs.next()
                    units = []
                    for ct in [ct for ct in range(NCT) if 16 * 128 * ct + 31 <= tq0 + 127]:
                        nk = min(128, NCMP - ct * 128)

                        def post_c(PT, PTk, nk, ct=ct):
                            if 16 * (128 * ct + nk - 1) + 31 > tq0:
                                P.op("pool", "affine_select", reads=[PTk], writes=[PTk], out=PT[:nk, :], in_=PT[:nk, :], pattern=[[0, 4], [1, 128]],
                                     compare_op=ALU.is_ge, fill=0.0, base=tq0 - 16 * 128 * ct - 31, channel_multiplier=-16)
                        units.append(dict(lhsT=KC[:, G, ct * 128:ct * 128 + nk], lkey="KC", nk=nk, bias=BKC3[:nk, 4 * G:4 * G + 4, qt * NCT + ct],
                                          post=post_c, v=VC[:nk, ct, G, :], vkey="VC", ct=ct))

                    def extra_c(u, PT, PTk, nk, fl):
                        ct = u["ct"]
                        P.op("pe", "matmul", reads=["COV", PTk], writes=["pI"], out=pI[:NSLC, :], lhsT=COV[:nk, ct * NSLC:(ct + 1) * NSLC],
                             rhs=PT[:nk, :], **fl)
                    run_branch(units, pN, pNk, pD, pDk, extra=extra_c)
                    rd, rdk = finish_branch(pN, pNk, pD, pDk, G, 0, True)
                    P.op("dve", "tensor_tensor", reads=["pI", rdk], writes=["impn"], out=impn[:NSLC, :], in0=pI[:NSLC, :], in1=rd[:NSLC, :], op=ALU.mult)
                    P.op("dve", "tensor_reduce", reads=["impn"], writes=["impT"], out=impT[:NSLC, :],
                         in_=impn[:NSLC, :].rearrange("p (a q) -> p q a", a=4), axis=AX.X, op=ALU.add)
                    units = []
                    for kt in range(max(0, qt - 4), qt + 1):
                        def post_w(PT, PTk, nk, kt=kt):
                            if kt == qt:
                                P.op("pool", "affine_select", reads=[PTk], writes=[PTk], out=PT[:, :], in_=PT[:, :], pattern=[[0, 4], [1, 128]],
                                     compare_op=ALU.is_ge, fill=0.0, base=0, channel_multiplier=-1)
                            if kt == qt - 4:
                                P.op("pool", "affine_select", reads=[PTk], writes=[PTk], out=PT[:, :], in_=PT[:, :], pattern=[[0, 4], [-1, 128]],
                                     compare_op=ALU.is_ge, fill=0.0, base=0, channel_multiplier=1)
                        units.append(dict(lhsT=KN[:, 4 + G, kt * 128:(kt + 1) * 128], lkey="KN", nk=128, bias=BK3[:, 4 * G:4 * G + 4, qt - kt],
                                          post=post_w, v=VN[:, kt, 4 + G, :], vkey="VN"))
                    pN, pNk = pNs.next()
                    pD, pDk = pDs.next()
                    run_branch(units, pN, pNk, pD, pDk)
                    finish_branch(pN, pNk, pD, pDk, G, 2, False)
                    pm, pmk = pM.next()
                    P.op("pe", "transpose", reads=["impT", "identf"], writes=[pmk], out=pm[:, :NSLC], in_=impT[:NSLC, :], identity=identf[:NSLC, :NSLC])
                    P.op("dve", "tensor_tensor", reads=[pmk, "FT"], writes=["score"], out=score[:, :NSLC], in0=pm[:, :NSLC],
                         in1=FT[:, qt * NSLC:(qt + 1) * NSLC], op=ALU.add)
                    P.op("dve", "max", reads=["score"], writes=["m8"], out=m8[:, 0:8], in_=score[:, :NSLC])
                    P.op("dve", "match_replace", reads=["m8", "score"], writes=["score2"], out=sc2[:, :NSLC], in_to_replace=m8[:, 0:8],
                         in_values=score[:, :NSLC], imm_value=-3e30)
                    P.op("dve", "max", reads=["score2"], writes=["m8"], out=m8[:, 8:16], in_=sc2[:, :NSLC])
                    P.op("dve", "tensor_reduce", reads=["m8"], writes=["thr"], out=thr[:], in_=m8[:, 8:16], axis=AX.X, op=ALU.min)
                    P.op("dve", "tensor_scalar", reads=["FT"], writes=["vm"], out=vm[:, :NSLC], in0=FT[:, qt * NSLC:(qt + 1) * NSLC], scalar1=-1.0,
                         scalar2=None, op0=ALU.is_ge)
                    P.op("dve", "scalar_tensor_tensor", reads=["score", "thr", "vm"], writes=["selb"], out=selb[:, :NSLC], in0=score[:, :NSLC],
                         scalar=thr[:, 0:1], in1=vm[:, :NSLC], op0=ALU.is_ge, op1=ALU.mult)
                    pm, pmk = pM.next()
                    P.op("pe", "transpose", reads=["selb", "identf"], writes=[pmk], out=pm[:NSLC, 0:128], in_=selb[:, :NSLC], identity=identf[:, :])
                    P.op("act", "activation", reads=[pmk], writes=["selT"], out=selT[:NSLC, :], in_=pm[:NSLC, 0:128], func=AF.Copy)
                    for k0 in range(0, qt + 1, 4):
                        kn = min(4, qt + 1 - k0)
                        pm, pmk = pM.next()
                        for kk in range(kn):
                            kt = k0 + kk
                            P.op("pe", "matmul", reads=["Eb", "selT"], writes=[pmk], out=pm[:, kk * 128:(kk + 1) * 128], lhsT=Eb[:, kt * 128:(kt + 1) * 128],
                                 rhs=selT[:NSLC, :], start=True, stop=True)
                        if k0 + kn - 1 == qt:
                            if kn > 1:
                                P.op("dve", "tensor_copy", reads=[pmk], writes=["msk"], out=msk[:, k0:k0 + kn - 1, :],
                                     in_=pm[:, 0:(kn - 1) * 128].rearrange("p (a q) -> p a q", q=128))
                            P.op("dve", "tensor_tensor", reads=[pmk, "CAUS"], writes=["msk"], out=msk[:, qt, :], in0=pm[:, (kn - 1) * 128:kn * 128],
                                 in1=CAUS[:], op=ALU.mult)
                        else:
                            P.op("dve", "tensor_copy", reads=[pmk], writes=["msk"], out=msk[:, k0:k0 + kn, :],
                                 in_=pm[:, 0:kn * 128].rearrange("p (a q) -> p a q", q=128))
                    units = []
                    for kt in range(qt + 1):
                        def post_s(PT, PTk, nk, kt=kt):
                            P.op("pool", "tensor_tensor", reads=[PTk, "msk"], writes=[PTk], out=PT[:, :].rearrange("p (a q) -> p a q", a=4),
                                 in0=PT[:, :].rearrange("p (a q) -> p a q", a=4), in1=msk[:, kt:kt + 1, :].to_broadcast([128, 4, 128]), op=ALU.mult)
                        units.append(dict(lhsT=KN[:, G, kt * 128:(kt + 1) * 128], lkey="KN", nk=128, bias=BK3[:, 4 * G:4 * G + 4, qt - kt],
                                          post=post_s, v=VN[:, kt, G, :], vkey="VN"))
                    pN, pNk = pNs.next()
                    pD, pDk = pDs.next()
                    run_branch(units, pN, pNk, pD, pDk)
                    finish_branch(pN, pNk, pD, pDk, G, 1, False)
                    P.op("act", "activation", reads=["oacc"], writes=["oTb"], out=oTb[:, 4 * G:4 * G + 4, :],
                         in_=oacc[:].rearrange("p (a q) -> p a q", a=4), func=AF.Copy)
                P.dma(OT_d[:, :, tq0:tq0 + 128], oTb[:], reads=["oTb"], writes=["OT_d%d" % tq0])

    def phase_mixC_out(self, l, j):
        P = self.P
        L, NS = self.L, self.NS
        NQT = L // 128
        OT_d = self.OT_d
        with self.phase("c4_%d" % l) as ph:
            stg = Rot([ph.sbuf("stg%d" % i, [64, 16, 512], F32) for i in range(2)], ph.name + "stg")
            wo = ph.sbuf("wo", [64, 16, D], BF16)
            wv = self.w["w_c_o"][j].rearrange("(h p) n -> p h n", p=64)
            for n0 in range(0, D, 512):
                s, sk = stg.next()
                P.dma(s[:], wv[:, :, n0:n0 + 512], writes=[sk])
                self.cast(wo[:, :, n0:n0 + 512], s[:], [sk], ["wo"])
            xts = Rot([ph.sbuf("xt%d" % i, [128, 1, D], F32) for i in range(2)], ph.name + "xt")
            oTs = Rot([ph.sbuf("oT%d" % i, [64, 16, 128], BF16) for i in range(2)], ph.name + "oT")
            pos = Rot([ph.psum("po%d" % i, [128, 512], F32) for i in range(2)], ph.name + "po")
            groups = [("p", qt * 128, 128) for qt in range(NQT)]
            if self.nsa_sample:
                groups.append(("s", 0, NS))
            for kind, r0, n in groups:
                xt, xtk = xts.next()
                self.load_x(xt, xtk, kind, r0, n)
                oT, oTk = oTs.next()
                if kind == "p":
                    P.dma(oT[:], OT_d[:, :, r0:r0 + n], reads=["OT_d%d" % r0], writes=[oTk])
                else:
                    P.dma(oT[:, :, :n], self.OTS_d[:, :, :], reads=["OTS_d"], writes=[oTk])
                for nh in range(2):
                    po, pok = pos.next()
                    for h in range(16):
                        P.op("pe", "matmul", reads=[oTk, "wo"], writes=[pok], out=po[:n, :], lhsT=oT[:, h, :n], rhs=wo[:, h, nh * 512:(nh + 1) * 512],
                             start=(h == 0), stop=(h == 15))
                    P.op("dve", "tensor_tensor", reads=[pok, xtk], writes=[xtk], out=xt[:n, 0, nh * 512:(nh + 1) * 512],
                         in0=xt[:n, 0, nh * 512:(nh + 1) * 512], in1=po[:n, :], op=ALU.add)
                self.store_x(xt, xtk, kind, r0, n)

    def idma(self, out, in_, idx_ap, reads, writes):
        P = self.P
        i = P.dma_rr
        P.dma_rr = (P.dma_rr + 1) % N_DMA_SEMS
        tr = ("dma", i)
        deps = P._deps(reads, writes)
        if P.dma_count[i] > 0:
            deps[tr] = max(deps.get(tr, 0), P.dma_count[i])
        waits = P._waits("pool", deps)
        P.dma_count[i] += 1
        P.streams["pool"].append(("idma", waits, dict(out=out, out_offset=None, in_=in_,
                                                      in_offset=bass.IndirectOffsetOnAxis(ap=idx_ap, axis=0)), P.dma_sems[i]))
        P._commit((tr, P.dma_count[i]), reads, writes)
        P.n_ops += 1

    def phase_mixC_sample(self, l, j):
        P = self.P
        NS = self.NS
        LP = 8192
        NPG = 64
        NCMP = 511
        NCT = 4
        pool = self.pool_rows
        for s in range(NS):
            with self.phase("cs1_%d_%d" % (l, s)) as ph:
                pti = ph.sbuf("pti", [128, NPG], I32)
                ptf = ph.sbuf("ptf", [128, NPG], F32)
                io = ph.sbuf("io", [128, 1], F32)
                idx = ph.sbuf("idx", [128, NPG], I32)
                P.dma(pti[:], self.page_table[s].partition_broadcast(128), writes=["pti"])
                P.op("pool", "iota", writes=["io"], out=io[:], pattern=[[0, 1]], base=0, channel_multiplier=1, allow_small_or_imprecise_dtypes=True)
                P.op("dve", "tensor_copy", reads=["pti"], writes=["ptf"], out=ptf[:], in_=pti[:])
                P.op("dve", "tensor_scalar", reads=["ptf", "io"], writes=["ptf"], out=ptf[:], in0=ptf[:], scalar1=128.0, scalar2=io[:, 0:1],
                     op0=ALU.mult, op1=ALU.add)
                P.op("dve", "tensor_copy", reads=["ptf"], writes=["idx"], out=idx[:], in_=ptf[:])
                pgs = Rot([ph.sbuf("pg%d" % i, [128, 1024], F32) for i in range(3)], ph.name + "pg")
                pgds = Rot([ph.sbuf("pgd%d" % i, [128, 8, 128], BF16) for i in range(2)], ph.name + "pgd")
                pkss = Rot([ph.sbuf("pks%d" % i, [128, 256], BF16) for i in range(2)], ph.name + "pks")
                vss = Rot([ph.sbuf("vs%d" % i, [128, 256], BF16) for i in range(2)], ph.name + "vs")
                c2s = Rot([ph.sbuf("c2s%d" % i, [128, 8, 128], BF16) for i in range(2)], ph.name + "c2s")
                kts = Rot([ph.sbuf("kts%d" % i, [64, 4, 128], BF16) for i in range(2)], ph.name + "kts")
                pTs = Rot([ph.psum("pT%d" % i, [128, 8, 128], BF16) for i in range(2)], ph.name + "pT")
                pKs = Rot([ph.psum("pK%d" % i, [64, 4, 128], BF16) for i in range(2)], ph.name + "pK")
                c2v = self.C2S_d.rearrange("(b p) t -> p b t", p=128)
                for lp in range(NPG):
                    pg, pgk = pgs.next()
                    self.idma(pg[:], pool, idx[:, lp:lp + 1], ["idx"], [pgk])
                    pgd, pgdk = pgds.next()
                    src = pg[:, 0:512].rearrange("p (b d) -> p b d", d=64)
                    P.op("dve", "tensor_copy", reads=[pgk], writes=[pgdk], out=pgd[:, :, 0:64], in_=src)
                    P.op("pool", "tensor_copy", reads=[pgk], writes=[pgdk], out=pgd[:, :, 64:128], in_=src)
                    pks, pksk = pkss.next()
                    P.op("act", "activation", reads=[pgk], writes=[pksk], out=pks[:], in_=pg[:, 512:768], func=AF.Copy)
                    vs_, vsk = vss.next()
                    P.op("act", "activation", reads=[pgk], writes=[vsk], out=vs_[:], in_=pg[:, 768:1024], func=AF.Copy)
                    pT, pTk = pTs.next()
                    for b8 in range(8):
                        P.op("pe", "transpose", reads=[pgdk, "ident"], writes=[pTk], out=pT[:, b8, :], in_=pgd[:, b8, :], identity=self.ident[:, :])
                    c2, c2k = c2s.next()
                    P.op("dve", "tensor_copy", reads=[pTk], writes=[c2k], out=c2[:], in_=pT[:])
                    pK, pKk = pKs.next()
                    for G in range(4):
                        P.op("pe", "transpose", reads=[pksk, "ident"], writes=[pKk], out=pK[:, G, :], in_=pks[:, G * 64:(G + 1) * 64], identity=self.ident[:, :])
                    kt_, ktk = kts.next()
                    P.op("act", "activation", reads=[pKk], writes=[ktk], out=kt_[:], in_=pK[:], func=AF.Copy)
                    t0 = lp * 128
                    P.dma(c2v[0:64, :, t0:t0 + 128], c2[0:64, :, :], reads=[c2k], writes=["C2S_d"])
                    if lp == 0:
                        P.dma(c2v[64:128, :, 0:127], c2[64:128, :, 1:128], reads=[c2k], writes=["C2S_d"])
                    else:
                        P.dma(c2v[64:128, :, t0 - 1:t0 + 127], c2[64:128, :, :], reads=[c2k], writes=["C2S_d"])
                    P.dma(self.KSS_d[:, :, t0:t0 + 128], kt_[:], reads=[ktk], writes=["KSS_d"])
                    P.dma(self.VSS_d[t0:t0 + 128, :], vs_[:], reads=[vsk], writes=["VSS_d"])
            with self.phase("cs2_%d_%d" % (l, s)) as ph:
                stg = Rot([ph.sbuf("stg%d" % i, [128, 16, 128], F32) for i in range(2)], ph.name + "stg")
                w1 = []
                for ti, nm in enumerate(("w_c_k1", "w_c_v1")):
                    w1.append(self.load_weight(ph, "w1_%d" % ti, self.w[nm][j], 2048, stg, ncols=128, CH=128, KCH=16))
                w2, posb = [], []
                pw = Rot([ph.sbuf("pw%d" % i, [128, 16], F32) for i in range(2)], ph.name + "pw")
                for ti, (nm, pn) in enumerate((("w_c_k2", "c_pos_k"), ("w_c_v2", "c_pos_v"))):
                    sg, sk = stg.next()
                    P.dma(sg[:, 0, 0:64], self.w[nm][j], writes=[sk])
                    wb2 = ph.sbuf("w2_%d" % ti, [128, 64], BF16)
                    P.op("dve", "tensor_copy", reads=[sk], writes=["w2_%d" % ti], out=wb2[:], in_=sg[:, 0, 0:64])
                    w2.append((wb2, "w2_%d" % ti))
                    pf, pfk = pw.next()
                    P.dma(pf[:], self.w[pn][j].rearrange("(k a) d -> (a d) k", a=2), writes=[pfk], allow_slow_non_contiguous=True)
                    pb_ = ph.sbuf("posb%d" % ti, [128, 16], BF16)
                    P.op("dve", "tensor_copy", reads=[pfk], writes=["posb%d" % ti], out=pb_[:], in_=pf[:])
                    posb.append((pb_, "posb%d" % ti))
                tmp = {"x": Rot([ph.sbuf("gx%d" % i, [128, 128], F32) for i in range(2)], ph.name + "gx"),
                       "u": Rot([ph.sbuf("gu%d" % i, [128, 128], F32) for i in range(2)], ph.name + "gu")}
                php = Rot([ph.psum("php%d" % i, [128, 512], F32) for i in range(1)], ph.name + "php")
                pco = Rot([ph.psum("pco%d" % i, [128, 512], F32) for i in range(1)], ph.name + "pco")
                pss = Rot([ph.psum("ps%d" % i, [128, 512], F32) for i in range(2)], ph.name + "ps")
                pN = ph.psum("pN", [128, 512], F32)
                pD = ph.psum("pD", [128, 512], F32)
                pM = Rot([ph.psum("pM0", [128, 512], F32)], ph.name + "pM")
                hbs = []
                for ti in range(2):
                    pc, pck = pco.next()
                    for kc in range(16):
                        P.op("pe", "matmul", reads=[w1[ti][1], posb[ti][1]], writes=[pck], out=pc[:, 0:1], lhsT=w1[ti][0][:, kc, :],
                             rhs=posb[ti][0][:, kc:kc + 1], start=(kc == 0), stop=(kc == 15))
                    hb = ph.sbuf("hb%d" % ti, [128, 1], F32)
                    P.op("dve", "tensor_copy", reads=[pck], writes=["hb"], out=hb[:], in_=pc[:, 0:1])
                    hbs.append(hb)
                X2s = Rot([ph.sbuf("X2_%d" % i, [128, LP], BF16) for i in range(1)], ph.name + "X2")
                gels = Rot([ph.sbuf("gel%d" % i, [128, 128], BF16) for i in range(2)], ph.name + "gel")
                KC = ph.sbuf("KC", [64, 4, 512], BF16)
                VC = ph.sbuf("VC", [128, NCT, 4, 64], BF16)
                for ti in range(2):
                    for G in range(4):
                        X2, X2k = X2s.next()
                        b8 = ti * 4 + G
                        P.dma(X2[:], self.C2S_d[b8 * 128:(b8 + 1) * 128, 0:LP], reads=["C2S_d"], writes=[X2k])
                        for ct in range(NCT):
                            c0 = ct * 128
                            nk = min(128, NCMP - c0)
                            hp, hpk = php.next()
                            for kc in range(16):
                                st0 = 16 * c0 + 2 * kc
                                P.op("pe", "matmul", reads=[w1[ti][1], X2k], writes=[hpk], out=hp[:, :nk], lhsT=w1[ti][0][:, kc, :],
                                     rhs=X2[:, st0:st0 + 16 * (nk - 1) + 1:16], start=(kc == 0), stop=(kc == 15))
                            gel, gelk = gels.next()
                            self.gelu_tanh(ph, tmp, hp[:, :nk], hpk, hbs[ti][:, 0:1], nk, gel[:, :nk], gelk)
                            pc, pck = pco.next()
                            if ti == 0:
                                P.op("pe", "matmul", reads=[gelk, w2[0][1]], writes=[pck], out=pc[:64, :nk], lhsT=w2[0][0][:, :], rhs=gel[:, :nk],
                                     start=True, stop=True)
                                P.op("act", "activation", reads=[pck], writes=["KC"], out=KC[:, G, c0:c0 + nk], in_=pc[:64, :nk], func=AF.Copy)
                            else:
                                P.op("pe", "matmul", reads=[gelk, w2[1][1]], writes=[pck], out=pc[:nk, :64], lhsT=gel[:, :nk], rhs=w2[1][0][:, :],
                                     start=True, stop=True)
                                P.op("act", "activation", reads=[pck], writes=["VC"], out=VC[:nk, ct, G, :], in_=pc[:nk, :64], func=AF.Copy)
                def cload(name, src, shape2, dt=F32):
                    t = ph.sbuf(name, shape2, F32)
                    P.dma(t[:], src, writes=[name])
                    if dt == F32:
                        return t
                    d = ph.sbuf(name + "b", shape2, BF16)
                    self.cast(d[:], t[:], [name], [name + "b"])
                    return d
                BSC = cload("BSC", self.c_BSC, [128, 4 * 16])
                BSS = cload("BSS", self.c_BSSEL, [128, 64 * 16])
                BSW = cload("BSW", self.c_BSWIN, [128, 4 * 16])
                COVS = cload("COVS", self.c_COVS, [128, 4 * 128], BF16)
                FTS = cload("FTS", self.c_FTS, [4, 129])
                E2 = ph.sbuf("E2", [128, 8192], BF16)
                for c0 in range(0, 8192, 2048):
                    sg, sk = stg.next()
                    P.dma(sg[:].rearrange("p a b -> p (a b)"), self.c_E2[:, c0:c0 + 2048], writes=[sk])
                    self.cast(E2[:, c0:c0 + 2048], sg[:].rearrange("p a b -> p (a b)"), [sk], ["E2"])
                SEL = ph.sbuf("SELG", [48, 48 * 64], BF16)
                for c0 in range(0, 48 * 64, 1536):
                    sg, sk = stg.next()
                    P.dma(sg[:48].rearrange("p a b -> p (a b)")[:, 0:1536], self.c_SELG[:, c0:c0 + 1536], writes=[sk])
                    self.cast(SEL[:, c0:c0 + 1536], sg[:48].rearrange("p a b -> p (a b)")[:, 0:1536], [sk], ["SELG"])
                QS = ph.sbuf("QS", [64, 16, NS], BF16)
                P.dma(QS[:], self.QNS_d, reads=["QNS_d"], writes=["QS"])
                KNn = ph.sbuf("KNn", [64, 8, NS], BF16)
                P.dma(KNn[:], self.KNS_d, reads=["KNS_d"], writes=["KNn"])
                GTs = ph.sbuf("GTs", [48, NS], BF16)
                P.dma(GTs[:], self.GTS_d, reads=["GTS_d"], writes=["GTs"])
                vrow = ph.sbuf("vrow", [1, 512], BF16)
                P.dma(vrow[:], self.VNS_d[s:s + 1, :], reads=["VNS_d"], writes=["vrow"])
                q4 = ph.sbuf("q4", [64, 4, 4], BF16)
                P.op("dve", "tensor_copy", reads=["QS"], writes=["q4"], out=q4[:].rearrange("p g i -> p (g i)"), in_=QS[:, :, s])
                KSS = ph.sbuf("KSS", [64, 4, LP], BF16)
                for c0 in range(0, LP, 2048):
                    P.dma(KSS[:, :, c0:c0 + 2048], self.KSS_d[:, :, c0:c0 + 2048], reads=["KSS_d"], writes=["KSS"])
                VSS = ph.sbuf("VSS", [128, NPG, 256], BF16)
                P.dma(VSS[:], self.VSS_d.rearrange("(t p) f -> p t f", p=128), reads=["VSS_d"], writes=["VSS"])
                wf = ph.sbuf("wf", [128, 4, 512], F32)
                P.dma(wf[:], self.cache_nsa_win[s].rearrange("(t p) f -> p t f", p=128), writes=["wf"])
                wb = ph.sbuf("wb", [128, 4, 512], BF16)
                self.cast(wb[:], wf[:], ["wf"], ["wb"])
                KW = ph.sbuf("KW", [64, 4, 512], BF16)
                pTw = ph.psum("pTw", [64, 4, 128], BF16)
                for kt in range(4):
                    for G in range(4):
                        P.op("pe", "transpose", reads=["wb", "ident"], writes=["pTw"], out=pTw[:, G, :], in_=wb[:, kt, G * 64:(G + 1) * 64],
                             identity=self.ident[:, :])
                    P.op("act", "activation", reads=["pTw"], writes=["KW"], out=KW[:, :, kt * 128:(kt + 1) * 128], in_=pTw[:, :, :], func=AF.Copy)
                t32 = Rot([ph.sbuf("t32_%d" % i, [128, 32], F32) for i in range(2)], ph.name + "t32")
                PTs = Rot([ph.sbuf("PT%d" % i, [128, 32], BF16) for i in range(3)], ph.name + "PT")
                impT4 = ph.sbuf("impT4", [128, 4], F32)
                rdc = ph.sbuf("rdc", [128, 4], F32)
                impn = ph.sbuf("impn", [128, 4], F32)
                score = ph.sbuf("score", [4, 136], F32)
                sc2 = ph.sbuf("score2", [4, 136], F32)
                m8 = ph.sbuf("m8", [4, 16], F32)
                thr = ph.sbuf("thr", [4, 1], F32)
                self_ = ph.sbuf("self", [4, 128], F32)
                selT = ph.sbuf("selT", [128, 4], BF16)
                mks = ph.sbuf("mks", [128, 64, 4], F32)
                oacc = ph.sbuf("oacc", [64, 4, 4], F32)
                nsum = ph.sbuf("nsum", [64, 3, 4, 4], F32)
                rden = ph.sbuf("rden", [64, 3, 4, 4], F32)
                gbs = ph.sbuf("gbs", [64, 3, 4, 4], F32)
                oTs = ph.sbuf("oTs", [64, 16], BF16)
                for G in range(4):
                    for ct in range(NCT):
                        nk = min(128, NCMP - ct * 128)
                        ps, psk = pss.next()
                        P.op("pe", "matmul", reads=["KC", "q4"], writes=[psk], out=ps[:nk, 0:4], lhsT=KC[:, G, ct * 128:ct * 128 + nk], rhs=q4[:, G, :],
                             start=True, stop=True)
                        tt, ttk = t32.next()
                        P.op("dve", "tensor_tensor", reads=[psk, "BSC"], writes=[ttk], out=tt[:nk, 0:4], in0=ps[:nk, 0:4],
                             in1=BSC[:nk, ct * 16 + 4 * G:ct * 16 + 4 * G + 4], op=ALU.add)
                        PT, PTk = PTs.next()
                        P.op("act", "activation", reads=[ttk], writes=[PTk], out=PT[:nk, 0:4], in_=tt[:nk, 0:4], func=AF.Exp)
                        fl = dict(start=(ct == 0), stop=(ct == NCT - 1))
                        P.op("pe", "matmul", reads=["VC", PTk], writes=["pN"], out=pN[:64, 0:4], lhsT=VC[:nk, ct, G, :], rhs=PT[:nk, 0:4], **fl)
                        P.op("pe", "matmul", reads=["ones", PTk], writes=["pD"], out=pD[:, 0:4], lhsT=self.ones[:nk, :], rhs=PT[:nk, 0:4], **fl)
                        pm, pmk = pM.next() if ct == 0 else (pm, pmk)
                        P.op("pe", "matmul", reads=["COVSb", PTk], writes=[pmk], out=pm[:, 0:4], lhsT=COVS[:nk, ct * 128:(ct + 1) * 128], rhs=PT[:nk, 0:4], **fl)
                    P.op("dve", "tensor_copy", reads=["pN"], writes=["nsum"], out=nsum[:, 0, G, :], in_=pN[:64, 0:4])
                    P.op("dve", "reciprocal", reads=["pD"], writes=["rdc"], out=rdc[:], in_=pD[:, 0:4])
                    P.op("dve", "tensor_copy", reads=["rdc"], writes=["rden"], out=rden[:, 0, G, :], in_=rdc[:64, :])
                    P.op("dve", "tensor_tensor", reads=[pmk, "rdc"], writes=["impn"], out=impn[:], in0=pm[:, 0:4], in1=rdc[:], op=ALU.mult)
                    P.op("dve", "tensor_reduce", reads=["impn"], writes=["impT4"], out=impT4[:, G:G + 1], in_=impn[:], axis=AX.X, op=ALU.add)
                pm, pmk = pM.next()
                P.op("pe", "transpose", reads=["impT4", "identf"], writes=[pmk], out=pm[:4, 0:128], in_=impT4[:, :], identity=self.identf[:, :])
                P.op("dve", "tensor_copy", reads=["FTS"], writes=["score"], out=score[:, 0:129], in_=FTS[:, :])
                P.op("dve", "tensor_tensor", reads=[pmk, "score"], writes=["score"], out=score[:, 0:128], in0=pm[:4, 0:128], in1=score[:, 0:128], op=ALU.add)
                P.op("dve", "max", reads=["score"], writes=["m8"], out=m8[:, 0:8], in_=score[:, 0:129])
                P.op("dve", "match_replace", reads=["m8", "score"], writes=["score2"], out=sc2[:, 0:129], in_to_replace=m8[:, 0:8],
                     in_values=score[:, 0:129], imm_value=-3e30)
                P.op("dve", "max", reads=["score2"], writes=["m8"], out=m8[:, 8:16], in_=sc2[:, 0:129])
                P.op("dve", "tensor_reduce", reads=["m8"], writes=["thr"], out=thr[:], in_=m8[:, 8:16], axis=AX.X, op=ALU.min)
                P.op("dve", "tensor_scalar", reads=["score", "thr"], writes=["self"], out=self_[:], in0=score[:, 0:128], scalar1=thr[:, 0:1], scalar2=None,
                     op0=ALU.is_ge)
                pm, pmk = pM.next()
                P.op("pe", "transpose", reads=["self", "identf"], writes=[pmk], out=pm[:, 0:4], in_=self_[:, :], identity=self.identf[:4, :4])
                P.op("act", "activation", reads=[pmk], writes=["selT"], out=selT[:], in_=pm[:, 0:4], func=AF.Copy)
                pm, pmk = pM.next()
                for kt in range(NPG):
                    P.op("pe", "matmul", reads=["E2", "selT"], writes=[pmk], out=pm[:, kt * 4:kt * 4 + 4], lhsT=E2[:, kt * 128:(kt + 1) * 128], rhs=selT[:, :],
                         start=True, stop=True)
                P.op("dve", "tensor_copy", reads=[pmk], writes=["mks"], out=mks[:].rearrange("p k g -> p (k g)"), in_=pm[:, 0:256])
                KB = 8
                for G in range(4):
                    nacc = 0
                    for k0 in range(0, NPG, KB):
                        ps, psk = pss.next()
                        for kk in range(KB):
                            kt = k0 + kk
                            P.op("pe", "matmul", reads=["KSS", "q4"], writes=[psk], out=ps[:, kk * 4:kk * 4 + 4], lhsT=KSS[:, G, kt * 128:(kt + 1) * 128],
                                 rhs=q4[:, G, :], start=True, stop=True)
                        tt, ttk = t32.next()
                        bias = BSS[:, :].rearrange("p (k h) -> p k h", h=16)[:, k0:k0 + KB, 4 * G:4 * G + 4]
                        P.op("dve", "tensor_tensor", reads=[psk, "BSS"], writes=[ttk], out=tt[:, :].rearrange("p (k h) -> p k h", h=4),
                             in0=ps[:, 0:4 * KB].rearrange("p (k h) -> p k h", h=4), in1=bias, op=ALU.add)
                        PT, PTk = PTs.next()
                        P.op("act", "activation", reads=[ttk], writes=[PTk], out=PT[:, :], in_=tt[:, :], func=AF.Exp)
                        P.op("pool", "tensor_tensor", reads=[PTk, "mks"], writes=[PTk], out=PT[:, :].rearrange("p (k h) -> p k h", h=4),
                             in0=PT[:, :].rearrange("p (k h) -> p k h", h=4), in1=mks[:, k0:k0 + KB, G:G + 1].to_broadcast([128, KB, 4]), op=ALU.mult)
                        for kk in range(KB):
                            kt = k0 + kk
                            P.op("pe", "matmul", reads=["VSS", PTk], writes=["pN"], out=pN[:64, 0:4], lhsT=VSS[:, kt, G * 64:(G + 1) * 64],
                                 rhs=PT[:, kk * 4:kk * 4 + 4], start=(nacc == 0), stop=False)
                            P.op("pe", "matmul", reads=["ones", PTk], writes=["pD"], out=pD[:64, 0:4], lhsT=self.ones[:, 0:64], rhs=PT[:, kk * 4:kk * 4 + 4],
                                 start=(nacc == 0), stop=False)
                            nacc += 1
                    ps, psk = pss.next()
                    P.op("pe", "matmul", reads=["KNn", "q4"], writes=[psk], out=ps[0:1, 0:4], lhsT=KNn[:, G, s:s + 1], rhs=q4[:, G, :], start=True, stop=True)
                    PT, PTk = PTs.next()
                    P.op("act", "activation", reads=[psk], writes=[PTk], out=PT[0:1, 0:4], in_=ps[0:1, 0:4], func=AF.Exp)
                    P.op("pe", "matmul", reads=["vrow", PTk], writes=["pN"], out=pN[:64, 0:4], lhsT=vrow[0:1, G * 64:(G + 1) * 64], rhs=PT[0:1, 0:4],
                         start=False, stop=True)
                    P.op("pe", "matmul", reads=["ones", PTk], writes=["pD"], out=pD[:64, 0:4], lhsT=self.ones[0:1, 0:64], rhs=PT[0:1, 0:4], start=False, stop=True)
                    P.op("dve", "tensor_copy", reads=["pN"], writes=["nsum"], out=nsum[:, 1, G, :], in_=pN[:64, 0:4])
                    P.op("dve", "reciprocal", reads=["pD"], writes=["rden"], out=rden[:, 1, G, :], in_=pD[:64, 0:4])
                    ps, psk = pss.next()
                    for kt in range(4):
                        P.op("pe", "matmul", reads=["KW", "q4"], writes=[psk], out=ps[:, kt * 4:kt * 4 + 4], lhsT=KW[:, G, kt * 128:(kt + 1) * 128],
                             rhs=q4[:, G, :], start=True, stop=True)
                    tt, ttk = t32.next()
                    bias = BSW[:, :].rearrange("p (k h) -> p k h", h=16)[:, :, 4 * G:4 * G + 4]
                    P.op("dve", "tensor_tensor", reads=[psk, "BSW"], writes=[ttk], out=tt[:, 0:16].rearrange("p (k h) -> p k h", h=4),
                         in0=ps[:, 0:16].rearrange("p (k h) -> p k h", h=4), in1=bias, op=ALU.add)
                    PT, PTk = PTs.next()
                    P.op("act", "activation", reads=[ttk], writes=[PTk], out=PT[:, 0:16], in_=tt[:, 0:16], func=AF.Exp)
                    for kt in range(4):
                        P.op("pe", "matmul", reads=["wb", PTk], writes=["pN"], out=pN[:64, 0:4], lhsT=wb[:, kt, 256 + G * 64:256 + (G + 1) * 64],
                             rhs=PT[:, kt * 4:kt * 4 + 4], start=(kt == 0), stop=False)
                        P.op("pe", "matmul", reads=["ones", PTk], writes=["pD"], out=pD[:64, 0:4], lhsT=self.ones[:, 0:64], rhs=PT[:, kt * 4:kt * 4 + 4],
                             start=(kt == 0), stop=False)
                    ps, psk = pss.next()
                    P.op("pe", "matmul", reads=["KNn", "q4"], writes=[psk], out=ps[0:1, 0:4], lhsT=KNn[:, 4 + G, s:s + 1], rhs=q4[:, G, :], start=True, stop=True)
                    PT, PTk = PTs.next()
                    P.op("act", "activation", reads=[psk], writes=[PTk], out=PT[0:1, 0:4], in_=ps[0:1, 0:4], func=AF.Exp)
                    P.op("pe", "matmul", reads=["vrow", PTk], writes=["pN"], out=pN[:64, 0:4], lhsT=vrow[0:1, 256 + G * 64:256 + (G + 1) * 64], rhs=PT[0:1, 0:4],
                         start=False, stop=True)
                    P.op("pe", "matmul", reads=["ones", PTk], writes=["pD"], out=pD[:64, 0:4], lhsT=self.ones[0:1, 0:64], rhs=PT[0:1, 0:4], start=False, stop=True)
                    P.op("dve", "tensor_copy", reads=["pN"], writes=["nsum"], out=nsum[:, 2, G, :], in_=pN[:64, 0:4])
                    P.op("dve", "reciprocal", reads=["pD"], writes=["rden"], out=rden[:, 2, G, :], in_=pD[:64, 0:4])
                pm, pmk = pM.next()
                for b in range(3):
                    for h in range(16):
                        r = h * 3 + b
                        P.op("pe", "matmul", reads=["SELG", "GTs"], writes=[pmk], out=pm[:64, b * 16 + h:b * 16 + h + 1], lhsT=SEL[:, r * 64:(r + 1) * 64],
                             rhs=GTs[:, s:s + 1], start=True, stop=True)
                P.op("dve", "tensor_copy", reads=[pmk], writes=["gbs"], out=gbs[:].rearrange("p b g i -> p (b g i)"), in_=pm[:64, 0:48])
                P.op("dve", "tensor_tensor", reads=["nsum", "rden"], writes=["nsum"], out=nsum[:].rearrange("p b g i -> p (b g i)"),
                     in0=nsum[:].rearrange("p b g i -> p (b g i)"), in1=rden[:].rearrange("p b g i -> p (b g i)"), op=ALU.mult)
                P.op("dve", "tensor_tensor", reads=["nsum", "gbs"], writes=["nsum"], out=nsum[:].rearrange("p b g i -> p (b g i)"),
                     in0=nsum[:].rearrange("p b g i -> p (b g i)"), in1=gbs[:].rearrange("p b g i -> p (b g i)"), op=ALU.mult)
                P.op("dve", "tensor_tensor", reads=["nsum"], writes=["oacc"], out=oacc[:].rearrange("p g i -> p (g i)"),
                     in0=nsum[:, 0].rearrange("p g i -> p (g i)"), in1=nsum[:, 1].rearrange("p g i -> p (g i)"), op=ALU.add)
                P.op("dve", "tensor_tensor", reads=["nsum", "oacc"], writes=["oacc"], out=oacc[:].rearrange("p g i -> p (g i)"),
                     in0=oacc[:].rearrange("p g i -> p (g i)"), in1=nsum[:, 2].rearrange("p g i -> p (g i)"), op=ALU.add)
                P.op("act", "activation", reads=["oacc"], writes=["oTs"], out=oTs[:], in_=oacc[:].rearrange("p g i -> p (g i)"), func=AF.Copy)
                P.dma(self.OTS_d[:, :, s], oTs[:], reads=["oTs"], writes=["OTS_d"], allow_slow_non_contiguous=True)

    def declare_io(self):
        L, NS = self.L, self.NS
        self.x_prompt = self.inp("x_prompt", [L, D])
        self.x_sample = self.inp("x_sample", [NS, D])
        self.mem_prompt = self.inp("mem_prompt", [MEM, D])
        self.cache_mem = self.inp("cache_mem_kv", [DEPTH, NS, MEM, 2 * D])
        self.w = {}
        for nm, shp in (("g_mix", [DEPTH, D]), ("g_cross", [DEPTH, D]), ("g_mem", [DEPTH, D]), ("g_ffn", [DEPTH, D]),
                        ("g_final", [D]), ("w_ffn_in", [DEPTH, D, 2 * FH]), ("w_ffn_out", [DEPTH, FH, D]),
                        ("w_x_q", [DEPTH, D, D]), ("w_x_kv", [DEPTH, D, 2 * D]), ("w_x_o", [DEPTH, D, D])):
            self.w[nm] = self.inp(nm, shp)
        for nm, shp in (("w_a_qkv", [2, D, 4608]), ("w_a_o", [2, 512, D])):
            self.w[nm] = self.inp(nm, shp)
        self.cache_dil128 = self.inp("cache_dil_w128", [2, NS, 128, D])
        self.cache_dil512 = self.inp("cache_dil_w512", [2, NS, 512, D])
        self.cache_dil2048 = self.inp("cache_dil_w2048", [2, NS, 2048, D])
        self.c_RA = self.inp("c_RA", [128, 24 * 128])
        self.c_R0 = self.inp("c_R0", [128, 24 * 128])
        self.c_MA = self.inp("c_MA", [128, 24 * 128])
        self.c_CBA = self.inp("c_CBA", [128, 24 * 17])
        self.c_CBS = self.inp("c_CBS", [128, 24])
        self.o_dil128_p = self.out("dil128_prompt", [2, min(128, L), D])
        self.o_dil512_p = self.out("dil512_prompt", [2, min(512, L), D])
        self.o_dil2048_p = self.out("dil2048_prompt", [2, min(2048, L), D])
        self.o_dil128_s = self.out("dil128_sample", [2, NS, D])
        self.o_dil512_s = self.out("dil512_sample", [2, NS, D])
        self.o_dil2048_s = self.out("dil2048_sample", [2, NS, D])
        self.QT_d = self.scratch("QT_d", [1536, L], BF16)
        self.KT_d = self.scratch("KT_d", [1536, L], BF16)
        self.V_d = self.scratch("V_d", [L, 1536], BF16)
        self.VS_d = self.scratch("VS_d", [NS, 1536], BF16)
        for nm, shp in (("w_b_in", [1, D, 3088]), ("w_b_gate2", [1, 16, 512]), ("b_b_gate", [1, 512]), ("g_b_head", [1, 4, 256]),
                        ("w_b_o", [1, D, D])):
            self.w[nm] = self.inp(nm, shp)
        self.state_gla = self.inp("state_gla", [NS, 4, 128, 256])
        self.c_TRI4 = self.inp("c_TRI4", [64, 256])
        self.o_gla_p = self.out("gla_prompt", [4, 128, 256])
        self.o_gla_s = self.out("gla_sample", [NS, 4, 128, 256])
        self.QK_d = self.scratch("QK_d", [1024, L])
        self.LA_d = self.scratch("LA_d", [L, 512])
        self.VB_d = self.scratch("VB_d", [L, 1024], BF16)
        self.SR_d = self.scratch("SR_d", [L, 1024])
        self.O_d = self.scratch("O_d", [L, 1024])
        self.QKS_d = self.scratch("QKS_d", [1024, NS])
        self.LAS_d = self.scratch("LAS_d", [NS, 512])
        self.VBS_d = self.scratch("VBS_d", [NS, 1024], BF16)
        self.SRS_d = self.scratch("SRS_d", [NS, 1024])
        self.OS_d = self.scratch("OS_d", [NS, 1024])
        for nm, shp in (("w_c_in", [1, D, 2608]), ("b_c_gate", [1, 48]), ("c_pos_k", [1, 32, 64]), ("c_pos_v", [1, 32, 64]),
                        ("w_c_k1", [1, 2048, 128]), ("w_c_k2", [1, 128, 64]), ("w_c_v1", [1, 2048, 128]), ("w_c_v2", [1, 128, 64]),
                        ("w_c_o", [1, D, D])):
            self.w[nm] = self.inp(nm, shp)
        NQT, NCT, NSLC = L // 128, (L // 16 - 1 + 127) // 128, L // 64
        self.c_BK = self.inp("c_BK", [128, 16 * 33])
        self.c_BKC = self.inp("c_BKC", [128, 16 * NQT * NCT])
        self.c_FT = self.inp("c_FT", [128, NQT * NSLC])
        self.c_E = self.inp("c_E", [NSLC, L])
        self.c_CAUS = self.inp("c_CAUS", [128, 128])
        self.c_COVER = self.inp("c_COVER", [128, NCT * NSLC])
        self.c_SELG = self.inp("c_SELG", [48, 48 * 64])
        self.c_BSC = self.inp("c_BSC", [128, 64])
        self.c_BSSEL = self.inp("c_BSSEL", [128, 1024])
        self.c_BSWIN = self.inp("c_BSWIN", [128, 64])
        self.c_COVS = self.inp("c_COVS", [128, 512])
        self.c_E2 = self.inp("c_E2", [128, 8192])
        self.c_FTS = self.inp("c_FTS", [4, 129])
        if self.nsa_sample:
            self.pool_rows = self.inp("cache_nsa_kv", [self.n_pool * 128, 1024])
            self.page_table = self.inp("page_table", [NS, 64], I32)
            self.cache_nsa_win = self.inp("cache_nsa_win", [NS, 512, 512])
        self.C2S_d = self.scratch("C2S_d", [1024, 8192], BF16)
        self.KSS_d = self.scratch("KSS_d", [64, 4, 8192], BF16)
        self.VSS_d = self.scratch("VSS_d", [8192, 256], BF16)
        self.o_nsa_win_p = self.out("nsa_win_prompt", [min(512, L), 512])
        self.o_nsa_win_s = self.out("nsa_win_sample", [NS, 512])
        self.o_nsa_kv_p = self.out("nsa_kv_prompt", [L, 1024])
        self.o_nsa_kv_s = self.out("nsa_kv_sample", [NS, 1024])
        self.QN_d = self.scratch("QN_d", [64, 16, L], BF16)
        self.KN_d = self.scratch("KN_d", [64, 8, L], BF16)
        self.C2_d = self.scratch("C2_d", [1024, L], BF16)
        self.VN_d = self.scratch("VN_d", [L, 512], BF16)
        self.GT_d = self.scratch("GT_d", [48, L], BF16)
        self.KC_d = self.scratch("KC_d", [64, 4, 256], BF16)
        self.VC_d = self.scratch("VC_d", [256, 4, 64], BF16)
        self.OT_d = self.scratch("OT_d", [64, 16, L], BF16)
        self.QNS_d = self.scratch("QNS_d", [64, 16, NS], BF16)
        self.KNS_d = self.scratch("KNS_d", [64, 8, NS], BF16)
        self.GTS_d = self.scratch("GTS_d", [48, NS], BF16)
        self.VNS_d = self.scratch("VNS_d", [NS, 512], BF16)
        self.OTS_d = self.scratch("OTS_d", [64, 16, NS], BF16)
        self.y_prompt = self.out("y_prompt", [L, D])
        self.y_sample = self.out("y_sample", [NS, D])
        self.o_mem = self.out("mem_kv_prompt", [DEPTH, MEM, 2 * D])
        self.X = self.scratch("X", [L, D])
        self.XS = self.scratch("XS", [NS, D])

    def build(self):
        self.declare_io()
        with ExitStack() as st:
            self.P = Prog(self.nc, st)
            self.setup_consts(st)
            self.phase_init(self.x_prompt, self.x_sample)
            for l in self.layers:
                if "mix" in self.parts and l % 3 == 0:
                    self.phase_mixA(l, l // 3)
                if "mix" in self.parts and l % 3 == 1:
                    self.phase_mixB(l, l // 3)
                if "mix" in self.parts and l % 3 == 2:
                    self.phase_mixC(l, l // 3)
                    if self.nsa_sample:
                        self.phase_mixC_sample(l, l // 3)
                    self.phase_mixC_out(l, l // 3)
                if "cross" in self.parts:
                    self.phase_cross(l)
                if "ffn" in self.parts:
                    self.phase_ffn(l)
            self.phase_final(self.y_prompt, self.y_sample)
            self.P.barrier()
            self.P.emit()
        return self.nc


def host_consts(L=4096):
    sl = (2.0 ** (-8.0 * np.arange(1, 25) / 24)).astype(np.float64)
    k = np.arange(128)[:, None]
    q = np.arange(128)[None, :]
    WIN = (128, 512, 2048)
    DIL = (1, 4, 16)
    RA = np.zeros((128, 24, 128), np.float64)
    R0 = np.zeros((128, 24, 128), np.float64)
    for h in range(24):
        RA[:, h, :] = -sl[h] * (q - k)
        R0[:, h, :] = np.where(q >= k, -sl[h] * (q - k), -30000.0)
    MA = np.zeros((128, 24, 128), np.float64)
    idx = 0
    for g in range(3):
        for dl in range(WIN[g] // 128 + 1):
            dist = 128 * dl + q - k
            MA[:, idx, :] = ((dist >= 0) & (dist <= WIN[g]) & (dist % DIL[g] == 0))
            idx += 1
    CBA = np.zeros((128, 24, 17), np.float64)
    for h in range(24):
        for dl in range(17):
            CBA[:, h, dl] = -sl[h] * 128 * dl
    CBS = np.zeros((128, 24), np.float64)
    for h in range(24):
        CBS[:, h] = -sl[h] * DIL[h // 8] * (128 - np.arange(128))
    f = lambda a: np.ascontiguousarray(a.reshape(128, -1)).astype(np.float32)
    NQT, NCMP, NSLC = L // 128, L // 16 - 1, L // 64
    NCT = (NCMP + 127) // 128
    slc = 2.0 ** (-8.0 * np.arange(1, 17) / 16)
    kk = np.arange(128)
    BK = np.zeros((128, 16, 33))
    for h in range(16):
        for dl in range(33):
            BK[:, h, dl] = slc[h] * (-128 * dl + kk - 64)
    BKC = np.zeros((128, 16, NQT, NCT))
    for h in range(16):
        for qt in range(NQT):
            for ct in range(NCT):
                BKC[:, h, qt, ct] = np.minimum(slc[h] * (16 * (128 * ct + kk) + 31 - (128 * qt + 64)), 60.0)
    FT = np.zeros((128, NQT, NSLC))
    jj = np.arange(NSLC)[None, :]
    for qt in range(NQT):
        t = (128 * qt + np.arange(128))[:, None]
        valid = 64 * jj <= t
        forced = (jj == 0) | (jj == t // 64) | (jj == t // 64 - 1)
        FT[:, qt, :] = np.where(valid, np.where(forced, 1e30, 0.0), -1e30)
    E = (np.arange(L)[None, :] // 64 == np.arange(NSLC)[:, None]).astype(np.float32)
    CAUS = (k <= q).astype(np.float32)
    COVER = np.zeros((128, NCT, NSLC))
    for ct in range(NCT):
        cs = 16 * (128 * ct + np.arange(128))[:, None]
        COVER[:, ct, :] = (cs < 64 * (jj + 1)) & (cs + 32 > 64 * jj)
    SELG = np.zeros((48, 48, 64))
    for r in range(48):
        SELG[r, r, :] = 1.0
    BSC = np.zeros((128, 4, 16)); BSS = np.zeros((128, 64, 16)); BSW = np.zeros((128, 4, 16))
    for h in range(16):
        for ct in range(4):
            BSC[:, ct, h] = np.minimum(slc[h] * (16 * (128 * ct + kk) + 31 - 8192), 0.0)
            BSW[:, ct, h] = slc[h] * (128 * ct + kk - 512)
        for kt in range(64):
            BSS[:, kt, h] = slc[h] * (128 * kt + kk - 8192)
    COVS = np.zeros((128, 4, 128))
    j128 = np.arange(128)[None, :]
    for ct in range(4):
        cs = 16 * (128 * ct + np.arange(128))[:, None]
        COVS[:, ct, :] = (cs < 64 * (j128 + 1)) & (cs + 32 > 64 * j128)
    E2 = np.zeros((128, 64, 128))
    for kt in range(64):
        E2[:, kt, :] = (np.arange(128)[:, None] == (2 * kt + np.arange(128)[None, :] // 64))
    FTS = np.zeros((4, 129)); FTS[:, [0, 127, 128]] = 1e30
    nsa_s = {"c_BSC": BSC, "c_BSSEL": BSS, "c_BSWIN": BSW, "c_COVS": COVS, "c_E2": E2, "c_FTS": FTS}
    nsa = {"c_BK": BK, "c_BKC": BKC, "c_FT": FT, "c_E": E, "c_CAUS": CAUS, "c_COVER": COVER, "c_SELG": SELG}
    nsa.update(nsa_s)
    nsa = {kk_: np.ascontiguousarray(v.reshape(v.shape[0], -1)).astype(np.float32) for kk_, v in nsa.items()}
    tri = (np.arange(64)[:, None] <= np.arange(64)[None, :]).astype(np.float32)
    tri4 = np.ascontiguousarray(np.tile(tri, (1, 4)))
    return {**nsa, "c_TRI4": tri4, "c_RA": f(RA), "c_R0": f(R0), "c_MA": f(MA), "c_CBA": f(CBA), "c_CBS": f(CBS)}


_CACHE = {}


def kernel(**inputs):
    L, NS = 4096, 4
    f32 = lambda a: np.ascontiguousarray(np.asarray(a), dtype=np.float32)
    if "nc" not in _CACHE:
        B = Builder(L=L, NS=NS)
        B.nsa_sample = True
        _CACHE["nc"] = B.build()
        _CACHE["B"] = B
    B, nc = _CACHE["B"], _CACHE["nc"]
    consts = host_consts(L)
    xp = f32(inputs["x_prompt"])
    xs = f32(inputs["x_sample"]).reshape(32, D)
    memp = f32(inputs["mem_prompt"])
    cmem = f32(inputs["cache_mem_kv"]).reshape(DEPTH, 32, MEM, 2 * D)
    cd = {n: f32(inputs[n]).reshape(2, 32, -1, D) for n in ("cache_dil_w128", "cache_dil_w512", "cache_dil_w2048")}
    shared = {n: f32(inputs[n]) for n in ("g_mix", "g_cross", "g_mem", "g_ffn", "g_final", "w_ffn_in", "w_ffn_out",
                                          "w_x_q", "w_x_kv", "w_x_o", "w_a_qkv", "w_a_o", "w_b_in", "w_b_gate2", "b_b_gate",
                                          "g_b_head", "w_b_o", "w_c_in", "b_c_gate", "c_pos_k", "c_pos_v", "w_c_k1", "w_c_k2",
                                          "w_c_v1", "w_c_v2", "w_c_o")}
    shared.update(consts)
    sgla = f32(inputs["state_gla"]).reshape(32, 4, 128, 256)
    pool = f32(inputs["cache_nsa_kv"]).reshape(-1, 1024)
    ptab = np.ascontiguousarray(np.asarray(inputs["page_table"]), dtype=np.int32)
    cwin = f32(inputs["cache_nsa_win"]).reshape(32, 512, 512)
    in_maps = []
    for c in range(NCORES):
        m = dict(shared)
        m["x_prompt"] = xp[c % 4]
        m["x_sample"] = np.ascontiguousarray(xs[4 * c:4 * c + 4])
        m["mem_prompt"] = memp[c % 4]
        m["cache_mem_kv"] = np.ascontiguousarray(cmem[:, 4 * c:4 * c + 4])
        for n in cd:
            m[n] = np.ascontiguousarray(cd[n][:, 4 * c:4 * c + 4])
        m["state_gla"] = np.ascontiguousarray(sgla[4 * c:4 * c + 4])
        m["cache_nsa_kv"] = pool
        m["page_table"] = np.ascontiguousarray(ptab[4 * c:4 * c + 4])
        m["cache_nsa_win"] = np.ascontiguousarray(cwin[4 * c:4 * c + 4])
        in_maps.append({k: m[k] for k in B.ins})
    res = run_bass_kernel_spmd(nc, in_maps, core_ids=list(range(NCORES))).results
    cat = lambda name, cores: np.stack([np.asarray(res[c][name]) for c in cores])
    y_prompt = cat("y_prompt", range(4))
    y_sample = np.concatenate([np.asarray(res[c]["y_sample"]) for c in range(8)]).reshape(32, 1, D)
    outs = [y_prompt, y_sample]
    for W in (128, 512, 2048):
        p = cat("dil%d_prompt" % W, range(4))
        outs.append(np.ascontiguousarray(p.transpose(1, 0, 2, 3)).reshape(2, 4, W, 2, 8, 64))
        s = np.concatenate([np.asarray(res[c]["dil%d_sample" % W]) for c in range(8)], axis=1)
        outs.append(s.reshape(2, 32, 1, 2, 8, 64))
    outs.append(cat("gla_prompt", range(4)).reshape(1, 4, 4, 128, 256))
    outs.append(np.concatenate([np.asarray(res[c]["gla_sample"]) for c in range(8)]).reshape(1, 32, 4, 128, 256))
    outs.append(cat("nsa_win_prompt", range(4)).reshape(1, 4, 512, 2, 4, 64))
    outs.append(np.concatenate([np.asarray(res[c]["nsa_win_sample"]) for c in range(8)]).reshape(1, 32, 1, 2, 4, 64))
    outs.append(cat("nsa_kv_prompt", range(4)).reshape(1, 4, 4096, 4, 4, 64))
    outs.append(np.concatenate([np.asarray(res[c]["nsa_kv_sample"]) for c in range(8)]).reshape(1, 32, 1, 4, 4, 64))
    mem = cat("mem_kv_prompt", range(4))
    outs.append(np.ascontiguousarray(mem.transpose(1, 0, 2, 3)).reshape(DEPTH, 4, MEM, 2, 4, 256))
    return tuple(outs)
```

```python
import numpy as np
from contextlib import ExitStack
import concourse.bass as bass
import concourse.mybir as mybir
from concourse.bass_utils import run_bass_kernel_spmd

F32 = mybir.dt.float32
BF16 = mybir.dt.bfloat16
I32 = mybir.dt.int32
AF = mybir.ActivationFunctionType
ALU = mybir.AluOpType
AX = mybir.AxisListType

ENGS = ("pe", "act", "dve", "pool", "sp")
SEM_EPOCH = 20000
N_DMA_SEMS = 12

D = 1024
DEPTH = 4
FH = 2816
MEM = 256
NCORES = 8


class Prog:
    def __init__(self, nc, stack, same_engine_sync=True):
        self.nc = nc
        self.stack = stack
        self.same_engine_sync = same_engine_sync
        self.streams = {e: [] for e in ENGS}
        self.count = {e: 0 for e in ENGS}
        self.eng_sems = {e: [] for e in ENGS}
        self.dma_sems = []
        self.dma_count = []
        self.dma_rr = 0
        self.clock = {e: {} for e in ENGS}
        self.last_write = {}
        self.readers = {}
        self.n_sem = 0
        self.n_ops = 0
        for _ in range(N_DMA_SEMS):
            self.dma_sems.append(self._sem())
            self.dma_count.append(0)

    def _sem(self):
        self.n_sem += 1
        return self.stack.enter_context(self.nc.semaphore("s%d" % self.n_sem))

    def _deps(self, reads, writes):
        deps = {}

        def add(t):
            tr, v = t
            if deps.get(tr, 0) < v:
                deps[tr] = v
        for k in reads:
            if k in self.last_write:
                add(self.last_write[k])
        for k in writes:
            if k in self.last_write:
                add(self.last_write[k])
            for r in self.readers.get(k, ()):
                add(r)
        return deps

    def _commit(self, token, reads, writes):
        for k in writes:
            self.last_write[k] = token
            self.readers[k] = []
        for k in reads:
            self.readers.setdefault(k, []).append(token)

    def _waits(self, eng, deps):
        waits = []
        clk = self.clock[eng]
        for tr, v in deps.items():
            if tr == eng and (eng == "pe" or not self.same_engine_sync):
                continue
            if clk.get(tr, 0) >= v:
                continue
            clk[tr] = v
            waits.append((tr, v))
        return waits

    def _sem_for(self, tr, v):
        if isinstance(tr, tuple):
            return self.dma_sems[tr[1]], v * 16
        ep = (v - 1) // SEM_EPOCH
        while len(self.eng_sems[tr]) <= ep:
            self.eng_sems[tr].append(self._sem())
        return self.eng_sems[tr][ep], v - ep * SEM_EPOCH

    def op(self, eng, fn, reads=(), writes=(), **kw):
        fn = (fn, kw)
        deps = self._deps(reads, writes)
        waits = self._waits(eng, deps)
        self.count[eng] += 1
        seq = self.count[eng]
        sem, _ = self._sem_for(eng, seq)
        self.streams[eng].append(("op", waits, fn, sem))
        self._commit((eng, seq), reads, writes)
        self.n_ops += 1

    def dma(self, out, in_, reads=(), writes=(), eng="sp", **kw):
        eng = "sp"
        i = self.dma_rr
        self.dma_rr = (self.dma_rr + 1) % N_DMA_SEMS
        tr = ("dma", i)
        deps = self._deps(reads, writes)
        if self.dma_count[i] > 0:
            v = self.dma_count[i]
            if deps.get(tr, 0) < v:
                deps[tr] = v
        waits = self._waits(eng, deps)
        self.dma_count[i] += 1
        self.streams[eng].append(("dma", waits, (out, in_, kw), self.dma_sems[i]))
        self._commit((tr, self.dma_count[i]), reads, writes)
        self.n_ops += 1

    def barrier(self):
        deps = {}
        for i in range(N_DMA_SEMS):
            if self.dma_count[i] > 0:
                deps[("dma", i)] = self.dma_count[i]
        for e in ENGS:
            if self.count[e] > 0:
                deps[e] = self.count[e]
        for eng in ENGS:
            d = {k: v for k, v in deps.items() if k != eng}
            waits = self._waits(eng, d)
            if waits:
                self.streams[eng].append(("wait", waits, None, None))

    def emit(self):
        nc = self.nc
        for e in ENGS:
            for item in self.streams[e]:
                for tr, v in item[1]:
                    self._sem_for(tr, v)
        streams = self.streams
        self.streams = {e: [] for e in ENGS}
        with nc.Block() as block:
            def run(engine, items):
                for kind, waits, fn, sem in items:
                    for tr, v in waits:
                        s, val = self._sem_for(tr, v)
                        engine.wait_ge(s, val)
                    if kind == "op":
                        getattr(engine, fn[0])(**fn[1]).then_inc(sem, 1)
                    elif kind == "dma":
                        out, in_, kw = fn
                        engine.dma_start(out=out, in_=in_, **kw).then_inc(sem, 16)
                    elif kind == "idma":
                        engine.indirect_dma_start(**fn).then_inc(sem, 16)
                    elif kind == "cc":
                        engine.collective_compute(**fn).then_inc(sem, 16)

            @block.tensor
            def _(e):
                run(e, streams["pe"])

            @block.scalar
            def _(e):
                run(e, streams["act"])

            @block.vector
            def _(e):
                run(e, streams["dve"])

            @block.gpsimd
            def _(e):
                run(e, streams["pool"])

            @block.sync
            def _(e):
                run(e, streams["sp"])


class Phase:
    def __init__(self, B, name):
        self.B = B
        self.P = B.P
        self.nc = B.nc
        self.name = name
        self.stack = ExitStack()
        self.n = 0

    def __enter__(self):
        self.stack.__enter__()
        return self

    def __exit__(self, *a):
        self.P.barrier()
        self.P.emit()
        return self.stack.__exit__(*a)

    def sbuf(self, name, shape, dtype):
        self.n += 1
        nm = "%s_%s" % (self.name, name)
        t = self.stack.enter_context(self.nc.sbuf_tensor(nm, list(shape), dtype))
        return t

    def psum(self, name, shape, dtype=F32):
        nm = "%s_%s" % (self.name, name)
        return self.stack.enter_context(self.nc.psum_tensor(nm, list(shape), dtype))


class Rot:
    def __init__(self, tiles, name):
        self.tiles = tiles
        self.name = name
        self.i = 0

    def next(self):
        t = self.tiles[self.i % len(self.tiles)]
        k = "%s#%d" % (self.name, self.i % len(self.tiles))
        self.i += 1
        return t, k


class Builder:
    def __init__(self, L=4096, NS=4, layers=(0, 1, 2, 3), dbg=False, parts=("mix", "cross", "ffn")):
        self.L = L
        self.NS = NS
        self.layers = layers
        self.dbg = dbg
        self.parts = parts
        self.nc = bass.Bass("TRN2", target_bir_lowering=False)
        self.ins = {}
        self.outs = {}
        self._rr = 0
        self.dbg_skip = set()
        self.nsa_sample = False
        self.n_pool = 2560

    def inp(self, name, shape, dtype=F32):
        t = self.nc.dram_tensor(name, list(shape), dtype, kind="ExternalInput").ap()
        self.ins[name] = t
        return t

    def out(self, name, shape, dtype=F32):
        t = self.nc.dram_tensor(name, list(shape), dtype, kind="ExternalOutput").ap()
        self.outs[name] = t
        return t

    def scratch(self, name, shape, dtype=F32):
        if self.dbg and dtype == F32:
            return self.out("dbg_" + name, shape, dtype)
        return self.nc.dram_tensor(name, list(shape), dtype, kind="Internal").ap()

    def phase(self, name):
        return Phase(self, name)

    def setup_consts(self, st):
        nc, P = self.nc, self.P
        self.identf = st.enter_context(nc.sbuf_tensor("identf", [128, 128], F32))
        self.ident = st.enter_context(nc.sbuf_tensor("ident", [128, 128], BF16))
        self.ones = st.enter_context(nc.sbuf_tensor("onesb", [128, 128], BF16))
        self.epsb = st.enter_context(nc.sbuf_tensor("epsb", [128, 1], F32))
        self.A_qks = st.enter_context(nc.sbuf_tensor("A_qks", [128, 24, self.NS], BF16))
        P.op("pool", "memset", writes=["identf"], ap=self.identf[:], constant=0.0)
        P.op("pool", "affine_select", reads=["identf"], writes=["identf"], out=self.identf[:], in_=self.identf[:],
             pattern=[[-1, 128]], compare_op=ALU.not_equal, fill=1.0, base=0, channel_multiplier=1)
        P.op("dve", "tensor_copy", reads=["identf"], writes=["ident"], out=self.ident[:], in_=self.identf[:])
        P.op("pool", "memset", writes=["epsb"], ap=self.epsb[:], constant=1e-6)
        P.op("pool", "memset", writes=["ones"], ap=self.ones[:], constant=1.0)

    def cast(self, dst, src, reads, writes, eng=None):
        P = self.P
        if eng is None:
            eng = ("act", "dve", "pool")[self._rr % 3]
            self._rr += 1
        if eng == "act":
            P.op("act", "activation", reads=reads, writes=writes, out=dst, in_=src, func=AF.Copy)
        else:
            P.op(eng, "tensor_copy", reads=reads, writes=writes, out=dst, in_=src)

    def load_weight(self, ph, name, w, K, stg, col0=0, ncols=None, CH=512, KCH=8):
        P = self.P
        KC = (K + 127) // 128
        wb = ph.sbuf(name, [128, KC, ncols], BF16)
        key = ph.name + "_" + name
        if K % 128 == 0:
            wv = w.rearrange("(k p) n -> p k n", p=128)
            for k0 in range(0, KC, KCH):
                kn = min(KCH, KC - k0)
                for n0 in range(0, ncols, CH):
                    cw = min(CH, ncols - n0)
                    s, sk = stg.next()
                    P.dma(s[:, :kn, :cw], wv[:, k0:k0 + kn, col0 + n0:col0 + n0 + cw], writes=[sk])
                    self.cast(wb[:, k0:k0 + kn, n0:n0 + cw], s[:, :kn, :cw], [sk], [key])
        else:
            assert K < 128
            for n0 in range(0, ncols, CH):
                cw = min(CH, ncols - n0)
                s, sk = stg.next()
                P.dma(s[:K, 0, :cw], w[:, col0 + n0:col0 + n0 + cw], writes=[sk])
                self.cast(wb[:K, 0, n0:n0 + cw], s[:K, 0, :cw], [sk], [key])
        return wb, key

    def load_gain(self, ph, name, g_row):
        t = ph.sbuf(name, [128, D], F32)
        key = ph.name + "_" + name
        self.P.dma(t[:], g_row.partition_broadcast(128), writes=[key])
        return t, key

    def make_norm_bufs(self, ph, nbuf=2):
        nb = {}
        nb["sq"] = Rot([ph.sbuf("nsq%d" % i, [128, D], F32) for i in range(1)], ph.name + "nsq")
        nb["ss"] = Rot([ph.sbuf("nss%d" % i, [128, 1], F32) for i in range(nbuf)], ph.name + "nss")
        nb["rs"] = Rot([ph.sbuf("nrs%d" % i, [128, 1], F32) for i in range(nbuf)], ph.name + "nrs")
        nb["hb"] = Rot([ph.sbuf("nhb%d" % i, [128, D], BF16) for i in range(nbuf)], ph.name + "nhb")
        nb["pT"] = Rot([ph.psum("npT%d" % i, [128, 8, 128], BF16) for i in range(1)], ph.name + "npT")
        return nb

    def rstd(self, nb, x_ap, nt, xkey, width=D):
        P = self.P
        sq, sqk = nb["sq"].next()
        ss, ssk = nb["ss"].next()
        rs, rsk = nb["rs"].next()
        P.op("act", "activation", reads=[xkey], writes=[sqk, ssk], out=sq[:nt, :width], in_=x_ap, func=AF.Square, accum_out=ss[:nt])
        P.op("act", "activation", reads=[ssk, "epsb"], writes=[rsk], out=rs[:nt], in_=ss[:nt], func=AF.Sqrt,
             scale=1.0 / width, bias=self.epsb[:nt])
        P.op("dve", "reciprocal", reads=[rsk], writes=[rsk], out=rs[:nt], in_=rs[:nt])
        return rs, rsk

    def norm_T(self, nb, x_ap, nt, xkey, g, gkey, hT_ap, hTkey):
        P = self.P
        rs, rsk = self.rstd(nb, x_ap, nt, xkey)
        hb, hbk = nb["hb"].next()
        P.op("dve", "scalar_tensor_tensor", reads=[xkey, rsk, gkey], writes=[hbk], out=hb[:nt], in0=x_ap,
             scalar=rs[:nt, 0:1], in1=g[:nt], op0=ALU.mult, op1=ALU.mult)
        self.transpose_to(nb, hb[:nt], hbk, 8, nt, hT_ap, hTkey)

    def transpose_to(self, nb, src_ap, srckey, nchunks, nt, dst_ap, dstkey, eng="act"):
        P = self.P
        for c0 in range(0, nchunks, 8):
            cn = min(8, nchunks - c0)
            pT, pTk = nb["pT"].next()
            for k in range(cn):
                P.op("pe", "transpose", reads=[srckey, "ident"], writes=[pTk], out=pT[:, k, :nt],
                     in_=src_ap[:, (c0 + k) * 128:(c0 + k + 1) * 128], identity=self.ident[:nt, :nt])
            self.cast(dst_ap[:, c0:c0 + cn, :], pT[:, :cn, :nt], [pTk], [dstkey], eng=eng)

    def row_groups(self, TG):
        groups = []
        for r0 in range(0, self.L, TG):
            groups.append(("p", r0, min(TG, self.L - r0)))
        groups.append(("s", 0, self.NS))
        return groups

    def xrows(self, kind, r0, n):
        if kind == "p":
            return self.X[r0:r0 + n, :]
        return self.XS[r0:r0 + n, :]

    def xkeys(self, kind, r0, n):
        if kind == "p":
            return ["X_%d" % t for t in range(r0 // 128, (r0 + n + 127) // 128)]
        return ["XS"]

    def load_x(self, xt, xtk, kind, r0, n):
        P = self.P
        src = self.xrows(kind, r0, n)
        keys = self.xkeys(kind, r0, n)
        if n >= 128:
            assert n % 128 == 0
            P.dma(xt[:, :n // 128, :], src.rearrange("(t p) d -> p t d", p=128), reads=keys, writes=[xtk])
            return [(t, 128) for t in range(n // 128)]
        P.dma(xt[:n, 0, :], src, reads=keys, writes=[xtk])
        return [(0, n)]

    def store_x(self, xt, xtk, kind, r0, n):
        P = self.P
        dst = self.xrows(kind, r0, n)
        keys = self.xkeys(kind, r0, n)
        if n >= 128:
            P.dma(dst.rearrange("(t p) d -> p t d", p=128), xt[:, :n // 128, :], reads=[xtk], writes=keys, eng="pool")
        else:
            P.dma(dst, xt[:n, 0, :], reads=[xtk], writes=keys, eng="pool")

    def out_proj_add(self, aT, aTk, KC, wo, wok, xt, xtk, tiles, pos):
        P = self.P
        for t, nt in tiles:
            for nh in range(2):
                po, pok = pos.next()
                for k in range(KC):
                    P.op("pe", "matmul", reads=[aTk, wok], writes=[pok], out=po[:nt, :], lhsT=aT[:, k, t * 128:t * 128 + nt],
                         rhs=wo[:, k, nh * 512:(nh + 1) * 512], start=(k == 0), stop=(k == KC - 1))
                P.op("dve", "tensor_tensor", reads=[pok, xtk], writes=[xtk], out=xt[:nt, t, nh * 512:(nh + 1) * 512],
                     in0=xt[:nt, t, nh * 512:(nh + 1) * 512], in1=po[:nt, :], op=ALU.add)

    def phase_init(self, x_prompt, x_sample):
        P = self.P
        for r0 in range(0, self.L, 512):
            n = min(512, self.L - r0)
            P.dma(self.X[r0:r0 + n, :], x_prompt[r0:r0 + n, :], writes=self.xkeys("p", r0, n))
        P.dma(self.XS[:, :], x_sample, writes=["XS"])

    def phase_ffn(self, l):
        P = self.P
        TG = 256
        with self.phase("ffn%d" % l) as ph:
            stg = Rot([ph.sbuf("stg%d" % i, [128, 8, 256], F32) for i in range(2)], ph.name + "stg")
            win, wink = self.load_weight(ph, "win", self.w["w_ffn_in"][l], D, stg, ncols=2 * FH, CH=256)
            wout, woutk = self.load_weight(ph, "wout", self.w["w_ffn_out"][l], FH, stg, ncols=D, CH=256)
            g, gk = self.load_gain(ph, "g", self.w["g_ffn"][l])
            nb = self.make_norm_bufs(ph)
            xts = Rot([ph.sbuf("xt%d" % i, [128, TG // 128, D], F32) for i in range(2)], ph.name + "xt")
            hTs = Rot([ph.sbuf("hT%d" % i, [128, 8, TG], BF16) for i in range(1)], ph.name + "hT")
            gTs = Rot([ph.sbuf("gT%d" % i, [128, 22, TG], BF16) for i in range(1)], ph.name + "gT")
            sas = Rot([ph.sbuf("sa%d" % i, [128, TG], F32) for i in range(2)], ph.name + "sa")
            pas = Rot([ph.psum("pa%d" % i, [128, 512], F32) for i in range(2)], ph.name + "pa")
            pbs = Rot([ph.psum("pb%d" % i, [128, 512], F32) for i in range(2)], ph.name + "pb")
            pos = Rot([ph.psum("po%d" % i, [128, 512], F32) for i in range(2)], ph.name + "po")
            for kind, r0, n in self.row_groups(TG):
                xt, xtk = xts.next()
                tiles = self.load_x(xt, xtk, kind, r0, n)
                hT, hTk = hTs.next()
                for t, nt in tiles:
                    self.norm_T(nb, xt[:nt, t, :], nt, xtk, g, gk, hT[:, :, t * 128:t * 128 + nt], hTk)
                gT, gTk = gTs.next()
                for hc in range(22):
                    pa, pak = pas.next()
                    pb, pbk = pbs.next()
                    for k in range(8):
                        P.op("pe", "matmul", reads=[wink, hTk], writes=[pak], out=pa[:, :n], lhsT=win[:, k, hc * 128:(hc + 1) * 128],
                             rhs=hT[:, k, :n], start=(k == 0), stop=(k == 7))
                    for k in range(8):
                        P.op("pe", "matmul", reads=[wink, hTk], writes=[pbk], out=pb[:, :n],
                             lhsT=win[:, k, FH + hc * 128:FH + (hc + 1) * 128], rhs=hT[:, k, :n], start=(k == 0), stop=(k == 7))
                    sa, sak = sas.next()
                    P.op("act", "activation", reads=[pak], writes=[sak], out=sa[:, :n], in_=pa[:, :n], func=AF.Silu)
                    P.op("dve", "tensor_tensor", reads=[sak, pbk], writes=[gTk], out=gT[:, hc, :n], in0=sa[:, :n], in1=pb[:, :n], op=ALU.mult)
                self.out_proj_add(gT, gTk, 22, wout, woutk, xt, xtk, tiles, pos)
                self.store_x(xt, xtk, kind, r0, n)

    def phase_final(self, y_prompt, y_sample):
        P = self.P
        with self.phase("fin") as ph:
            g, gk = self.load_gain(ph, "g", self.w["g_final"])
            nb = self.make_norm_bufs(ph)
            xts = Rot([ph.sbuf("xt%d" % i, [128, 4, D], F32) for i in range(2)], ph.name + "xt")
            yts = Rot([ph.sbuf("yt%d" % i, [128, 4, D], F32) for i in range(2)], ph.name + "yt")
            for kind, r0, n in self.row_groups(512):
                xt, xtk = xts.next()
                yt, ytk = yts.next()
                tiles = self.load_x(xt, xtk, kind, r0, n)
                for t, nt in tiles:
                    rs, rsk = self.rstd(nb, xt[:nt, t, :], nt, xtk)
                    P.op("dve", "scalar_tensor_tensor", reads=[xtk, rsk, gk], writes=[ytk], out=yt[:nt, t, :], in0=xt[:nt, t, :],
                         scalar=rs[:nt, 0:1], in1=g[:nt], op0=ALU.mult, op1=ALU.mult)
                if kind == "p":
                    P.dma(y_prompt[r0:r0 + n, :].rearrange("(t p) d -> p t d", p=128), yt[:, :n // 128, :], reads=[ytk],
                          writes=["y%d" % r0], eng="pool")
                else:
                    P.dma(y_sample, yt[:n, 0, :], reads=[ytk], writes=["ys"], eng="pool")

    def attn_units(self, units, PTs, pss, LA=2):
        P = self.P
        staged = {}
        n = len(units)
        for i in range(n + LA):
            if i < n:
                u = units[i]
                ps, psk = pss.next()
                nq = u["nq"]
                nc_ = len(u["k"])
                nk = None
                for c in range(nc_):
                    ka, kk = u["k"][c]
                    qa, qk = u["q"][c]
                    nk = ka.shape[-1]
                    P.op("pe", "matmul", reads=[kk, qk], writes=[psk], out=ps[:nk, :nq], lhsT=ka, rhs=qa, start=(c == 0), stop=(c == nc_ - 1))
                PT, PTk = PTs.next()
                P.op("act", "activation", reads=[psk], writes=[PTk], out=PT[:nk, :nq], in_=ps[:nk, :nq], func=AF.Exp)
                staged[i] = (PT, PTk, nk)
            jb = i - LA
            if jb >= 0:
                u = units[jb]
                PT, PTk, nk = staged.pop(jb)
                nq = u["nq"]
                for va, vk, acc, acck in u["v"]:
                    P.op("pe", "matmul", reads=[vk, PTk], writes=[acck], out=acc, lhsT=va, rhs=PT[:nk, :nq], start=u["first"], stop=u["last"])
                if u["last"] and u.get("fin") is not None:
                    u["fin"]()

    def phase_cross(self, l):
        P = self.P
        NS = self.NS
        TG = 512
        with self.phase("cr%d" % l) as ph:
            stg = Rot([ph.sbuf("stg%d" % i, [128, 8, 256], F32) for i in range(2)], ph.name + "stg")
            wq, wqk = self.load_weight(ph, "wq", self.w["w_x_q"][l], D, stg, ncols=D, CH=256)
            wo, wok = self.load_weight(ph, "wo", self.w["w_x_o"][l], D, stg, ncols=D, CH=256)
            wkv, wkvk = self.load_weight(ph, "wkv", self.w["w_x_kv"][l], D, stg, ncols=2 * D, CH=256)
            g, gk = self.load_gain(ph, "g", self.w["g_cross"][l])
            gm, gmk = self.load_gain(ph, "gm", self.w["g_mem"][l])
            nb = self.make_norm_bufs(ph)
            kTs = [ph.sbuf("kT%d" % i, [128, 8, MEM], BF16) for i in range(NS + 1)]
            Vs = [ph.sbuf("V%d" % i, [128, 2, D], BF16) for i in range(NS + 1)]
            pss = Rot([ph.psum("ps%d" % i, [128, 512], F32) for i in range(2)], ph.name + "ps")
            pos = Rot([ph.psum("po%d" % i, [128, 512], F32) for i in range(2)], ph.name + "po")
            paccs = [ph.psum("pacc%d" % i, [128, 512], F32) for i in range(3)]
            kvf = ph.sbuf("kvf", [128, 2, 2 * D], F32)
            kvfk = ph.name + "kvf"
            mfull = kvf[:, :, D:2 * D]
            mfk = kvfk
            P.dma(mfull, self.mem_prompt.rearrange("(t p) d -> p t d", p=128), writes=[mfk])
            hmT = ph.sbuf("hmT", [128, 8, MEM], BF16)
            hmTk = ph.name + "hmT"
            for t in range(2):
                self.norm_T(nb, kvf[:, t, D:2 * D], 128, mfk, gm, gmk, hmT[:, :, t * 128:(t + 1) * 128], hmTk)
            for t in range(2):
                for nbk in range(4):
                    po, pok = pos.next()
                    for k in range(8):
                        P.op("pe", "matmul", reads=[hmTk, wkvk], writes=[pok], out=po[:, :], lhsT=hmT[:, k, t * 128:(t + 1) * 128],
                             rhs=wkv[:, k, nbk * 512:(nbk + 1) * 512], start=(k == 0), stop=(k == 7))
                    self.cast(kvf[:, t, nbk * 512:(nbk + 1) * 512], po[:, :], [pok], [kvfk], eng=("act", "dve")[nbk % 2])
            P.dma(self.o_mem[l].rearrange("(t p) d -> p t d", p=128), kvf[:], reads=[kvfk], writes=["omem%d" % l], eng="pool")
            self.cast(Vs[0][:, :, :], kvf[:, :, D:2 * D], [kvfk], [ph.name + "V0"])
            for hc in range(8):
                po, pok = pos.next()
                for k in range(8):
                    P.op("pe", "matmul", reads=[hmTk, wkvk], writes=[pok], out=po[:, :MEM], lhsT=wkv[:, k, hc * 128:(hc + 1) * 128],
                         rhs=hmT[:, k, :], start=(k == 0), stop=(k == 7))
                self.cast(kTs[0][:, hc, :], po[:, :MEM], [pok], [ph.name + "kT0"], eng=("act", "dve")[hc % 2])
            kvb = ph.sbuf("kvb", [128, 2, D], BF16)
            kvbk = ph.name + "kvb"
            for s in range(NS):
                P.dma(kvf[:], self.cache_mem[l, s].rearrange("(t p) d -> p t d", p=128), reads=[], writes=[kvfk])
                self.cast(kvb[:], kvf[:, :, 0:D], [kvfk], [kvbk])
                self.cast(Vs[s + 1][:, :, :], kvf[:, :, D:2 * D], [kvfk], [ph.name + "V%d" % (s + 1)])
                for t in range(2):
                    self.transpose_to(nb, kvb[:, t, 0:D], kvbk, 8, 128, kTs[s + 1][:, :, t * 128:(t + 1) * 128], ph.name + "kT%d" % (s + 1))
            xts = Rot([ph.sbuf("xt%d" % i, [128, TG // 128, D], F32) for i in range(1)], ph.name + "xt")
            hT = ph.sbuf("hT", [128, 8, TG], BF16)
            hTk = ph.name + "hT"
            qT = ph.sbuf("qT", [128, 8, TG], BF16)
            qTk = ph.name + "qT"
            oT = ph.sbuf("oT", [128, 8, TG], BF16)
            oTk = ph.name + "oT"
            PTs = Rot([ph.sbuf("PT%d" % i, [128, TG], BF16) for i in range(4)], ph.name + "PT")
            rden = ph.sbuf("rden", [128, TG], F32)
            rdenk = ph.name + "rden"
            for kind, r0, n in self.row_groups(TG):
                xt, xtk = xts.next()
                tiles = self.load_x(xt, xtk, kind, r0, n)
                for t, nt in tiles:
                    self.norm_T(nb, xt[:nt, t, :], nt, xtk, g, gk, hT[:, :, t * 128:t * 128 + nt], hTk)
                for hc in range(8):
                    po, pok = pos.next()
                    for k in range(8):
                        P.op("pe", "matmul", reads=[hTk, wqk], writes=[pok], out=po[:, :n], lhsT=wq[:, k, hc * 128:(hc + 1) * 128],
                             rhs=hT[:, k, :n], start=(k == 0), stop=(k == 7))
                    P.op("act", "activation", reads=[pok], writes=[qTk], out=qT[:, hc, :n], in_=po[:, :n], func=AF.Copy, scale=1.0 / 16)
                seqs = [(0, 0, n)] if kind == "p" else [(s + 1, s, 1) for s in range(n)]
                units = []
                for mi, q0, nq in seqs:
                    kTm, Vm = kTs[mi], Vs[mi]
                    kTmk, Vmk = ph.name + "kT%d" % mi, ph.name + "V%d" % mi
                    for h in range(4):
                        accs = [(paccs[j][:, :nq], ph.name + "pacc%d" % j) for j in range(3)]

                        def fin(h=h, q0=q0, nq=nq, accs=accs):
                            P.op("dve", "reciprocal", reads=[accs[2][1]], writes=[rdenk], out=rden[:, :nq], in_=accs[2][0])
                            for j in range(2):
                                P.op("dve", "tensor_tensor", reads=[accs[j][1], rdenk], writes=[oTk], out=oT[:, h * 2 + j, q0:q0 + nq],
                                     in0=accs[j][0], in1=rden[:, :nq], op=ALU.mult)
                        for t in range(2):
                            units.append(dict(
                                k=[(kTm[:, h * 2 + c, t * 128:(t + 1) * 128], kTmk) for c in range(2)],
                                q=[(qT[:, h * 2 + c, q0:q0 + nq], qTk) for c in range(2)], nq=nq,
                                v=[(Vm[:, t, h * 256 + j * 128:h * 256 + (j + 1) * 128], Vmk, accs[j][0], accs[j][1]) for j in range(2)]
                                  + [(self.ones[:, :], "ones", accs[2][0], accs[2][1])],
                                first=(t == 0), last=(t == 1), fin=fin))
                self.attn_units(units, PTs, pss, LA=2)
                self.out_proj_add(oT, oTk, 8, wo, wok, xt, xtk, tiles, pos)
                self.store_x(xt, xtk, kind, r0, n)

    def phase_mixA(self, l, j):
        P = self.P
        L, NS = self.L, self.NS
        TG = 512
        QT_d, KT_d, V_d, VS_d = self.QT_d, self.KT_d, self.V_d, self.VS_d
        WIN = (128, 512, 2048)
        DIL = (1, 4, 16)
        outs_p = (self.o_dil128_p, self.o_dil512_p, self.o_dil2048_p)
        outs_s = (self.o_dil128_s, self.o_dil512_s, self.o_dil2048_s)
        with self.phase("a1_%d" % l) as ph:
            stg = Rot([ph.sbuf("stg%d" % i, [128, 8, 512], F32) for i in range(2)], ph.name + "stg")
            wqkv, wk = self.load_weight(ph, "wqkv", self.w["w_a_qkv"][j], D, stg, ncols=4608)
            g, gk = self.load_gain(ph, "g", self.w["g_mix"][l])
            nb = self.make_norm_bufs(ph)
            xts = Rot([ph.sbuf("xt%d" % i, [128, TG // 128, D], F32) for i in range(1)], ph.name + "xt")
            hT = ph.sbuf("hT", [128, 8, TG], BF16)
            hTk = ph.name + "hT"
            qks = Rot([ph.sbuf("qk%d" % i, [128, 24, TG], BF16) for i in range(1)], ph.name + "qk")
            vbs = Rot([ph.sbuf("vb%d" % i, [128, 1536], BF16) for i in range(2)], ph.name + "vb")
            sfs = Rot([ph.sbuf("sf%d" % i, [128, 512], F32) for i in range(3)], ph.name + "sf")
            pos = Rot([ph.psum("po%d" % i, [128, 512], F32) for i in range(4)], ph.name + "po")
            for kind, r0, n in self.row_groups(TG):
                if kind in self.dbg_skip:
                    continue
                xt, xtk = xts.next()
                tiles = self.load_x(xt, xtk, kind, r0, n)
                for t, nt in tiles:
                    self.norm_T(nb, xt[:nt, t, :], nt, xtk, g, gk, hT[:, :, t * 128:t * 128 + nt], hTk)
                qk, qkk = qks.next()
                for c in range(24):
                    po, pok = pos.next()
                    for k in range(8):
                        P.op("pe", "matmul", reads=[hTk, wk], writes=[pok], out=po[:, :n], lhsT=wqkv[:, k, c * 128:(c + 1) * 128],
                             rhs=hT[:, k, :n], start=(k == 0), stop=(k == 7))
                    if c < 12:
                        P.op("act", "activation", reads=[pok], writes=[qkk], out=qk[:, c, :n], in_=po[:, :n], func=AF.Copy, scale=0.125)
                    else:
                        P.op("dve", "tensor_copy", reads=[pok], writes=[qkk], out=qk[:, c, :n], in_=po[:, :n])
                if "qk" in self.dbg_skip:
                    pass
                elif kind == "p":
                    P.dma(QT_d[:, r0:r0 + n].rearrange("(c p) t -> p c t", p=128), qk[:, 0:12, :n], reads=[qkk], writes=["QT_d%d" % r0], eng="sp")
                    P.dma(KT_d[:, r0:r0 + n].rearrange("(c p) t -> p c t", p=128), qk[:, 12:24, :n], reads=[qkk], writes=["KT_d%d" % r0], eng="sp")
                else:
                    P.op("pool", "tensor_copy", reads=[qkk], writes=["A_qks"], out=self.A_qks[:, :, :n], in_=qk[:, :, :n])
                for t, nt in tiles:
                    if "v" in self.dbg_skip:
                        break
                    row0 = r0 + t * 128
                    vb, vbk = vbs.next()
                    for gI in range(3):
                        Wc = min(WIN[gI], L)
                        need_state = (kind == "s") or (row0 + nt > L - Wc)
                        po, pok = pos.next()
                        for k in range(8):
                            P.op("pe", "matmul", reads=[hTk, wk], writes=[pok], out=po[:nt, :], lhsT=hT[:, k, t * 128:t * 128 + nt],
                                 rhs=wqkv[:, k, 3072 + gI * 512:3072 + (gI + 1) * 512], start=(k == 0), stop=(k == 7))
                        if not need_state:
                            P.op("act", "activation", reads=[pok], writes=[vbk], out=vb[:nt, gI * 512:(gI + 1) * 512], in_=po[:nt, :], func=AF.Copy)
                        if need_state:
                            sf, sfk = sfs.next()
                            P.op("dve", "tensor_copy", reads=[pok], writes=[sfk], out=sf[:nt, :], in_=po[:nt, :])
                            P.op("act", "activation", reads=[sfk], writes=[vbk], out=vb[:nt, gI * 512:(gI + 1) * 512], in_=sf[:nt, :], func=AF.Copy)
                            if kind == "p":
                                o0 = row0 - (L - Wc)
                                P.dma(outs_p[gI][j, o0:o0 + nt, 512:1024], sf[:nt, :], reads=[sfk], writes=["odil"], eng="pool")
                            else:
                                P.dma(outs_s[gI][j, :, 512:1024], sf[:nt, :], reads=[sfk], writes=["odil"], eng="pool")
                            po, pok = pos.next()
                            for k in range(8):
                                P.op("pe", "matmul", reads=[hTk, wk], writes=[pok], out=po[:nt, :], lhsT=hT[:, k, t * 128:t * 128 + nt],
                                     rhs=wqkv[:, k, 1536 + gI * 512:1536 + (gI + 1) * 512], start=(k == 0), stop=(k == 7))
                            sf, sfk = sfs.next()
                            P.op("dve", "tensor_copy", reads=[pok], writes=[sfk], out=sf[:nt, :], in_=po[:nt, :])
                            if kind == "p":
                                P.dma(outs_p[gI][j, o0:o0 + nt, 0:512], sf[:nt, :], reads=[sfk], writes=["odil"], eng="pool")
                            else:
                                P.dma(outs_s[gI][j, :, 0:512], sf[:nt, :], reads=[sfk], writes=["odil"], eng="pool")
                    if kind == "p":
                        P.dma(V_d[row0:row0 + nt, :], vb[:nt, :], reads=[vbk], writes=["V_d%d" % row0], eng="sp")
                    else:
                        P.dma(VS_d[:, :], vb[:nt, :], reads=[vbk], writes=["VS_d"], eng="sp")
        NR = 17
        LA = 3
        if getattr(self, "skip_a2", False):
            return
        gidx0 = (0, 2, 7)
        with self.phase("a2_%d" % l) as ph:
            stg = Rot([ph.sbuf("stg%d" % i, [128, 4, 512], F32) for i in range(2)], ph.name + "stg")
            wo, wok = self.load_weight(ph, "wo", self.w["w_a_o"][j], 512, stg, ncols=D, KCH=4)
            RA = ph.sbuf("RA", [128, 24, 128], F32)
            P.dma(RA[:].rearrange("p a q -> p (a q)"), self.c_RA, writes=["RA"])
            CBA = ph.sbuf("CBA", [128, 24 * 17], F32)
            P.dma(CBA[:], self.c_CBA, writes=["CBA"])
            MA = ph.sbuf("MA", [128, 24, 128], BF16)
            for i in range(2):
                sg, sk = stg.next()
                P.dma(sg[:, 0:3, :].rearrange("p a q -> p (a q)"), self.c_MA[:, i * 1536:(i + 1) * 1536], writes=[sk])
                self.cast(MA[:, i * 12:(i + 1) * 12, :].rearrange("p a q -> p (a q)"), sg[:, 0:3, :].rearrange("p a q -> p (a q)"), [sk], ["MA"])
            R0 = ph.sbuf("R0", [128, 24, 128], F32)
            P.dma(R0[:].rearrange("p a q -> p (a q)"), self.c_R0, writes=["R0"])
            KR = ph.sbuf("KR", [128, 12, NR, 128], BF16)
            VR = ph.sbuf("VR", [128, NR, 1536], BF16)
            QTs = Rot([ph.sbuf("QT%d" % i, [128, 12, 128], BF16) for i in range(2)], ph.name + "QT")
            xts = Rot([ph.sbuf("xt%d" % i, [128, 1, D], F32) for i in range(2)], ph.name + "xt")
            t1s = Rot([ph.sbuf("t1%d" % i, [128, 4, 128], F32) for i in range(4)], ph.name + "t1")
            PTs = Rot([ph.sbuf("PT%d" % i, [128, 4, 128], BF16) for i in range(LA + 2)], ph.name + "PT")
            oTs = Rot([ph.sbuf("oT%d" % i, [128, 4, 128], BF16) for i in range(2)], ph.name + "oT")
            rden = ph.sbuf("rden", [128, 128], F32)
            rdenk = ph.name + "rden"
            pss = Rot([ph.psum("ps%d" % i, [128, 4, 128], F32) for i in range(4)], ph.name + "ps")
            paccs = Rot([ph.psum("pacc%d" % i, [128, 512], F32) for i in range(2)], ph.name + "pacc")
            pdens = Rot([ph.psum("pden%d" % i, [128, 512], F32) for i in range(1)], ph.name + "pden")
            pos = Rot([ph.psum("po%d" % i, [128, 512], F32) for i in range(1)], ph.name + "po")
            for qt in range(L // 128):
                slot = qt % NR
                P.dma(KR[:, :, slot, :], KT_d[:, qt * 128:(qt + 1) * 128].rearrange("(c p) t -> p c t", p=128),
                      reads=["KT_d%d" % (qt * 128 // TG * TG)], writes=["KR%d" % slot])
                P.dma(VR[:, slot, :], V_d[qt * 128:(qt + 1) * 128, :], reads=["V_d%d" % (qt * 128)], writes=["VR%d" % slot])
                QT, QTk = QTs.next()
                P.dma(QT[:], QT_d[:, qt * 128:(qt + 1) * 128].rearrange("(c p) t -> p c t", p=128),
                      reads=["QT_d%d" % (qt * 128 // TG * TG)], writes=[QTk])
                xt, xtk = xts.next()
                self.load_x(xt, xtk, "p", qt * 128, 128)
                oT, oTk = oTs.next()
                batches = []
                for s in range(8):
                    bl = []
                    for gI in range(3):
                        dmax = min(qt, WIN[gI] // 128)
                        for d0 in range(0, dmax + 1, 4):
                            bl.append((gI, d0, min(4, dmax + 1 - d0)))
                    for bi, (gI, d0, nbt) in enumerate(bl):
                        batches.append((s, gI, d0, nbt, bi == 0, bi == len(bl) - 1))
                staged = {}
                accs = {}
                for i in range(len(batches) + LA):
                    if i < len(batches):
                        s, gI, d0, nbt, first, last = batches[i]
                        half = slice(0, 64) if s % 2 == 0 else slice(64, 128)
                        h = gI * 8 + s
                        c = gI * 4 + s // 2
                        ps, psk = pss.next()
                        for u in range(nbt):
                            ks = (qt - d0 - u) % NR
                            P.op("pe", "matmul", reads=["KR%d" % ks, QTk], writes=[psk], out=ps[:, u, :], lhsT=KR[half, c, ks, :],
                                 rhs=QT[half, c, :], start=True, stop=True)
                        t1, t1k = t1s.next()
                        u0 = 0
                        if d0 == 0:
                            P.op("dve", "tensor_tensor", reads=[psk, "R0"], writes=[t1k], out=t1[:, 0, :], in0=ps[:, 0, :], in1=R0[:, h, :], op=ALU.add)
                            u0 = 1
                        if nbt > u0:
                            P.op("dve", "tensor_tensor", reads=[psk, "RA"], writes=[t1k], out=t1[:, u0:nbt, :], in0=ps[:, u0:nbt, :],
                                 in1=RA[:, h:h + 1, :].to_broadcast([128, nbt - u0, 128]), op=ALU.add)
                        PT, PTk = PTs.next()
                        pks = ["%s_%d" % (PTk, u) for u in range(nbt)]
                        for u in range(nbt):
                            dl = d0 + u
                            P.op("act", "activation", reads=[t1k, "CBA"], writes=[pks[u]], out=PT[:, u, :], in_=t1[:, u, :], func=AF.Exp,
                                 bias=CBA[:, h * 17 + dl:h * 17 + dl + 1])
                        m0 = gidx0[gI] + d0
                        P.op("pool", "tensor_tensor", reads=pks + ["MA"], writes=pks, out=PT[:, 0:nbt, :], in0=PT[:, 0:nbt, :],
                             in1=MA[:, m0:m0 + nbt, :], op=ALU.mult)
                        pall = ["%s_%d" % (PTk, u) for u in range(4)]
                        if nbt < 4:
                            P.op("pool", "memset", writes=pall[nbt:], ap=PT[:, nbt:4, :], constant=0.0)
                        staged[i] = (PT, pks, pall)
                    jb = i - LA
                    if jb >= 0:
                        s, gI, d0, nbt, first, last = batches[jb]
                        PT, pks, pall = staged.pop(jb)
                        if first:
                            accs[s] = (paccs.next(), pdens.next())
                        (acc, acck), (den, denk) = accs[s]
                        for u in range(nbt):
                            ks = (qt - d0 - u) % NR
                            fl = dict(start=(first and u == 0), stop=(last and u == nbt - 1))
                            P.op("pe", "matmul", reads=["VR%d" % ks, pks[u]], writes=[acck], out=acc[:, 0:128],
                                 lhsT=VR[:, ks, gI * 512 + (s // 2) * 128:gI * 512 + (s // 2 + 1) * 128], rhs=PT[:, u, :], **fl)
                        P.op("pe", "matmul", reads=["ones"] + pall, writes=[denk], out=den[:, :], lhsT=self.ones[:, :],
                             rhs=PT[:, :, :].rearrange("p u q -> p (u q)"), start=first, stop=last)
                        if last:
                            half = slice(0, 64) if s % 2 == 0 else slice(64, 128)
                            P.op("dve", "tensor_reduce", reads=[denk], writes=[rdenk], out=rden[half, :],
                                 in_=den[half, :].rearrange("p (u q) -> p q u", u=4), axis=AX.X, op=ALU.add)
                            P.op("dve", "reciprocal", reads=[rdenk], writes=[rdenk], out=rden[half, :], in_=rden[half, :])
                            P.op("dve", "tensor_tensor", reads=[acck, rdenk], writes=[oTk], out=oT[half, s // 2, :], in0=acc[half, 0:128],
                                 in1=rden[half, :], op=ALU.mult)
                self.out_proj_add(oT, oTk, 4, wo, wok, xt, xtk, [(0, 128)], pos)
                self.store_x(xt, xtk, "p", qt * 128, 128)
        with self.phase("a2s_%d" % l) as ph:
            stg = Rot([ph.sbuf("stg%d" % i, [128, 4, 512], F32) for i in range(2)], ph.name + "stg")
            wo, wok = self.load_weight(ph, "wo", self.w["w_a_o"][j], 512, stg, ncols=D, KCH=4)
            CBS = ph.sbuf("CBS", [128, 24], F32)
            P.dma(CBS[:], self.c_CBS, writes=["CBS"])
            xts = Rot([ph.sbuf("xt%d" % i, [128, 1, D], F32) for i in range(1)], ph.name + "xt")
            PTs = Rot([ph.sbuf("PT%d" % i, [128, 128], BF16) for i in range(3)], ph.name + "PT")
            rden = ph.sbuf("rden", [128, 128], F32)
            rdenk = ph.name + "rden"
            pss = Rot([ph.psum("ps%d" % i, [128, 512], F32) for i in range(2)], ph.name + "ps")
            paccs = Rot([ph.psum("pacc%d" % i, [128, 512], F32) for i in range(2)], ph.name + "pacc")
            pdens = Rot([ph.psum("pden%d" % i, [128, 512], F32) for i in range(2)], ph.name + "pden")
            pos = Rot([ph.psum("po%d" % i, [128, 512], F32) for i in range(1)], ph.name + "po")
            vrow = ph.sbuf("vrow", [1, 1536], BF16)
            kvc = Rot([ph.sbuf("kvc%d" % i, [128, 1024], F32) for i in range(2)], ph.name + "kvc")
            kvbs = Rot([ph.sbuf("kvb%d" % i, [128, 1024], BF16) for i in range(3)], ph.name + "kvb")
            kTcs = Rot([ph.sbuf("kTc%d" % i, [128, 4, 128], BF16) for i in range(3)], ph.name + "kTc")
            nb = {"pT": Rot([ph.psum("npT0", [128, 8, 128], BF16)], ph.name + "npT")}
            oTS = ph.sbuf("oTS", [128, 4, NS], BF16)
            oTSk = ph.name + "oTS"
            caches = (self.cache_dil128, self.cache_dil512, self.cache_dil2048)
            for sI in range(NS):
                P.dma(vrow[:], VS_d[sI:sI + 1, :], reads=["VS_d"], writes=["vrow"])
                grp = []
                for gI in range(3):
                    kv, kvk = kvc.next()
                    W, dd = WIN[gI], DIL[gI]
                    src = caches[gI][j, sI].rearrange("(i d) f -> i d f", d=dd)[:, 0, :]
                    P.dma(kv[:], src, writes=[kvk])
                    kvb, kvbk = kvbs.next()
                    self.cast(kvb[:], kv[:], [kvk], [kvbk])
                    kTc, kTck = kTcs.next()
                    self.transpose_to(nb, kvb[:, 0:512], kvbk, 4, 128, kTc[:, :, :], kTck)
                    grp.append((kvb, kvbk, kTc, kTck))
                for s in range(8):
                    half = slice(0, 64) if s % 2 == 0 else slice(64, 128)
                    acc, acck = paccs.next()
                    den, denk = pdens.next()
                    for gI in range(3):
                        h = gI * 8 + s
                        c = gI * 4 + s // 2
                        kvb, kvbk, kTc, kTck = grp[gI]
                        ps, psk = pss.next()
                        P.op("pe", "matmul", reads=[kTck, "A_qks"], writes=[psk], out=ps[:, 0:1], lhsT=kTc[half, s // 2, :],
                             rhs=self.A_qks[half, c, sI:sI + 1], start=True, stop=True)
                        PT, PTk = PTs.next()
                        P.op("act", "activation", reads=[psk, "CBS"], writes=[PTk], out=PT[:, 0:1], in_=ps[:, 0:1], func=AF.Exp,
                             bias=CBS[:, h:h + 1])
                        P.op("pe", "matmul", reads=[kvbk, PTk], writes=[acck], out=acc[:, 0:1],
                             lhsT=kvb[:, 512 + (s // 2) * 128:512 + (s // 2 + 1) * 128], rhs=PT[:, 0:1], start=(gI == 0), stop=False)
                        P.op("pe", "matmul", reads=["ones", PTk], writes=[denk], out=den[:, 0:1], lhsT=self.ones[:, :], rhs=PT[:, 0:1],
                             start=(gI == 0), stop=False)
                        ps, psk = pss.next()
                        P.op("pe", "matmul", reads=["A_qks"], writes=[psk], out=ps[0:1, 0:1], lhsT=self.A_qks[half, 12 + c, sI:sI + 1],
                             rhs=self.A_qks[half, c, sI:sI + 1], start=True, stop=True)
                        PT, PTk = PTs.next()
                        P.op("act", "activation", reads=[psk], writes=[PTk], out=PT[0:1, 0:1], in_=ps[0:1, 0:1], func=AF.Exp)
                        v0 = gI * 512 + (s // 2) * 128
                        P.op("pe", "matmul", reads=["vrow", PTk], writes=[acck], out=acc[:, 0:1], lhsT=vrow[0:1, v0:v0 + 128], rhs=PT[0:1, 0:1],
                             start=False, stop=(gI == 2))
                        P.op("pe", "matmul", reads=["ones", PTk], writes=[denk], out=den[:, 0:1], lhsT=self.ones[0:1, :], rhs=PT[0:1, 0:1],
                             start=False, stop=(gI == 2))
                    hs = slice(0, 64) if s % 2 == 0 else slice(64, 128)
                    P.op("dve", "reciprocal", reads=[denk], writes=[rdenk], out=rden[hs, 0:1], in_=den[hs, 0:1])
                    P.op("dve", "tensor_tensor", reads=[acck, rdenk], writes=[oTSk], out=oTS[hs, s // 2, sI:sI + 1], in0=acc[hs, 0:1],
                         in1=rden[hs, 0:1], op=ALU.mult)
            xt, xtk = xts.next()
            tiles = self.load_x(xt, xtk, "s", 0, NS)
            self.out_proj_add(oTS, oTSk, 4, wo, wok, xt, xtk, tiles, pos)
            self.store_x(xt, xtk, "s", 0, NS)

    def gla_chunk(self, bufs, c, qkT_ap, la_ap, v_ap, S, Sk, Sb, Sbk, o_ap, ok, rk):
        P = self.P
        pb, pbk = bufs["pb"].next()
        for h in range(4):
            P.op("pe", "matmul", reads=rk + ["TRI"], writes=[pbk], out=pb[:, h, :c], lhsT=la_ap[:, h * 128:(h + 1) * 128],
                 rhs=self.TRI[:c, :c], start=True, stop=True)
        eb, ebk = bufs["eb"].next()
        enb, enbk = bufs["enb"].next()
        P.op("act", "activation", reads=[pbk], writes=[ebk], out=eb[:, :, :c], in_=pb[:, :, :c], func=AF.Exp)
        P.op("act", "activation", reads=[pbk], writes=[enbk], out=enb[:, :, :c], in_=pb[:, :, :c], func=AF.Exp, scale=-1.0)
        qt, qtk = bufs["qt"].next()
        kt_, ktk = bufs["kt"].next()
        q32, q32k = bufs["q32"].next()
        k32, k32k = bufs["k32"].next()
        P.op("dve", "tensor_tensor", reads=rk + [ebk], writes=[q32k], out=q32[:, :, :c], in0=qkT_ap[:, 0:4, :], in1=eb[:, :, :c], op=ALU.mult)
        P.op("pool", "tensor_tensor", reads=rk + [enbk], writes=[k32k], out=k32[:, :, :c], in0=qkT_ap[:, 4:8, :], in1=enb[:, :, :c], op=ALU.mult)
        P.op("act", "activation", reads=[q32k], writes=[qtk], out=qt[:, :, :c], in_=q32[:, :, :c], func=AF.Copy)
        P.op("pool", "tensor_copy", reads=[k32k], writes=[ktk], out=kt_[:, :, :c], in_=k32[:, :, :c])
        pa, pak = bufs["pa"].next()
        for h in range(4):
            P.op("pe", "matmul", reads=[q32k, k32k], writes=[pak], out=pa[:c, h, :c], lhsT=k32[:, h, :c], rhs=q32[:, h, :c], start=True, stop=True)
        at, atk = bufs["at"].next()
        P.op("dve", "tensor_tensor", reads=[pak, "TRI"], writes=[atk], out=at[:c, :, :c], in0=pa[:c, :, :c],
             in1=self.TRI4[:c, :].rearrange("p (h i) -> p h i", h=4)[:, :, :c], op=ALU.mult)
        pt, ptk = bufs["pt"].next()
        for h in range(4):
            P.op("pe", "transpose", reads=[ktk, "ident"], writes=[ptk], out=pt[:c, h, :], in_=kt_[:, h, :c], identity=self.ident[:, :])
        ktm, ktmk = bufs["ktm"].next()
        P.op("act", "activation", reads=[ptk], writes=[ktmk], out=ktm[:c, :, :], in_=pt[:c, :, :], func=AF.Copy)
        for hp in range(2):
            po, pok = bufs["po"].next()
            for hh in range(2):
                h = hp * 2 + hh
                P.op("pe", "matmul", reads=[atk] + rk, writes=[pok], out=po[:c, hh, :], lhsT=at[:c, h, :c], rhs=v_ap[:, h * 256:(h + 1) * 256],
                     start=True, stop=False)
                P.op("pe", "matmul", reads=[qtk, Sbk], writes=[pok], out=po[:c, hh, :], lhsT=qt[:, h, :c], rhs=Sb[:, h, :], start=False, stop=True)
            P.op("dve", "tensor_copy", reads=[pok], writes=[ok], out=o_ap[:, hp * 512:(hp + 1) * 512], in_=po[:c, :, :].rearrange("p a v -> p (a v)"))
        for hp in range(2):
            pk, pkk = bufs["pk"].next()
            for hh in range(2):
                h = hp * 2 + hh
                P.op("pe", "matmul", reads=[ktmk] + rk, writes=[pkk], out=pk[:, hh, :], lhsT=ktm[:c, h, :], rhs=v_ap[:, h * 256:(h + 1) * 256],
                     start=True, stop=True)
            T, Tk = bufs["T"].next()
            P.op("dve", "tensor_tensor", reads=[pkk, Sk], writes=[Tk], out=T[:, :, :], in0=S[:, hp * 2:hp * 2 + 2, :], in1=pk[:, :, :], op=ALU.add)
            for hh in range(2):
                h = hp * 2 + hh
                P.op("pool", "tensor_scalar", reads=[Tk, ebk], writes=[Sk], out=S[:, h, :], in0=T[:, hh, :], scalar1=eb[:, h, c - 1:c],
                     scalar2=None, op0=ALU.mult)
                P.op("act", "activation", reads=[Tk, ebk], writes=[Sbk], out=Sb[:, h, :], in_=T[:, hh, :], func=AF.Copy, scale=eb[:, h, c - 1:c])

    def phase_mixB(self, l, j):
        P = self.P
        L, NS = self.L, self.NS
        TG = 512
        QK_d, LA_d, VB_d, SR_d, O_d = self.QK_d, self.LA_d, self.VB_d, self.SR_d, self.O_d
        QKS_d, LAS_d, VBS_d, SRS_d, OS_d = self.QKS_d, self.LAS_d, self.VBS_d, self.SRS_d, self.OS_d
        with self.phase("b1_%d" % l) as ph:
            stg = Rot([ph.sbuf("stg%d" % i, [128, 8, 512], F32) for i in range(2)], ph.name + "stg")
            win, wk = self.load_weight(ph, "win", self.w["w_b_in"][j], D, stg, col0=1024, ncols=2064)
            wqk = ph.sbuf("wqk", [128, 8, 1024], F32)
            P.dma(wqk[:], self.w["w_b_in"][j].rearrange("(k p) n -> p k n", p=128)[:, :, 0:1024], writes=["wqk"])
            g, gk = self.load_gain(ph, "g", self.w["g_mix"][l])
            hT32 = ph.sbuf("hT32", [128, 8, TG], F32)
            hf32 = Rot([ph.sbuf("hf32_%d" % i, [128, D], F32) for i in range(1)], ph.name + "hf32")
            pT32 = Rot([ph.psum("pT32", [128, 4, 128], F32)], ph.name + "pT32")
            wgs = ph.sbuf("wgs", [17, 512], F32)
            P.dma(wgs[0:16, :], self.w["w_b_gate2"][j], writes=["wgs"])
            P.dma(wgs[16:17, :], self.w["b_b_gate"][j:j + 1, :], writes=["wgs"])
            wg = ph.sbuf("wg", [17, 512], BF16)
            P.op("dve", "tensor_copy", reads=["wgs"], writes=["wg"], out=wg[:], in_=wgs[:])
            onesf = ph.sbuf("onesf", [128, 1], F32)
            P.op("pool", "memset", writes=["onesf"], ap=onesf[:], constant=1.0)
            nb = self.make_norm_bufs(ph)
            xts = Rot([ph.sbuf("xt%d" % i, [128, TG // 128, D], F32) for i in range(1)], ph.name + "xt")
            hT = ph.sbuf("hT", [128, 8, TG], BF16)
            hTk = ph.name + "hT"
            qkf = ph.sbuf("qkf", [128, 8, TG], F32)
            qkfk = ph.name + "qkf"
            glT = ph.sbuf("glT", [17, TG], BF16)
            glTk = ph.name + "glT"
            P.op("pool", "memset", writes=[glTk], ap=glT[:], constant=1.0)
            ezs = Rot([ph.sbuf("ez%d" % i, [128, 512], F32) for i in range(2)], ph.name + "ez")
            las = Rot([ph.sbuf("la%d" % i, [128, 512], F32) for i in range(2)], ph.name + "la")
            vbs = Rot([ph.sbuf("vb%d" % i, [128, 1024], BF16) for i in range(2)], ph.name + "vb")
            srs = Rot([ph.sbuf("sr%d" % i, [128, 1024], F32) for i in range(2)], ph.name + "sr")
            pos = Rot([ph.psum("po%d" % i, [128, 512], F32) for i in range(4)], ph.name + "po")
            for kind, r0, n in self.row_groups(TG):
                xt, xtk = xts.next()
                tiles = self.load_x(xt, xtk, kind, r0, n)
                for t, nt in tiles:
                    self.norm_T(nb, xt[:nt, t, :], nt, xtk, g, gk, hT[:, :, t * 128:t * 128 + nt], hTk)
                    rs, rsk = self.rstd(nb, xt[:nt, t, :], nt, xtk)
                    hf, hfk = hf32.next()
                    P.op("dve", "scalar_tensor_tensor", reads=[xtk, rsk, gk], writes=[hfk], out=hf[:nt], in0=xt[:nt, t, :],
                         scalar=rs[:nt, 0:1], in1=g[:nt], op0=ALU.mult, op1=ALU.mult)
                    for c0 in range(0, 8, 4):
                        pt3, pt3k = pT32.next()
                        for k in range(4):
                            P.op("pe", "transpose", reads=[hfk, "identf"], writes=[pt3k], out=pt3[:, k, :nt],
                                 in_=hf[:nt, (c0 + k) * 128:(c0 + k + 1) * 128], identity=self.identf[:nt, :nt])
                        P.op("act", "activation", reads=[pt3k], writes=["hT32"], out=hT32[:, c0:c0 + 4, t * 128:t * 128 + nt],
                             in_=pt3[:, :, :nt], func=AF.Copy)
                for c in range(8):
                    po, pok = pos.next()
                    for k in range(8):
                        P.op("pe", "matmul", reads=["hT32", "wqk"], writes=[pok], out=po[:, :n], lhsT=wqk[:, k, c * 128:(c + 1) * 128],
                             rhs=hT32[:, k, :n], start=(k == 0), stop=(k == 7))
                    if c < 4:
                        P.op("act", "activation", reads=[pok], writes=[qkfk], out=qkf[:, c, :n], in_=po[:, :n], func=AF.Copy, scale=128.0 ** -0.5)
                    else:
                        P.op("dve", "tensor_copy", reads=[pok], writes=[qkfk], out=qkf[:, c, :n], in_=po[:, :n])
                if kind == "p":
                    P.dma(QK_d[:, r0:r0 + n].rearrange("(c p) t -> p c t", p=128), qkf[:, :, :n], reads=[qkfk], writes=["QK_d%d" % r0])
                else:
                    P.dma(QKS_d.rearrange("(c p) t -> p c t", p=128), qkf[:, :, :n], reads=[qkfk], writes=["QKS_d"])
                po, pok = pos.next()
                for k in range(8):
                    P.op("pe", "matmul", reads=[hTk, wk], writes=[pok], out=po[:16, :n], lhsT=win[:, k, 1024:1040], rhs=hT[:, k, :n],
                         start=(k == 0), stop=(k == 7))
                P.op("act", "activation", reads=[pok], writes=[glTk], out=glT[0:16, :n], in_=po[:16, :n], func=AF.Copy)
                for t, nt in tiles:
                    row0 = r0 + t * 128
                    po, pok = pos.next()
                    P.op("pe", "matmul", reads=[glTk, "wg"], writes=[pok], out=po[:nt, :], lhsT=glT[:, t * 128:t * 128 + nt], rhs=wg[:, :],
                         start=True, stop=True)
                    ez, ezk = ezs.next()
                    P.op("act", "activation", reads=[pok], writes=[ezk], out=ez[:nt, :], in_=po[:nt, :], func=AF.Exp, scale=-1.0)
                    la, lak = las.next()
                    P.op("act", "activation", reads=[ezk, "onesf"], writes=[lak], out=la[:nt, :], in_=ez[:nt, :], func=AF.Ln, bias=onesf[:nt])
                    P.op("dve", "tensor_scalar", reads=[lak], writes=[lak], out=la[:nt, :], in0=la[:nt, :], scalar1=-1.0 / 16, scalar2=None, op0=ALU.mult)
                    if kind == "p":
                        P.dma(LA_d[row0:row0 + nt, :], la[:nt, :], reads=[lak], writes=["LA_d%d" % row0])
                    else:
                        P.dma(LAS_d[:, :], la[:nt, :], reads=[lak], writes=["LAS_d"])
                    vb, vbk = vbs.next()
                    sr, srk = srs.next()
                    for hf in range(2):
                        po, pok = pos.next()
                        for k in range(8):
                            P.op("pe", "matmul", reads=[hTk, wk], writes=[pok], out=po[:nt, :], lhsT=hT[:, k, t * 128:t * 128 + nt],
                                 rhs=win[:, k, hf * 512:(hf + 1) * 512], start=(k == 0), stop=(k == 7))
                        P.op("dve", "tensor_copy", reads=[pok], writes=[vbk], out=vb[:nt, hf * 512:(hf + 1) * 512], in_=po[:nt, :])
                        po, pok = pos.next()
                        for k in range(8):
                            P.op("pe", "matmul", reads=[hTk, wk], writes=[pok], out=po[:nt, :], lhsT=hT[:, k, t * 128:t * 128 + nt],
                                 rhs=win[:, k, 1040 + hf * 512:1040 + (hf + 1) * 512], start=(k == 0), stop=(k == 7))
                        P.op("act", "activation", reads=[pok], writes=[srk], out=sr[:nt, hf * 512:(hf + 1) * 512], in_=po[:nt, :], func=AF.Silu)
                    if kind == "p":
                        P.dma(VB_d[row0:row0 + nt, :], vb[:nt, :], reads=[vbk], writes=["VB_d%d" % row0])
                        P.dma(SR_d[row0:row0 + nt, :], sr[:nt, :], reads=[srk], writes=["SR_d%d" % row0])
                    else:
                        P.dma(VBS_d[:, :], vb[:nt, :], reads=[vbk], writes=["VBS_d"])
                        P.dma(SRS_d[:, :], sr[:nt, :], reads=[srk], writes=["SRS_d"])
        with self.phase("b2_%d" % l) as ph:
            self.TRI4 = ph.sbuf("TRI4", [64, 256], F32)
            P.dma(self.TRI4[:], self.c_TRI4, writes=["TRI"])
            self.TRI = self.TRI4[:, 0:64]
            bufs = {
                "pb": Rot([ph.psum("pb0", [128, 4, 64], F32)], ph.name + "pb"),
                "pa": Rot([ph.psum("pa0", [64, 4, 64], F32)], ph.name + "pa"),
                "pt": Rot([ph.psum("pt0", [64, 4, 128], BF16)], ph.name + "pt"),
                "po": Rot([ph.psum("po%d" % i, [64, 2, 256], F32) for i in range(2)], ph.name + "po"),
                "pk": Rot([ph.psum("pk%d" % i, [128, 2, 256], F32) for i in range(2)], ph.name + "pk"),
                "eb": Rot([ph.sbuf("eb%d" % i, [128, 4, 64], F32) for i in range(2)], ph.name + "eb"),
                "enb": Rot([ph.sbuf("enb%d" % i, [128, 4, 64], F32) for i in range(2)], ph.name + "enb"),
                "qt": Rot([ph.sbuf("qt%d" % i, [128, 4, 64], BF16) for i in range(2)], ph.name + "qt"),
                "q32": Rot([ph.sbuf("q32_%d" % i, [128, 4, 64], F32) for i in range(2)], ph.name + "q32"),
                "k32": Rot([ph.sbuf("k32_%d" % i, [128, 4, 64], F32) for i in range(2)], ph.name + "k32"),
                "kt": Rot([ph.sbuf("kt%d" % i, [128, 4, 64], BF16) for i in range(2)], ph.name + "kt"),
                "at": Rot([ph.sbuf("at%d" % i, [64, 4, 64], BF16) for i in range(2)], ph.name + "at"),
                "ktm": Rot([ph.sbuf("ktm%d" % i, [64, 4, 128], BF16) for i in range(2)], ph.name + "ktm"),
                "T": Rot([ph.sbuf("T%d" % i, [128, 2, 256], F32) for i in range(2)], ph.name + "T"),
            }
            S = ph.sbuf("S", [128, 4, 256], F32)
            Sb = ph.sbuf("Sb", [128, 4, 256], BF16)
            Sk, Sbk = ph.name + "S", ph.name + "Sb"
            qks = Rot([ph.sbuf("qk%d" % i, [128, 8, TG], F32) for i in range(2)], ph.name + "qk")
            lag = Rot([ph.sbuf("lag%d" % i, [64, TG // 64, 512], F32) for i in range(2)], ph.name + "lag")
            vg = Rot([ph.sbuf("vg%d" % i, [64, TG // 64, 1024], BF16) for i in range(2)], ph.name + "vg")
            og = Rot([ph.sbuf("og%d" % i, [64, TG // 64, 1024], F32) for i in range(2)], ph.name + "og")
            P.op("pool", "memset", writes=[Sk], ap=S[:], constant=0.0)
            P.op("pool", "memset", writes=[Sbk], ap=Sb[:], constant=0.0)
            for r0 in range(0, L, TG):
                n = min(TG, L - r0)
                qk, qkk = qks.next()
                P.dma(qk[:, :, :n], QK_d[:, r0:r0 + n].rearrange("(c p) t -> p c t", p=128), reads=["QK_d%d" % r0], writes=[qkk])
                la, lak = lag.next()
                P.dma(la[:, :n // 64, :], LA_d[r0:r0 + n, :].rearrange("(c p) f -> p c f", p=64),
                      reads=["LA_d%d" % (r0 + i) for i in range(0, n, 128)], writes=[lak])
                v, vk = vg.next()
                P.dma(v[:, :n // 64, :], VB_d[r0:r0 + n, :].rearrange("(c p) f -> p c f", p=64),
                      reads=["VB_d%d" % (r0 + i) for i in range(0, n, 128)], writes=[vk])
                o, ok = og.next()
                for ci in range(n // 64):
                    self.gla_chunk(bufs, 64, qk[:, :, ci * 64:(ci + 1) * 64], la[:, ci, :], v[:, ci, :], S, Sk, Sb, Sbk, o[:, ci, :], ok,
                                   [qkk, lak, vk])
                P.dma(O_d[r0:r0 + n, :].rearrange("(c p) f -> p c f", p=64), o[:, :n // 64, :], reads=[ok],
                      writes=["O_d%d" % (r0 + i) for i in range(0, n, 128)])
            P.dma(self.o_gla_p.rearrange("h k v -> k h v"), S[:], reads=[Sk], writes=["ogla_p"])
            qk, qkk = qks.next()
            P.dma(qk[:, :, :NS], QKS_d.rearrange("(c p) t -> p c t", p=128), reads=["QKS_d"], writes=[qkk])
            for s in range(NS):
                P.dma(S[:], self.state_gla[s].rearrange("h k v -> k h v"), reads=["ogla_p", "ogla_s"], writes=[Sk])
                P.op("act", "activation", reads=[Sk], writes=[Sbk], out=Sb[:], in_=S[:], func=AF.Copy)
                la, lak = lag.next()
                P.dma(la[0:1, 0, :], LAS_d[s:s + 1, :], reads=["LAS_d"], writes=[lak])
                v, vk = vg.next()
                P.dma(v[0:1, 0, :], VBS_d[s:s + 1, :], reads=["VBS_d"], writes=[vk])
                o, ok = og.next()
                self.gla_chunk(bufs, 1, qk[:, :, s:s + 1], la[0:1, 0, :], v[0:1, 0, :], S, Sk, Sb, Sbk, o[0:1, 0, :], ok, [qkk, lak, vk])
                P.dma(OS_d[s:s + 1, :], o[0:1, 0, :], reads=[ok], writes=["OS_d"])
                P.dma(self.o_gla_s[s].rearrange("h k v -> k h v"), S[:], reads=[Sk], writes=["ogla_s"])
        with self.phase("b3_%d" % l) as ph:
            stg = Rot([ph.sbuf("stg%d" % i, [128, 8, 512], F32) for i in range(2)], ph.name + "stg")
            wo, wok = self.load_weight(ph, "wo", self.w["w_b_o"][j], D, stg, ncols=D)
            gh, ghk = self.load_gain(ph, "gh", self.w["g_b_head"][j].rearrange("h v -> (h v)"))
            nb = self.make_norm_bufs(ph, nbuf=4)
            xts = Rot([ph.sbuf("xt%d" % i, [128, TG // 128, D], F32) for i in range(1)], ph.name + "xt")
            ots = Rot([ph.sbuf("ot%d" % i, [128, TG // 128, D], F32) for i in range(1)], ph.name + "ot")
            sts = Rot([ph.sbuf("st%d" % i, [128, TG // 128, D], F32) for i in range(1)], ph.name + "st")
            ons = Rot([ph.sbuf("on%d" % i, [128, D], F32) for i in range(2)], ph.name + "on")
            o2s = Rot([ph.sbuf("o2%d" % i, [128, D], BF16) for i in range(2)], ph.name + "o2")
            oT = ph.sbuf("oT", [128, 8, TG], BF16)
            oTk = ph.name + "oT"
            pos = Rot([ph.psum("po%d" % i, [128, 512], F32) for i in range(2)], ph.name + "po")
            for kind, r0, n in self.row_groups(TG):
                xt, xtk = xts.next()
                tiles = self.load_x(xt, xtk, kind, r0, n)
                ot, otk = ots.next()
                st_, stk = sts.next()
                if kind == "p":
                    rd = ["O_d%d" % (r0 + i) for i in range(0, n, 128)]
                    P.dma(ot[:, :n // 128, :], O_d[r0:r0 + n, :].rearrange("(t p) d -> p t d", p=128), reads=rd, writes=[otk])
                    rd = ["SR_d%d" % (r0 + i) for i in range(0, n, 128)]
                    P.dma(st_[:, :n // 128, :], SR_d[r0:r0 + n, :].rearrange("(t p) d -> p t d", p=128), reads=rd, writes=[stk])
                else:
                    P.dma(ot[:n, 0, :], OS_d[:, :], reads=["OS_d"], writes=[otk])
                    P.dma(st_[:n, 0, :], SRS_d[:, :], reads=["SRS_d"], writes=[stk])
                for t, nt in tiles:
                    on, onk = ons.next()
                    for h in range(4):
                        rs, rsk = self.rstd(nb, ot[:nt, t, h * 256:(h + 1) * 256], nt, otk, width=256)
                        P.op("dve", "scalar_tensor_tensor", reads=[otk, rsk, ghk], writes=[onk], out=on[:nt, h * 256:(h + 1) * 256],
                             in0=ot[:nt, t, h * 256:(h + 1) * 256], scalar=rs[:nt, 0:1], in1=gh[:nt, h * 256:(h + 1) * 256],
                             op0=ALU.mult, op1=ALU.mult)
                    o2, o2k = o2s.next()
                    P.op("pool", "tensor_tensor", reads=[onk, stk], writes=[o2k], out=o2[:nt, :], in0=on[:nt, :], in1=st_[:nt, t, :], op=ALU.mult)
                    self.transpose_to(nb, o2[:nt], o2k, 8, nt, oT[:, :, t * 128:t * 128 + nt], oTk)
                self.out_proj_add(oT, oTk, 8, wo, wok, xt, xtk, tiles, pos)
                self.store_x(xt, xtk, kind, r0, n)

    def gelu_tanh(self, ph, tmp, x_ps, xk, hb, n, out_ap, outk):
        P = self.P
        x, xk2 = tmp["x"].next()
        u, uk = tmp["u"].next()
        P.op("act", "activation", reads=[xk, "hb"], writes=[xk2], out=x[:, :n], in_=x_ps, func=AF.Identity, bias=hb)
        P.op("dve", "tensor_tensor", reads=[xk2], writes=[uk], out=u[:, :n], in0=x[:, :n], in1=x[:, :n], op=ALU.mult)
        P.op("dve", "tensor_scalar", reads=[uk], writes=[uk], out=u[:, :n], in0=u[:, :n], scalar1=0.044715, scalar2=1.0, op0=ALU.mult, op1=ALU.add)
        P.op("dve", "tensor_tensor", reads=[uk, xk2], writes=[uk], out=u[:, :n], in0=u[:, :n], in1=x[:, :n], op=ALU.mult)
        P.op("act", "activation", reads=[uk], writes=[uk], out=u[:, :n], in_=u[:, :n], func=AF.Sigmoid, scale=1.5957691216057308)
        P.op("dve", "tensor_tensor", reads=[uk, xk2], writes=[outk], out=out_ap, in0=u[:, :n], in1=x[:, :n], op=ALU.mult)

    def phase_mixC(self, l, j):
        P = self.P
        L, NS = self.L, self.NS
        TG = 512
        NQT = L // 128
        NCMP = L // 16 - 1
        NCT = (NCMP + 127) // 128
        NSLC = L // 64
        QN_d, KN_d, C2_d, VN_d, GT_d, KC_d, VC_d, OT_d = self.QN_d, self.KN_d, self.C2_d, self.VN_d, self.GT_d, self.KC_d, self.VC_d, self.OT_d
        OFF = {"q": 0, "kc": 1024, "vc": 1280, "ks": 1536, "vs": 1792, "kw": 2048, "vw": 2304, "g": 2560}
        with self.phase("c1_%d" % l) as ph:
            stg = Rot([ph.sbuf("stg%d" % i, [128, 8, 512], F32) for i in range(2)], ph.name + "stg")
            win, wk = self.load_weight(ph, "win", self.w["w_c_in"][j], D, stg, ncols=2608)
            g, gk = self.load_gain(ph, "g", self.w["g_mix"][l])
            wdup = ph.sbuf("wdup", [128, 8, 8, 128], BF16)
            for ti, ty in enumerate(("kc", "vc")):
                for G in range(4):
                    src = win[:, :, OFF[ty] + G * 64:OFF[ty] + (G + 1) * 64]
                    P.op("dve", "tensor_copy", reads=[wk], writes=["wdup"], out=wdup[:, ti * 4 + G, :, 0:64], in_=src)
                    P.op("pool", "tensor_copy", reads=[wk], writes=["wdup"], out=wdup[:, ti * 4 + G, :, 64:128], in_=src)
            bg = ph.sbuf("bg", [48, 1], F32)
            P.dma(bg[:], self.w["b_c_gate"][j].rearrange("(p o) -> p o", o=1), writes=["bg"])
            nb = self.make_norm_bufs(ph)
            xts = Rot([ph.sbuf("xt%d" % i, [128, TG // 128, D], F32) for i in range(1)], ph.name + "xt")
            hT = ph.sbuf("hT", [128, 8, TG], BF16)
            hTk = ph.name + "hT"
            qn = ph.sbuf("qn", [64, 16, TG], BF16)
            kn = ph.sbuf("kn", [64, 8, TG], BF16)
            c2 = ph.sbuf("c2", [128, 8, TG], BF16)
            gt = ph.sbuf("gt", [48, TG], BF16)
            sfs = Rot([ph.sbuf("sf%d" % i, [128, 512], F32) for i in range(3)], ph.name + "sf")
            vns = Rot([ph.sbuf("vn%d" % i, [128, 512], BF16) for i in range(2)], ph.name + "vn")
            pos = Rot([ph.psum("po%d" % i, [128, 512], F32) for i in range(4)], ph.name + "po")
            Wn = min(512, L)
            for kind, r0, n in self.row_groups(TG):
                xt, xtk = xts.next()
                tiles = self.load_x(xt, xtk, kind, r0, n)
                for t, nt in tiles:
                    self.norm_T(nb, xt[:nt, t, :], nt, xtk, g, gk, hT[:, :, t * 128:t * 128 + nt], hTk)
                for h in range(16):
                    po, pok = pos.next()
                    for k in range(8):
                        P.op("pe", "matmul", reads=[hTk, wk], writes=[pok], out=po[:64, :n], lhsT=win[:, k, h * 64:(h + 1) * 64], rhs=hT[:, k, :n],
                             start=(k == 0), stop=(k == 7))
                    P.op("act", "activation", reads=[pok], writes=["qn"], out=qn[:, h, :n], in_=po[:64, :n], func=AF.Copy, scale=0.125)
                for ti, ty in enumerate(("ks", "kw")):
                    for G in range(4):
                        po, pok = pos.next()
                        for k in range(8):
                            P.op("pe", "matmul", reads=[hTk, wk], writes=[pok], out=po[:64, :n],
                                 lhsT=win[:, k, OFF[ty] + G * 64:OFF[ty] + (G + 1) * 64], rhs=hT[:, k, :n], start=(k == 0), stop=(k == 7))
                        P.op("dve", "tensor_copy", reads=[pok], writes=["kn"], out=kn[:, ti * 4 + G, :n], in_=po[:64, :n])
                for b8 in range(8):
                    po, pok = pos.next()
                    for k in range(8):
                        P.op("pe", "matmul", reads=[hTk, "wdup"], writes=[pok], out=po[:, :n], lhsT=wdup[:, b8, k, :], rhs=hT[:, k, :n],
                             start=(k == 0), stop=(k == 7))
                    self.cast(c2[:, b8, :n], po[:, :n], [pok], ["c2"], eng=("act", "dve")[b8 % 2])
                po, pok = pos.next()
                for k in range(8):
                    P.op("pe", "matmul", reads=[hTk, wk], writes=[pok], out=po[:48, :n], lhsT=win[:, k, 2560:2608], rhs=hT[:, k, :n],
                         start=(k == 0), stop=(k == 7))
                P.op("act", "activation", reads=[pok, "bg"], writes=["gt"], out=gt[:, :n], in_=po[:48, :n], func=AF.Sigmoid, bias=bg[:, 0:1])
                if kind == "p":
                    P.dma(QN_d[:, :, r0:r0 + n], qn[:, :, :n], reads=["qn"], writes=["QN_d%d" % r0])
                    P.dma(KN_d[:, :, r0:r0 + n], kn[:, :, :n], reads=["kn"], writes=["KN_d%d" % r0])
                    P.dma(GT_d[:, r0:r0 + n], gt[:, :n], reads=["gt"], writes=["GT_d%d" % r0])
                    c2v = C2_d.rearrange("(b p) t -> p b t", p=128)
                    P.dma(c2v[0:64, :, r0:r0 + n], c2[0:64, :, :n], reads=["c2"], writes=["C2_d"])
                    if r0 == 0:
                        P.dma(c2v[64:128, :, 0:n - 1], c2[64:128, :, 1:n], reads=["c2"], writes=["C2_d"])
                    else:
                        P.dma(c2v[64:128, :, r0 - 1:r0 + n - 1], c2[64:128, :, :n], reads=["c2"], writes=["C2_d"])
                else:
                    P.dma(self.QNS_d[:, :, :], qn[:, :, :n], reads=["qn"], writes=["QNS_d"])
                    P.dma(self.KNS_d[:, :, :], kn[:, :, :n], reads=["kn"], writes=["KNS_d"])
                    P.dma(self.GTS_d[:, :], gt[:, :n], reads=["gt"], writes=["GTS_d"])
                for t, nt in tiles:
                    row0 = r0 + t * 128
                    vn, vnk = vns.next()
                    for blk in range(3):
                        need_f32 = blk < 2 or kind == "s" or (row0 + nt > L - Wn)
                        po, pok = pos.next()
                        for k in range(8):
                            P.op("pe", "matmul", reads=[hTk, wk], writes=[pok], out=po[:nt, :], lhsT=hT[:, k, t * 128:t * 128 + nt],
                                 rhs=win[:, k, 1024 + blk * 512:1024 + (blk + 1) * 512], start=(k == 0), stop=(k == 7))
                        sf, sfk = sfs.next()
                        P.op("dve", "tensor_copy", reads=[pok], writes=[sfk], out=sf[:nt, :], in_=po[:nt, :])
                        if blk == 1:
                            P.op("act", "activation", reads=[sfk], writes=[vnk], out=vn[:nt, 0:256], in_=sf[:nt, 256:512], func=AF.Copy)
                        if blk == 2:
                            P.op("act", "activation", reads=[sfk], writes=[vnk], out=vn[:nt, 256:512], in_=sf[:nt, 256:512], func=AF.Copy)
                        if kind == "p":
                            if blk < 2:
                                P.dma(self.o_nsa_kv_p[row0:row0 + nt, blk * 512:(blk + 1) * 512], sf[:nt, :], reads=[sfk], writes=["onsa"])
                            elif need_f32:
                                o0 = row0 - (L - Wn)
                                P.dma(self.o_nsa_win_p[o0:o0 + nt, :], sf[:nt, :], reads=[sfk], writes=["onsa"])
                        else:
                            if blk < 2:
                                P.dma(self.o_nsa_kv_s[:, blk * 512:(blk + 1) * 512], sf[:nt, :], reads=[sfk], writes=["onsa"])
                            else:
                                P.dma(self.o_nsa_win_s[:, :], sf[:nt, :], reads=[sfk], writes=["onsa"])
                    if kind == "p":
                        P.dma(VN_d[row0:row0 + nt, :], vn[:nt, :], reads=[vnk], writes=["VN_d%d" % row0])
                    else:
                        P.dma(self.VNS_d[:, :], vn[:nt, :], reads=[vnk], writes=["VNS_d"])
        with self.phase("c2_%d" % l) as ph:
            stg = Rot([ph.sbuf("stg%d" % i, [128, 16, 128], F32) for i in range(2)], ph.name + "stg")
            w1 = []
            for ti, nm in enumerate(("w_c_k1", "w_c_v1")):
                wb, wbk = self.load_weight(ph, "w1_%d" % ti, self.w[nm][j], 2048, stg, ncols=128, CH=128, KCH=16)
                w1.append((wb, wbk))
            w2 = []
            pw = Rot([ph.sbuf("pw%d" % i, [128, 16], F32) for i in range(2)], ph.name + "pw")
            posb = []
            for ti, (nm, pn) in enumerate((("w_c_k2", "c_pos_k"), ("w_c_v2", "c_pos_v"))):
                s, sk = stg.next()
                P.dma(s[:, 0, 0:64], self.w[nm][j], writes=[sk])
                wb2 = ph.sbuf("w2_%d" % ti, [128, 64], BF16)
                P.op("dve", "tensor_copy", reads=[sk], writes=["w2_%d" % ti], out=wb2[:], in_=s[:, 0, 0:64])
                w2.append((wb2, "w2_%d" % ti))
                pf, pfk = pw.next()
                P.dma(pf[:], self.w[pn][j].rearrange("(k a) d -> (a d) k", a=2), writes=[pfk], allow_slow_non_contiguous=True)
                pb_ = ph.sbuf("posb%d" % ti, [128, 16], BF16)
                P.op("dve", "tensor_copy", reads=[pfk], writes=["posb%d" % ti], out=pb_[:], in_=pf[:])
                posb.append((pb_, "posb%d" % ti))
            tmp = {"x": Rot([ph.sbuf("gx%d" % i, [128, 256], F32) for i in range(2)], ph.name + "gx"),
                   "u": Rot([ph.sbuf("gu%d" % i, [128, 256], F32) for i in range(2)], ph.name + "gu")}
            php = Rot([ph.psum("php%d" % i, [128, 512], F32) for i in range(2)], ph.name + "php")
            pco = Rot([ph.psum("pco%d" % i, [128, 512], F32) for i in range(2)], ph.name + "pco")
            hbs = []
            for ti in range(2):
                pc, pck = pco.next()
                for kc in range(16):
                    P.op("pe", "matmul", reads=[w1[ti][1], posb[ti][1]], writes=[pck], out=pc[:, 0:1], lhsT=w1[ti][0][:, kc, :],
                         rhs=posb[ti][0][:, kc:kc + 1], start=(kc == 0), stop=(kc == 15))
                hb = ph.sbuf("hb%d" % ti, [128, 1], F32)
                P.op("dve", "tensor_copy", reads=[pck], writes=["hb"], out=hb[:], in_=pc[:, 0:1])
                hbs.append(hb)
            X2s = Rot([ph.sbuf("X2_%d" % i, [128, L], BF16) for i in range(2)], ph.name + "X2")
            gels = Rot([ph.sbuf("gel%d" % i, [128, 256], BF16) for i in range(2)], ph.name + "gel")
            kcs = ph.sbuf("kcs", [64, 4, 256], BF16)
            vcs = ph.sbuf("vcs", [128, NCT, 4, 64], BF16)
            for ti in range(2):
                for G in range(4):
                    X2, X2k = X2s.next()
                    b8 = ti * 4 + G
                    P.dma(X2[:], C2_d[b8 * 128:(b8 + 1) * 128, :], reads=["C2_d"], writes=[X2k])
                    for ct in range(NCT):
                        c0 = ct * 128
                        nk = min(128, NCMP - c0)
                        hp, hpk = php.next()
                        for kc in range(16):
                            st0 = 16 * c0 + 2 * kc
                            P.op("pe", "matmul", reads=[w1[ti][1], X2k], writes=[hpk], out=hp[:, :nk], lhsT=w1[ti][0][:, kc, :],
                                 rhs=X2[:, st0:st0 + 16 * (nk - 1) + 1:16], start=(kc == 0), stop=(kc == 15))
                        gel, gelk = gels.next()
                        self.gelu_tanh(ph, tmp, hp[:, :nk], hpk, hbs[ti][:, 0:1], nk, gel[:, :nk], gelk)
                        pc, pck = pco.next()
                        if ti == 0:
                            P.op("pe", "matmul", reads=[gelk, w2[0][1]], writes=[pck], out=pc[:64, :nk], lhsT=w2[0][0][:, :], rhs=gel[:, :nk],
                                 start=True, stop=True)
                            P.op("act", "activation", reads=[pck], writes=["kcs"], out=kcs[:, G, c0:c0 + nk], in_=pc[:64, :nk], func=AF.Copy)
                        else:
                            P.op("pe", "matmul", reads=[gelk, w2[1][1]], writes=[pck], out=pc[:nk, :64], lhsT=gel[:, :nk], rhs=w2[1][0][:, :],
                                 start=True, stop=True)
                            P.op("act", "activation", reads=[pck], writes=["vcs"], out=vcs[:nk, ct, G, :], in_=pc[:nk, :64], func=AF.Copy)
            P.dma(KC_d[:, :, 0:NCMP], kcs[:, :, 0:NCMP], reads=["kcs"], writes=["KC_d"])
            for ct in range(NCT):
                nk = min(128, NCMP - ct * 128)
                P.dma(VC_d[ct * 128:ct * 128 + nk, :, :], vcs[:nk, ct, :, :], reads=["vcs"], writes=["VC_d"])
        with self.phase("c3_%d" % l) as ph:
            KN = ph.sbuf("KN", [64, 8, L], BF16)
            for r0 in range(0, L, TG):
                P.dma(KN[:, :, r0:r0 + TG], KN_d[:, :, r0:r0 + TG], reads=["KN_d%d" % r0], writes=["KN"])
            VN = ph.sbuf("VN", [128, NQT, 512], BF16)
            P.dma(VN[:], VN_d.rearrange("(t p) f -> p t f", p=128), reads=["VN_d%d" % (t * 128) for t in range(NQT)], writes=["VN"])
            KC = ph.sbuf("KC", [64, 4, 256], BF16)
            P.dma(KC[:, :, 0:NCMP], KC_d[:, :, 0:NCMP], reads=["KC_d"], writes=["KC"])
            VC = ph.sbuf("VC", [128, NCT, 4, 64], BF16)
            for ct in range(NCT):
                nk = min(128, NCMP - ct * 128)
                P.dma(VC[:nk, ct, :, :], VC_d[ct * 128:ct * 128 + nk, :, :], reads=["VC_d"], writes=["VC"])
            GT = ph.sbuf("GT", [48, L], BF16)
            P.dma(GT[:], GT_d, reads=["GT_d%d" % r0 for r0 in range(0, L, TG)], writes=["GT"])

            def const_bf16(name, src, shape2):
                s = ph.sbuf(name + "f", shape2, F32)
                P.dma(s[:], src, writes=[name + "f"])
                d = ph.sbuf(name, shape2, BF16)
                self.cast(d[:], s[:], [name + "f"], [name])
                return d
            COV = const_bf16("COV", self.c_COVER[:, 0:NCT * NSLC], [128, NCT * NSLC])
            Eb = ph.sbuf("Eb", [NSLC, L], BF16)
            sgs = Rot([ph.sbuf("sg%d" % i, [64, 2048], F32) for i in range(2)], ph.name + "sg")
            for c0 in range(0, L, 2048):
                cw = min(2048, L - c0)
                s, sk = sgs.next()
                P.dma(s[:NSLC, :cw], self.c_E[:, c0:c0 + cw], writes=[sk])
                self.cast(Eb[:, c0:c0 + cw], s[:NSLC, :cw], [sk], ["Eb"])
            SEL = ph.sbuf("SELG", [48, 48 * 64], BF16)
            for c0 in range(0, 48 * 64, 1536):
                s, sk = sgs.next()
                P.dma(s[:48, :1536], self.c_SELG[:, c0:c0 + 1536], writes=[sk])
                self.cast(SEL[:, c0:c0 + 1536], s[:48, :1536], [sk], ["SELG"])
            CAUS = ph.sbuf("CAUS", [128, 128], F32)
            P.dma(CAUS[:], self.c_CAUS, writes=["CAUS"])
            BK = ph.sbuf("BK", [128, 16 * 33], F32)
            P.dma(BK[:], self.c_BK, writes=["BK"])
            BKC = ph.sbuf("BKC", [128, 16 * NQT * NCT], F32)
            P.dma(BKC[:], self.c_BKC, writes=["BKC"])
            FT = ph.sbuf("FT", [128, NQT * NSLC], F32)
            P.dma(FT[:], self.c_FT, writes=["FT"])
            identf = self.identf
            Qs = Rot([ph.sbuf("Q%d" % i, [64, 16, 128], BF16) for i in range(2)], ph.name + "Q")
            PTs = Rot([ph.sbuf("PT%d" % i, [128, 512], BF16) for i in range(6)], ph.name + "PT")
            pss = Rot([ph.psum("ps%d" % i, [128, 512], F32) for i in range(4)], ph.name + "ps")
            pNs = Rot([ph.psum("pN%d" % i, [128, 512], F32) for i in range(1)], ph.name + "pN")
            pDs = Rot([ph.psum("pD%d" % i, [128, 512], F32) for i in range(1)], ph.name + "pD")
            pI = ph.psum("pI", [128, 512], F32)
            pM = Rot([ph.psum("pM0", [128, 512], F32)], ph.name + "pM")
            rdens = Rot([ph.sbuf("rden%d" % i, [64, 512], F32) for i in range(2)], ph.name + "rden")
            scs = Rot([ph.sbuf("sc%d" % i, [64, 512], F32) for i in range(2)], ph.name + "sc")
            oacc = ph.sbuf("oacc", [64, 512], F32)
            otmp = ph.sbuf("otmp", [64, 512], F32)
            oTb = ph.sbuf("oTb", [64, 16, 128], BF16)
            impn = ph.sbuf("impn", [64, 512], F32)
            impT = ph.sbuf("impT", [64, 128], F32)
            score = ph.sbuf("score", [128, 64], F32)
            sc2 = ph.sbuf("score2", [128, 64], F32)
            m8 = ph.sbuf("m8", [128, 16], F32)
            thr = ph.sbuf("thr", [128, 1], F32)
            vm = ph.sbuf("vm", [128, 64], F32)
            selb = ph.sbuf("selb", [128, 64], F32)
            selT = ph.sbuf("selT", [64, 128], BF16)
            pmc = ph.sbuf("pmc", [128, 128], F32)

            LA = 3
            tts = Rot([ph.sbuf("tt%d" % i, [128, 4, 128], F32) for i in range(4)], ph.name + "tt")
            msk = ph.sbuf("msk", [128, NQT + 3, 128], BF16)
            BK3 = BK[:, :].rearrange("p (h d) -> p h d", d=33)
            BKC3 = BKC[:, :].rearrange("p (h d) -> p h d", d=NQT * NCT)

            def stage(G, lhsT, lkey, nk, bias_ap, post):
                ps, psk = pss.next()
                P.op("pe", "matmul", reads=[lkey, Qk], writes=[psk], out=ps[:nk, :], lhsT=lhsT, rhs=qrhs, start=True, stop=True)
                tt, ttk = tts.next()
                P.op("dve", "tensor_tensor", reads=[psk, "BK", "BKC"], writes=[ttk], out=tt[:nk, :, :],
                     in0=ps[:nk, :].rearrange("p (a q) -> p a q", a=4), in1=bias_ap.unsqueeze(2).to_broadcast([nk, 4, 128]), op=ALU.add)
                PT, PTk = PTs.next()
                P.op("act", "activation", reads=[ttk], writes=[PTk], out=PT[:nk, :], in_=tt[:nk, :, :].rearrange("p a q -> p (a q)"), func=AF.Exp)
                post(PT, PTk, nk)
                return PT, PTk

            def run_branch(units, pN, pNk, pD, pDk, extra=None):
                staged = {}
                n = len(units)
                for i in range(n + LA):
                    if i < n:
                        u = units[i]
                        staged[i] = stage(G, u["lhsT"], u["lkey"], u["nk"], u["bias"], u["post"])
                    jb = i - LA
                    if jb >= 0:
                        u = units[jb]
                        PT, PTk = staged.pop(jb)
                        nk = u["nk"]
                        fl = dict(start=(jb == 0), stop=(jb == n - 1))
                        P.op("pe", "matmul", reads=[u["vkey"], PTk], writes=[pNk], out=pN[:64, :], lhsT=u["v"], rhs=PT[:nk, :], **fl)
                        P.op("pe", "matmul", reads=["ones", PTk], writes=[pDk], out=pD[:64, :], lhsT=self.ones[:nk, 0:64], rhs=PT[:nk, :], **fl)
                        if extra is not None:
                            extra(u, PT, PTk, nk, fl)

            def finish_branch(pN, pNk, pD, pDk, G, b, first):
                rd, rdk = rdens.next()
                P.op("dve", "tensor_scalar", reads=[pDk], writes=[rdk], out=rd[:], in0=pD[:64, :], scalar1=1e-30, scalar2=None, op0=ALU.max)
                P.op("dve", "reciprocal", reads=[rdk], writes=[rdk], out=rd[:], in_=rd[:])
                pm, pmk = pM.next()
                for i in range(4):
                    r = (4 * G + i) * 3 + b
                    P.op("pe", "matmul", reads=["SELG", "GT"], writes=[pmk], out=pm[:64, i * 128:(i + 1) * 128], lhsT=SEL[:, r * 64:(r + 1) * 64],
                         rhs=self._GTq, start=True, stop=True)
                sc, sck = scs.next()
                P.op("dve", "tensor_tensor", reads=[rdk, pmk], writes=[sck], out=sc[:], in0=rd[:], in1=pm[:64, :], op=ALU.mult)
                if first:
                    P.op("dve", "tensor_tensor", reads=[pNk, sck], writes=["oacc"], out=oacc[:], in0=pN[:64, :], in1=sc[:], op=ALU.mult)
                else:
                    P.op("dve", "tensor_tensor", reads=[pNk, sck], writes=["otmp"], out=otmp[:], in0=pN[:64, :], in1=sc[:], op=ALU.mult)
                    P.op("pool", "tensor_tensor", reads=["otmp", "oacc"], writes=["oacc"], out=oacc[:], in0=oacc[:], in1=otmp[:], op=ALU.add)
                return rd, rdk

            def no_post(PT, PTk, nk):
                pass

            for qt in range(NQT):
                tq0 = qt * 128
                Q, Qk = Qs.next()
                P.dma(Q[:], QN_d[:, :, tq0:tq0 + 128], reads=["QN_d%d" % (tq0 // TG * TG)], writes=[Qk])
                self._GTq = GT[:, tq0:tq0 + 128]
                for G in range(4):
                    qrhs = Q[:, 4 * G:4 * G + 4, :].rearrange("p a q -> p (a q)")
                    pN, pNk = pNs.next()
                    pD, pDk = pDs.next()
                    units = []
                    for ct in [ct for ct in range(NCT) if 16 * 128 * ct + 31 <= tq0 + 127]:
                        nk = min(128, NCMP - ct * 128)

                        def post_c(PT, PTk, nk, ct=ct):
                            if 16 * (128 * ct + nk - 1) + 31 > tq0:
                                P.op("pool", "affine_select", reads=[PTk], writes=[PTk], out=PT[:nk, :], in_=PT[:nk, :], pattern=[[0, 4], [1, 128]],
                                     compare_op=ALU.is_ge, fill=0.0, base=tq0 - 16 * 128 * ct - 31, channel_multiplier=-16)
                        units.append(dict(lhsT=KC[:, G, ct * 128:ct * 128 + nk], lkey="KC", nk=nk, bias=BKC3[:nk, 4 * G:4 * G + 4, qt * NCT + ct],
                                          post=post_c, v=VC[:nk, ct, G, :], vkey="VC", ct=ct))

                    def extra_c(u, PT, PTk, nk, fl):
                        ct = u["ct"]
                        P.op("pe", "matmul", reads=["COV", PTk], writes=["pI"], out=pI[:NSLC, :], lhsT=COV[:nk, ct * NSLC:(ct + 1) * NSLC],
                             rhs=PT[:nk, :], **fl)
                    run_branch(units, pN, pNk, pD, pDk, extra=extra_c)
                    rd, rdk = finish_branch(pN, pNk, pD, pDk, G, 0, True)
                    P.op("dve", "tensor_tensor", reads=["pI", rdk], writes=["impn"], out=impn[:NSLC, :], in0=pI[:NSLC, :], in1=rd[:NSLC, :], op=ALU.mult)
                    P.op("dve", "tensor_reduce", reads=["impn"], writes=["impT"], out=impT[:NSLC, :],
                         in_=impn[:NSLC, :].rearrange("p (a q) -> p q a", a=4), axis=AX.X, op=ALU.add)
                    units = []
                    for kt in range(max(0, qt - 4), qt + 1):
                        def post_w(PT, PTk, nk, kt=kt):
                            if kt == qt:
                                P.op("pool", "affine_select", reads=[PTk], writes=[PTk], out=PT[:, :], in_=PT[:, :], pattern=[[0, 4], [1, 128]],
                                     compare_op=ALU.is_ge, fill=0.0, base=0, channel_multiplier=-1)
                            if kt == qt - 4:
                                P.op("pool", "affine_select", reads=[PTk], writes=[PTk], out=PT[:, :], in_=PT[:, :], pattern=[[0, 4], [-1, 128]],
                                     compare_op=ALU.is_ge, fill=0.0, base=0, channel_multiplier=1)
                        units.append(dict(lhsT=KN[:, 4 + G, kt * 128:(kt + 1) * 128], lkey="KN", nk=128, bias=BK3[:, 4 * G:4 * G + 4, qt - kt],
                                          post=post_w, v=VN[:, kt, 256 + G * 64:256 + (G + 1) * 64], vkey="VN"))
                    pN, pNk = pNs.next()
                    pD, pDk = pDs.next()
                    run_branch(units, pN, pNk, pD, pDk)
                    finish_branch(pN, pNk, pD, pDk, G, 2, False)
                    pm, pmk = pM.next()
                    P.op("pe", "transpose", reads=["impT", "identf"], writes=[pmk], out=pm[:, :NSLC], in_=impT[:NSLC, :], identity=identf[:NSLC, :NSLC])
                    P.op("dve", "tensor_tensor", reads=[pmk, "FT"], writes=["score"], out=score[:, :NSLC], in0=pm[:, :NSLC],
                         in1=FT[:, qt * NSLC:(qt + 1) * NSLC], op=ALU.add)
                    P.op("dve", "max", reads=["score"], writes=["m8"], out=m8[:, 0:8], in_=score[:, :NSLC])
                    P.op("dve", "match_replace", reads=["m8", "score"], writes=["score2"], out=sc2[:, :NSLC], in_to_replace=m8[:, 0:8],
                         in_values=score[:, :NSLC], imm_value=-3e30)
                    P.op("dve", "max", reads=["score2"], writes=["m8"], out=m8[:, 8:16], in_=sc2[:, :NSLC])
                    P.op("dve", "tensor_reduce", reads=["m8"], writes=["thr"], out=thr[:], in_=m8[:, 8:16], axis=AX.X, op=ALU.min)
                    P.op("dve", "tensor_scalar", reads=["FT"], writes=["vm"], out=vm[:, :NSLC], in0=FT[:, qt * NSLC:(qt + 1) * NSLC], scalar1=-1.0,
                         scalar2=None, op0=ALU.is_ge)
                    P.op("dve", "scalar_tensor_tensor", reads=["score", "thr", "vm"], writes=["selb"], out=selb[:, :NSLC], in0=score[:, :NSLC],
                         scalar=thr[:, 0:1], in1=vm[:, :NSLC], op0=ALU.is_ge, op1=ALU.mult)
                    pm, pmk = pM.next()
                    P.op("pe", "transpose", reads=["selb", "identf"], writes=[pmk], out=pm[:NSLC, 0:128], in_=selb[:, :NSLC], identity=identf[:, :])
                    P.op("act", "activation", reads=[pmk], writes=["selT"], out=selT[:NSLC, :], in_=pm[:NSLC, 0:128], func=AF.Copy)
                    for k0 in range(0, qt + 1, 4):
                        kn = min(4, qt + 1 - k0)
                        pm, pmk = pM.next()
                        for kk in range(kn):
                            kt = k0 + kk
                            P.op("pe", "matmul", reads=["Eb", "selT"], writes=[pmk], out=pm[:, kk * 128:(kk + 1) * 128], lhsT=Eb[:, kt * 128:(kt + 1) * 128],
                                 rhs=selT[:NSLC, :], start=True, stop=True)
                        if k0 + kn - 1 == qt:
                            if kn > 1:
                                P.op("dve", "tensor_copy", reads=[pmk], writes=["msk"], out=msk[:, k0:k0 + kn - 1, :],
                                     in_=pm[:, 0:(kn - 1) * 128].rearrange("p (a q) -> p a q", q=128))
                            P.op("dve", "tensor_tensor", reads=[pmk, "CAUS"], writes=["msk"], out=msk[:, qt, :], in0=pm[:, (kn - 1) * 128:kn * 128],
                                 in1=CAUS[:], op=ALU.mult)
                        else:
                            P.op("dve", "tensor_copy", reads=[pmk], writes=["msk"], out=msk[:, k0:k0 + kn, :],
                                 in_=pm[:, 0:kn * 128].rearrange("p (a q) -> p a q", q=128))
                    units = []
                    for kt in range(qt + 1):
                        def post_s(PT, PTk, nk, kt=kt):
                            P.op("pool", "tensor_tensor", reads=[PTk, "msk"], writes=[PTk], out=PT[:, :].rearrange("p (a q) -> p a q", a=4),
                                 in0=PT[:, :].rearrange("p (a q) -> p a q", a=4), in1=msk[:, kt:kt + 1, :].to_broadcast([128, 4, 128]), op=ALU.mult)
                        units.append(dict(lhsT=KN[:, G, kt * 128:(kt + 1) * 128], lkey="KN", nk=128, bias=BK3[:, 4 * G:4 * G + 4, qt - kt],
                                          post=post_s, v=VN[:, kt, G * 64:(G + 1) * 64], vkey="VN"))
                    pN, pNk = pNs.next()
                    pD, pDk = pDs.next()
                    run_branch(units, pN, pNk, pD, pDk)
                    finish_branch(pN, pNk, pD, pDk, G, 1, False)
                    P.op("act", "activation", reads=["oacc"], writes=["oTb"], out=oTb[:, 4 * G:4 * G + 4, :],
                         in_=oacc[:].rearrange("p (a q) -> p a q", a=4), func=AF.Copy)
                P.dma(OT_d[:, :, tq0:tq0 + 128], oTb[:], reads=["oTb"], writes=["OT_d%d" % tq0])

    def phase_mixC_out(self, l, j):
        P = self.P
        L, NS = self.L, self.NS
        NQT = L // 128
        OT_d = self.OT_d
        with self.phase("c4_%d" % l) as ph:
            stg = Rot([ph.sbuf("stg%d" % i, [64, 16, 512], F32) for i in range(2)], ph.name + "stg")
            wo = ph.sbuf("wo", [64, 16, D], BF16)
            wv = self.w["w_c_o"][j].rearrange("(h p) n -> p h n", p=64)
            for n0 in range(0, D, 512):
                s, sk = stg.next()
                P.dma(s[:], wv[:, :, n0:n0 + 512], writes=[sk])
                self.cast(wo[:, :, n0:n0 + 512], s[:], [sk], ["wo"])
            xts = Rot([ph.sbuf("xt%d" % i, [128, 1, D], F32) for i in range(2)], ph.name + "xt")
            oTs = Rot([ph.sbuf("oT%d" % i, [64, 16, 128], BF16) for i in range(2)], ph.name + "oT")
            pos = Rot([ph.psum("po%d" % i, [128, 512], F32) for i in range(2)], ph.name + "po")
            groups = [("p", qt * 128, 128) for qt in range(NQT)]
            if self.nsa_sample:
                groups.append(("s", 0, NS))
            for kind, r0, n in groups:
                xt, xtk = xts.next()
                self.load_x(xt, xtk, kind, r0, n)
                oT, oTk = oTs.next()
                if kind == "p":
                    P.dma(oT[:], OT_d[:, :, r0:r0 + n], reads=["OT_d%d" % r0], writes=[oTk])
                else:
                    P.dma(oT[:, :, :n], self.OTS_d[:, :, :], reads=["OTS_d"], writes=[oTk])
                for nh in range(2):
                    po, pok = pos.next()
                    for h in range(16):
                        P.op("pe", "matmul", reads=[oTk, "wo"], writes=[pok], out=po[:n, :], lhsT=oT[:, h, :n], rhs=wo[:, h, nh * 512:(nh + 1) * 512],
                             start=(h == 0), stop=(h == 15))
                    P.op("dve", "tensor_tensor", reads=[pok, xtk], writes=[xtk], out=xt[:n, 0, nh * 512:(nh + 1) * 512],
                         in0=xt[:n, 0, nh * 512:(nh + 1) * 512], in1=po[:n, :], op=ALU.add)
                self.store_x(xt, xtk, kind, r0, n)

    def idma(self, out, in_, idx_ap, reads, writes):
        P = self.P
        i = P.dma_rr
        P.dma_rr = (P.dma_rr + 1) % N_DMA_SEMS
        tr = ("dma", i)
        deps = P._deps(reads, writes)
        if P.dma_count[i] > 0:
            deps[tr] = max(deps.get(tr, 0), P.dma_count[i])
        waits = P._waits("pool", deps)
        P.dma_count[i] += 1
        P.streams["pool"].append(("idma", waits, dict(out=out, out_offset=None, in_=in_,
                                                      in_offset=bass.IndirectOffsetOnAxis(ap=idx_ap, axis=0)), P.dma_sems[i]))
        P._commit((tr, P.dma_count[i]), reads, writes)
        P.n_ops += 1

    def phase_mixC_sample(self, l, j):
        P = self.P
        NS = self.NS
        LP = 8192
        NPG = 64
        NCMP = 511
        NCT = 4
        pool = self.pool_rows
        for s in range(NS):
            with self.phase("cs1_%d_%d" % (l, s)) as ph:
                pti = ph.sbuf("pti", [128, NPG], I32)
                ptf = ph.sbuf("ptf", [128, NPG], F32)
                io = ph.sbuf("io", [128, 1], F32)
                idx = ph.sbuf("idx", [128, NPG], I32)
                P.dma(pti[:], self.page_table[s].partition_broadcast(128), writes=["pti"])
                P.op("pool", "iota", writes=["io"], out=io[:], pattern=[[0, 1]], base=0, channel_multiplier=1, allow_small_or_imprecise_dtypes=True)
                P.op("dve", "tensor_copy", reads=["pti"], writes=["ptf"], out=ptf[:], in_=pti[:])
                P.op("dve", "tensor_scalar", reads=["ptf", "io"], writes=["ptf"], out=ptf[:], in0=ptf[:], scalar1=128.0, scalar2=io[:, 0:1],
                     op0=ALU.mult, op1=ALU.add)
                P.op("dve", "tensor_copy", reads=["ptf"], writes=["idx"], out=idx[:], in_=ptf[:])
                pgs = Rot([ph.sbuf("pg%d" % i, [128, 1024], F32) for i in range(3)], ph.name + "pg")
                pgds = Rot([ph.sbuf("pgd%d" % i, [128, 8, 128], BF16) for i in range(2)], ph.name + "pgd")
                pkss = Rot([ph.sbuf("pks%d" % i, [128, 256], BF16) for i in range(2)], ph.name + "pks")
                vss = Rot([ph.sbuf("vs%d" % i, [128, 256], BF16) for i in range(2)], ph.name + "vs")
                c2s = Rot([ph.sbuf("c2s%d" % i, [128, 8, 128], BF16) for i in range(2)], ph.name + "c2s")
                kts = Rot([ph.sbuf("kts%d" % i, [64, 4, 128], BF16) for i in range(2)], ph.name + "kts")
                pTs = Rot([ph.psum("pT%d" % i, [128, 8, 128], BF16) for i in range(2)], ph.name + "pT")
                pKs = Rot([ph.psum("pK%d" % i, [64, 4, 128], BF16) for i in range(2)], ph.name + "pK")
                c2v = self.C2S_d.rearrange("(b p) t -> p b t", p=128)
                for lp in range(NPG):
                    pg, pgk = pgs.next()
                    self.idma(pg[:], pool, idx[:, lp:lp + 1], ["idx"], [pgk])
                    pgd, pgdk = pgds.next()
                    src = pg[:, 0:512].rearrange("p (b d) -> p b d", d=64)
                    P.op("dve", "tensor_copy", reads=[pgk], writes=[pgdk], out=pgd[:, :, 0:64], in_=src)
                    P.op("pool", "tensor_copy", reads=[pgk], writes=[pgdk], out=pgd[:, :, 64:128], in_=src)
                    pks, pksk = pkss.next()
                    P.op("act", "activation", reads=[pgk], writes=[pksk], out=pks[:], in_=pg[:, 512:768], func=AF.Copy)
                    vs_, vsk = vss.next()
                    P.op("act", "activation", reads=[pgk], writes=[vsk], out=vs_[:], in_=pg[:, 768:1024], func=AF.Copy)
                    pT, pTk = pTs.next()
                    for b8 in range(8):
                        P.op("pe", "transpose", reads=[pgdk, "ident"], writes=[pTk], out=pT[:, b8, :], in_=pgd[:, b8, :], identity=self.ident[:, :])
                    c2, c2k = c2s.next()
                    P.op("dve", "tensor_copy", reads=[pTk], writes=[c2k], out=c2[:], in_=pT[:])
                    pK, pKk = pKs.next()
                    for G in range(4):
                        P.op("pe", "transpose", reads=[pksk, "ident"], writes=[pKk], out=pK[:, G, :], in_=pks[:, G * 64:(G + 1) * 64], identity=self.ident[:, :])
                    kt_, ktk = kts.next()
                    P.op("act", "activation", reads=[pKk], writes=[ktk], out=kt_[:], in_=pK[:], func=AF.Copy)
                    t0 = lp * 128
                    P.dma(c2v[0:64, :, t0:t0 + 128], c2[0:64, :, :], reads=[c2k], writes=["C2S_d"])
                    if lp == 0:
                        P.dma(c2v[64:128, :, 0:127], c2[64:128, :, 1:128], reads=[c2k], writes=["C2S_d"])
                    else:
                        P.dma(c2v[64:128, :, t0 - 1:t0 + 127], c2[64:128, :, :], reads=[c2k], writes=["C2S_d"])
                    P.dma(self.KSS_d[:, :, t0:t0 + 128], kt_[:], reads=[ktk], writes=["KSS_d"])
                    P.dma(self.VSS_d[t0:t0 + 128, :], vs_[:], reads=[vsk], writes=["VSS_d"])
            with self.phase("cs2_%d_%d" % (l, s)) as ph:
                stg = Rot([ph.sbuf("stg%d" % i, [128, 16, 128], F32) for i in range(2)], ph.name + "stg")
                w1 = []
                for ti, nm in enumerate(("w_c_k1", "w_c_v1")):
                    w1.append(self.load_weight(ph, "w1_%d" % ti, self.w[nm][j], 2048, stg, ncols=128, CH=128, KCH=16))
                w2, posb = [], []
                pw = Rot([ph.sbuf("pw%d" % i, [128, 16], F32) for i in range(2)], ph.name + "pw")
                for ti, (nm, pn) in enumerate((("w_c_k2", "c_pos_k"), ("w_c_v2", "c_pos_v"))):
                    sg, sk = stg.next()
                    P.dma(sg[:, 0, 0:64], self.w[nm][j], writes=[sk])
                    wb2 = ph.sbuf("w2_%d" % ti, [128, 64], BF16)
                    P.op("dve", "tensor_copy", reads=[sk], writes=["w2_%d" % ti], out=wb2[:], in_=sg[:, 0, 0:64])
                    w2.append((wb2, "w2_%d" % ti))
                    pf, pfk = pw.next()
                    P.dma(pf[:], self.w[pn][j].rearrange("(k a) d -> (a d) k", a=2), writes=[pfk], allow_slow_non_contiguous=True)
                    pb_ = ph.sbuf("posb%d" % ti, [128, 16], BF16)
                    P.op("dve", "tensor_copy", reads=[pfk], writes=["posb%d" % ti], out=pb_[:], in_=pf[:])
                    posb.append((pb_, "posb%d" % ti))
                tmp = {"x": Rot([ph.sbuf("gx%d" % i, [128, 128], F32) for i in range(2)], ph.name + "gx"),
                       "u": Rot([ph.sbuf("gu%d" % i, [128, 128], F32) for i in range(2)], ph.name + "gu")}
                php = Rot([ph.psum("php%d" % i, [128, 512], F32) for i in range(1)], ph.name + "php")
                pco = Rot([ph.psum("pco%d" % i, [128, 512], F32) for i in range(1)], ph.name + "pco")
                pss = Rot([ph.psum("ps%d" % i, [128, 512], F32) for i in range(2)], ph.name + "ps")
                pN = ph.psum("pN", [128, 512], F32)
                pD = ph.psum("pD", [128, 512], F32)
                pM = Rot([ph.psum("pM0", [128, 512], F32)], ph.name + "pM")
                hbs = []
                for ti in range(2):
                    pc, pck = pco.next()
                    for kc in range(16):
                        P.op("pe", "matmul", reads=[w1[ti][1], posb[ti][1]], writes=[pck], out=pc[:, 0:1], lhsT=w1[ti][0][:, kc, :],
                             rhs=posb[ti][0][:, kc:kc + 1], start=(kc == 0), stop=(kc == 15))
                    hb = ph.sbuf("hb%d" % ti, [128, 1], F32)
                    P.op("dve", "tensor_copy", reads=[pck], writes=["hb"], out=hb[:], in_=pc[:, 0:1])
                    hbs.append(hb)
                X2s = Rot([ph.sbuf("X2_%d" % i, [128, LP], BF16) for i in range(1)], ph.name + "X2")
                gels = Rot([ph.sbuf("gel%d" % i, [128, 128], BF16) for i in range(2)], ph.name + "gel")
                KC = ph.sbuf("KC", [64, 4, 512], BF16)
                VC = ph.sbuf("VC", [128, NCT, 4, 64], BF16)
                for ti in range(2):
                    for G in range(4):
                        X2, X2k = X2s.next()
                        b8 = ti * 4 + G
                        P.dma(X2[:], self.C2S_d[b8 * 128:(b8 + 1) * 128, 0:LP], reads=["C2S_d"], writes=[X2k])
                        for ct in range(NCT):
                            c0 = ct * 128
                            nk = min(128, NCMP - c0)
                            hp, hpk = php.next()
                            for kc in range(16):
                                st0 = 16 * c0 + 2 * kc
                                P.op("pe", "matmul", reads=[w1[ti][1], X2k], writes=[hpk], out=hp[:, :nk], lhsT=w1[ti][0][:, kc, :],
                                     rhs=X2[:, st0:st0 + 16 * (nk - 1) + 1:16], start=(kc == 0), stop=(kc == 15))
                            gel, gelk = gels.next()
                            self.gelu_tanh(ph, tmp, hp[:, :nk], hpk, hbs[ti][:, 0:1], nk, gel[:, :nk], gelk)
                            pc, pck = pco.next()
                            if ti == 0:
                                P.op("pe", "matmul", reads=[gelk, w2[0][1]], writes=[pck], out=pc[:64, :nk], lhsT=w2[0][0][:, :], rhs=gel[:, :nk],
                                     start=True, stop=True)
                                P.op("act", "activation", reads=[pck], writes=["KC"], out=KC[:, G, c0:c0 + nk], in_=pc[:64, :nk], func=AF.Copy)
                            else:
                                P.op("pe", "matmul", reads=[gelk, w2[1][1]], writes=[pck], out=pc[:nk, :64], lhsT=gel[:, :nk], rhs=w2[1][0][:, :],
                                     start=True, stop=True)
                                P.op("act", "activation", reads=[pck], writes=["VC"], out=VC[:nk, ct, G, :], in_=pc[:nk, :64], func=AF.Copy)
                def cload(name, src, shape2, dt=F32):
                    t = ph.sbuf(name, shape2, F32)
                    P.dma(t[:], src, writes=[name])
                    if dt == F32:
                        return t
                    d = ph.sbuf(name + "b", shape2, BF16)
                    self.cast(d[:], t[:], [name], [name + "b"])
                    return d
                BSC = cload("BSC", self.c_BSC, [128, 4 * 16])
                BSS = cload("BSS", self.c_BSSEL, [128, 64 * 16])
                BSW = cload("BSW", self.c_BSWIN, [128, 4 * 16])
                COVS = cload("COVS", self.c_COVS, [128, 4 * 128], BF16)
                FTS = cload("FTS", self.c_FTS, [4, 129])
                E2 = ph.sbuf("E2", [128, 8192], BF16)
                for c0 in range(0, 8192, 2048):
                    sg, sk = stg.next()
                    P.dma(sg[:].rearrange("p a b -> p (a b)"), self.c_E2[:, c0:c0 + 2048], writes=[sk])
                    self.cast(E2[:, c0:c0 + 2048], sg[:].rearrange("p a b -> p (a b)"), [sk], ["E2"])
                SEL = ph.sbuf("SELG", [48, 48 * 64], BF16)
                for c0 in range(0, 48 * 64, 1536):
                    sg, sk = stg.next()
                    P.dma(sg[:48].rearrange("p a b -> p (a b)")[:, 0:1536], self.c_SELG[:, c0:c0 + 1536], writes=[sk])
                    self.cast(SEL[:, c0:c0 + 1536], sg[:48].rearrange("p a b -> p (a b)")[:, 0:1536], [sk], ["SELG"])
                QS = ph.sbuf("QS", [64, 16, NS], BF16)
                P.dma(QS[:], self.QNS_d, reads=["QNS_d"], writes=["QS"])
                KNn = ph.sbuf("KNn", [64, 8, NS], BF16)
                P.dma(KNn[:], self.KNS_d, reads=["KNS_d"], writes=["KNn"])
                GTs = ph.sbuf("GTs", [48, NS], BF16)
                P.dma(GTs[:], self.GTS_d, reads=["GTS_d"], writes=["GTs"])
                vrow = ph.sbuf("vrow", [1, 512], BF16)
                P.dma(vrow[:], self.VNS_d[s:s + 1, :], reads=["VNS_d"], writes=["vrow"])
                q4 = ph.sbuf("q4", [64, 4, 4], BF16)
                P.op("dve", "tensor_copy", reads=["QS"], writes=["q4"], out=q4[:].rearrange("p g i -> p (g i)"), in_=QS[:, :, s])
                KSS = ph.sbuf("KSS", [64, 4, LP], BF16)
                for c0 in range(0, LP, 2048):
                    P.dma(KSS[:, :, c0:c0 + 2048], self.KSS_d[:, :, c0:c0 + 2048], reads=["KSS_d"], writes=["KSS"])
                VSS = ph.sbuf("VSS", [128, NPG, 256], BF16)
                P.dma(VSS[:], self.VSS_d.rearrange("(t p) f -> p t f", p=128), reads=["VSS_d"], writes=["VSS"])
                wf = ph.sbuf("wf", [128, 4, 512], F32)
                P.dma(wf[:], self.cache_nsa_win[s].rearrange("(t p) f -> p t f", p=128), writes=["wf"])
                wb = ph.sbuf("wb", [128, 4, 512], BF16)
                self.cast(wb[:], wf[:], ["wf"], ["wb"])
                KW = ph.sbuf("KW", [64, 4, 512], BF16)
                pTw = ph.psum("pTw", [64, 4, 128], BF16)
                for kt in range(4):
                    for G in range(4):
                        P.op("pe", "transpose", reads=["wb", "ident"], writes=["pTw"], out=pTw[:, G, :], in_=wb[:, kt, G * 64:(G + 1) * 64],
                             identity=self.ident[:, :])
                    P.op("act", "activation", reads=["pTw"], writes=["KW"], out=KW[:, :, kt * 128:(kt + 1) * 128], in_=pTw[:, :, :], func=AF.Copy)
                t32 = Rot([ph.sbuf("t32_%d" % i, [128, 32], F32) for i in range(2)], ph.name + "t32")
                PTs = Rot([ph.sbuf("PT%d" % i, [128, 32], BF16) for i in range(3)], ph.name + "PT")
                impT4 = ph.sbuf("impT4", [128, 4], F32)
                rdc = ph.sbuf("rdc", [128, 4], F32)
                impn = ph.sbuf("impn", [128, 4], F32)
                score = ph.sbuf("score", [4, 136], F32)
                sc2 = ph.sbuf("score2", [4, 136], F32)
                m8 = ph.sbuf("m8", [4, 16], F32)
                thr = ph.sbuf("thr", [4, 1], F32)
                self_ = ph.sbuf("self", [4, 128], F32)
                selT = ph.sbuf("selT", [128, 4], BF16)
                mks = ph.sbuf("mks", [128, 64, 4], F32)
                oacc = ph.sbuf("oacc", [64, 4, 4], F32)
                nsum = ph.sbuf("nsum", [64, 3, 4, 4], F32)
                rden = ph.sbuf("rden", [64, 3, 4, 4], F32)
                gbs = ph.sbuf("gbs", [64, 3, 4, 4], F32)
                oTs = ph.sbuf("oTs", [64, 16], BF16)
                for G in range(4):
                    for ct in range(NCT):
                        nk = min(128, NCMP - ct * 128)
                        ps, psk = pss.next()
                        P.op("pe", "matmul", reads=["KC", "q4"], writes=[psk], out=ps[:nk, 0:4], lhsT=KC[:, G, ct * 128:ct * 128 + nk], rhs=q4[:, G, :],
                             start=True, stop=True)
                        tt, ttk = t32.next()
                        P.op("dve", "tensor_tensor", reads=[psk, "BSC"], writes=[ttk], out=tt[:nk, 0:4], in0=ps[:nk, 0:4],
                             in1=BSC[:nk, ct * 16 + 4 * G:ct * 16 + 4 * G + 4], op=ALU.add)
                        PT, PTk = PTs.next()
                        P.op("act", "activation", reads=[ttk], writes=[PTk], out=PT[:nk, 0:4], in_=tt[:nk, 0:4], func=AF.Exp)
                        fl = dict(start=(ct == 0), stop=(ct == NCT - 1))
                        P.op("pe", "matmul", reads=["VC", PTk], writes=["pN"], out=pN[:64, 0:4], lhsT=VC[:nk, ct, G, :], rhs=PT[:nk, 0:4], **fl)
                        P.op("pe", "matmul", reads=["ones", PTk], writes=["pD"], out=pD[:, 0:4], lhsT=self.ones[:nk, :], rhs=PT[:nk, 0:4], **fl)
                        pm, pmk = pM.next() if ct == 0 else (pm, pmk)
                        P.op("pe", "matmul", reads=["COVSb", PTk], writes=[pmk], out=pm[:, 0:4], lhsT=COVS[:nk, ct * 128:(ct + 1) * 128], rhs=PT[:nk, 0:4], **fl)
                    P.op("dve", "tensor_copy", reads=["pN"], writes=["nsum"], out=nsum[:, 0, G, :], in_=pN[:64, 0:4])
                    P.op("dve", "reciprocal", reads=["pD"], writes=["rdc"], out=rdc[:], in_=pD[:, 0:4])
                    P.op("dve", "tensor_copy", reads=["rdc"], writes=["rden"], out=rden[:, 0, G, :], in_=rdc[:64, :])
                    P.op("dve", "tensor_tensor", reads=[pmk, "rdc"], writes=["impn"], out=impn[:], in0=pm[:, 0:4], in1=rdc[:], op=ALU.mult)
                    P.op("dve", "tensor_reduce", reads=["impn"], writes=["impT4"], out=impT4[:, G:G + 1], in_=impn[:], axis=AX.X, op=ALU.add)
                pm, pmk = pM.next()
                P.op("pe", "transpose", reads=["impT4", "identf"], writes=[pmk], out=pm[:4, 0:128], in_=impT4[:, :], identity=self.identf[:, :])
                P.op("dve", "tensor_copy", reads=["FTS"], writes=["score"], out=score[:, 0:129], in_=FTS[:, :])
                P.op("dve", "tensor_tensor", reads=[pmk, "score"], writes=["score"], out=score[:, 0:128], in0=pm[:4, 0:128], in1=score[:, 0:128], op=ALU.add)
                P.op("dve", "max", reads=["score"], writes=["m8"], out=m8[:, 0:8], in_=score[:, 0:129])
                P.op("dve", "match_replace", reads=["m8", "score"], writes=["score2"], out=sc2[:, 0:129], in_to_replace=m8[:, 0:8],
                     in_values=score[:, 0:129], imm_value=-3e30)
                P.op("dve", "max", reads=["score2"], writes=["m8"], out=m8[:, 8:16], in_=sc2[:, 0:129])
                P.op("dve", "tensor_reduce", reads=["m8"], writes=["thr"], out=thr[:], in_=m8[:, 8:16], axis=AX.X, op=ALU.min)
                P.op("dve", "tensor_scalar", reads=["score", "thr"], writes=["self"], out=self_[:], in0=score[:, 0:128], scalar1=thr[:, 0:1], scalar2=None,
                     op0=ALU.is_ge)
                pm, pmk = pM.next()
                P.op("pe", "transpose", reads=["self", "identf"], writes=[pmk], out=pm[:, 0:4], in_=self_[:, :], identity=self.identf[:4, :4])
                P.op("act", "activation", reads=[pmk], writes=["selT"], out=selT[:], in_=pm[:, 0:4], func=AF.Copy)
                pm, pmk = pM.next()
                for kt in range(NPG):
                    P.op("pe", "matmul", reads=["E2", "selT"], writes=[pmk], out=pm[:, kt * 4:kt * 4 + 4], lhsT=E2[:, kt * 128:(kt + 1) * 128], rhs=selT[:, :],
                         start=True, stop=True)
                P.op("dve", "tensor_copy", reads=[pmk], writes=["mks"], out=mks[:].rearrange("p k g -> p (k g)"), in_=pm[:, 0:256])
                KB = 8
                for G in range(4):
                    nacc = 0
                    for k0 in range(0, NPG, KB):
                        ps, psk = pss.next()
                        for kk in range(KB):
                            kt = k0 + kk
                            P.op("pe", "matmul", reads=["KSS", "q4"], writes=[psk], out=ps[:, kk * 4:kk * 4 + 4], lhsT=KSS[:, G, kt * 128:(kt + 1) * 128],
                                 rhs=q4[:, G, :], start=True, stop=True)
                        tt, ttk = t32.next()
                        bias = BSS[:, :].rearrange("p (k h) -> p k h", h=16)[:, k0:k0 + KB, 4 * G:4 * G + 4]
                        P.op("dve", "tensor_tensor", reads=[psk, "BSS"], writes=[ttk], out=tt[:, :].rearrange("p (k h) -> p k h", h=4),
                             in0=ps[:, 0:4 * KB].rearrange("p (k h) -> p k h", h=4), in1=bias, op=ALU.add)
                        PT, PTk = PTs.next()
                        P.op("act", "activation", reads=[ttk], writes=[PTk], out=PT[:, :], in_=tt[:, :], func=AF.Exp)
                        P.op("pool", "tensor_tensor", reads=[PTk, "mks"], writes=[PTk], out=PT[:, :].rearrange("p (k h) -> p k h", h=4),
                             in0=PT[:, :].rearrange("p (k h) -> p k h", h=4), in1=mks[:, k0:k0 + KB, G:G + 1].to_broadcast([128, KB, 4]), op=ALU.mult)
                        for kk in range(KB):
                            kt = k0 + kk
                            P.op("pe", "matmul", reads=["VSS", PTk], writes=["pN"], out=pN[:64, 0:4], lhsT=VSS[:, kt, G * 64:(G + 1) * 64],
                                 rhs=PT[:, kk * 4:kk * 4 + 4], start=(nacc == 0), stop=False)
                            P.op("pe", "matmul", reads=["ones", PTk], writes=["pD"], out=pD[:64, 0:4], lhsT=self.ones[:, 0:64], rhs=PT[:, kk * 4:kk * 4 + 4],
                                 start=(nacc == 0), stop=False)
                            nacc += 1
                    ps, psk = pss.next()
                    P.op("pe", "matmul", reads=["KNn", "q4"], writes=[psk], out=ps[0:1, 0:4], lhsT=KNn[:, G, s:s + 1], rhs=q4[:, G, :], start=True, stop=True)
                    PT, PTk = PTs.next()
                    P.op("act", "activation", reads=[psk], writes=[PTk], out=PT[0:1, 0:4], in_=ps[0:1, 0:4], func=AF.Exp)
                    P.op("pe", "matmul", reads=["vrow", PTk], writes=["pN"], out=pN[:64, 0:4], lhsT=vrow[0:1, G * 64:(G + 1) * 64], rhs=PT[0:1, 0:4],
                         start=False, stop=True)
                    P.op("pe", "matmul", reads=["ones", PTk], writes=["pD"], out=pD[:64, 0:4], lhsT=self.ones[0:1, 0:64], rhs=PT[0:1, 0:4], start=False, stop=True)
                    P.op("dve", "tensor_copy", reads=["pN"], writes=["nsum"], out=nsum[:, 1, G, :], in_=pN[:64, 0:4])
                    P.op("dve", "reciprocal", reads=["pD"], writes=["rden"], out=rden[:, 1, G, :], in_=pD[:64, 0:4])
                    ps, psk = pss.next()
                    for kt in range(4):
                        P.op("pe", "matmul", reads=["KW", "q4"], writes=[psk], out=ps[:, kt * 4:kt * 4 + 4], lhsT=KW[:, G, kt * 128:(kt + 1) * 128],
                             rhs=q4[:, G, :], start=True, stop=True)
                    tt, ttk = t32.next()
                    bias = BSW[:, :].rearrange("p (k h) -> p k h", h=16)[:, :, 4 * G:4 * G + 4]
                    P.op("dve", "tensor_tensor", reads=[psk, "BSW"], writes=[ttk], out=tt[:, 0:16].rearrange("p (k h) -> p k h", h=4),
                         in0=ps[:, 0:16].rearrange("p (k h) -> p k h", h=4), in1=bias, op=ALU.add)
                    PT, PTk = PTs.next()
                    P.op("act", "activation", reads=[ttk], writes=[PTk], out=PT[:, 0:16], in_=tt[:, 0:16], func=AF.Exp)
                    for kt in range(4):
                        P.op("pe", "matmul", reads=["wb", PTk], writes=["pN"], out=pN[:64, 0:4], lhsT=wb[:, kt, 256 + G * 64:256 + (G + 1) * 64],
                             rhs=PT[:, kt * 4:kt * 4 + 4], start=(kt == 0), stop=False)
                        P.op("pe", "matmul", reads=["ones", PTk], writes=["pD"], out=pD[:64, 0:4], lhsT=self.ones[:, 0:64], rhs=PT[:, kt * 4:kt * 4 + 4],
                             start=(kt == 0), stop=False)
                    ps, psk = pss.next()
                    P.op("pe", "matmul", reads=["KNn", "q4"], writes=[psk], out=ps[0:1, 0:4], lhsT=KNn[:, 4 + G, s:s + 1], rhs=q4[:, G, :], start=True, stop=True)
                    PT, PTk = PTs.next()
                    P.op("act", "activation", reads=[psk], writes=[PTk], out=PT[0:1, 0:4], in_=ps[0:1, 0:4], func=AF.Exp)
                    P.op("pe", "matmul", reads=["vrow", PTk], writes=["pN"], out=pN[:64, 0:4], lhsT=vrow[0:1, 256 + G * 64:256 + (G + 1) * 64], rhs=PT[0:1, 0:4],
                         start=False, stop=True)
                    P.op("pe", "matmul", reads=["ones", PTk], writes=["pD"], out=pD[:64, 0:4], lhsT=self.ones[0:1, 0:64], rhs=PT[0:1, 0:4], start=False, stop=True)
                    P.op("dve", "tensor_copy", reads=["pN"], writes=["nsum"], out=nsum[:, 2, G, :], in_=pN[:64, 0:4])
                    P.op("dve", "reciprocal", reads=["pD"], writes=["rden"], out=rden[:, 2, G, :], in_=pD[:64, 0:4])
                pm, pmk = pM.next()
                for b in range(3):
                    for h in range(16):
                        r = h * 3 + b
                        P.op("pe", "matmul", reads=["SELG", "GTs"], writes=[pmk], out=pm[:64, b * 16 + h:b * 16 + h + 1], lhsT=SEL[:, r * 64:(r + 1) * 64],
                             rhs=GTs[:, s:s + 1], start=True, stop=True)
                P.op("dve", "tensor_copy", reads=[pmk], writes=["gbs"], out=gbs[:].rearrange("p b g i -> p (b g i)"), in_=pm[:64, 0:48])
                P.op("dve", "tensor_tensor", reads=["nsum", "rden"], writes=["nsum"], out=nsum[:].rearrange("p b g i -> p (b g i)"),
                     in0=nsum[:].rearrange("p b g i -> p (b g i)"), in1=rden[:].rearrange("p b g i -> p (b g i)"), op=ALU.mult)
                P.op("dve", "tensor_tensor", reads=["nsum", "gbs"], writes=["nsum"], out=nsum[:].rearrange("p b g i -> p (b g i)"),
                     in0=nsum[:].rearrange("p b g i -> p (b g i)"), in1=gbs[:].rearrange("p b g i -> p (b g i)"), op=ALU.mult)
                P.op("dve", "tensor_tensor", reads=["nsum"], writes=["oacc"], out=oacc[:].rearrange("p g i -> p (g i)"),
                     in0=nsum[:, 0].rearrange("p g i -> p (g i)"), in1=nsum[:, 1].rearrange("p g i -> p (g i)"), op=ALU.add)
                P.op("dve", "tensor_tensor", reads=["nsum", "oacc"], writes=["oacc"], out=oacc[:].rearrange("p g i -> p (g i)"),
                     in0=oacc[:].rearrange("p g i -> p (g i)"), in1=nsum[:, 2].rearrange("p g i -> p (g i)"), op=ALU.add)
                P.op("act", "activation", reads=["oacc"], writes=["oTs"], out=oTs[:], in_=oacc[:].rearrange("p g i -> p (g i)"), func=AF.Copy)
                P.dma(self.OTS_d[:, :, s], oTs[:], reads=["oTs"], writes=["OTS_d"], allow_slow_non_contiguous=True)

    def declare_io(self):
        L, NS = self.L, self.NS
        self.x_prompt = self.inp("x_prompt", [L, D])
        self.x_sample = self.inp("x_sample", [NS, D])
        self.mem_prompt = self.inp("mem_prompt", [MEM, D])
        self.cache_mem = self.inp("cache_mem_kv", [DEPTH, NS, MEM, 2 * D])
        self.w = {}
        for nm, shp in (("g_mix", [DEPTH, D]), ("g_cross", [DEPTH, D]), ("g_mem", [DEPTH, D]), ("g_ffn", [DEPTH, D]),
                        ("g_final", [D]), ("w_ffn_in", [DEPTH, D, 2 * FH]), ("w_ffn_out", [DEPTH, FH, D]),
                        ("w_x_q", [DEPTH, D, D]), ("w_x_kv", [DEPTH, D, 2 * D]), ("w_x_o", [DEPTH, D, D])):
            self.w[nm] = self.inp(nm, shp)
        for nm, shp in (("w_a_qkv", [2, D, 4608]), ("w_a_o", [2, 512, D])):
            self.w[nm] = self.inp(nm, shp)
        self.cache_dil128 = self.inp("cache_dil_w128", [2, NS, 128, D])
        self.cache_dil512 = self.inp("cache_dil_w512", [2, NS, 512, D])
        self.cache_dil2048 = self.inp("cache_dil_w2048", [2, NS, 2048, D])
        self.c_RA = self.inp("c_RA", [128, 24 * 128])
        self.c_R0 = self.inp("c_R0", [128, 24 * 128])
        self.c_MA = self.inp("c_MA", [128, 24 * 128])
        self.c_CBA = self.inp("c_CBA", [128, 24 * 17])
        self.c_CBS = self.inp("c_CBS", [128, 24])
        self.o_dil128_p = self.out("dil128_prompt", [2, min(128, L), D])
        self.o_dil512_p = self.out("dil512_prompt", [2, min(512, L), D])
        self.o_dil2048_p = self.out("dil2048_prompt", [2, min(2048, L), D])
        self.o_dil128_s = self.out("dil128_sample", [2, NS, D])
        self.o_dil512_s = self.out("dil512_sample", [2, NS, D])
        self.o_dil2048_s = self.out("dil2048_sample", [2, NS, D])
        self.QT_d = self.scratch("QT_d", [1536, L], BF16)
        self.KT_d = self.scratch("KT_d", [1536, L], BF16)
        self.V_d = self.scratch("V_d", [L, 1536], BF16)
        self.VS_d = self.scratch("VS_d", [NS, 1536], BF16)
        for nm, shp in (("w_b_in", [1, D, 3088]), ("w_b_gate2", [1, 16, 512]), ("b_b_gate", [1, 512]), ("g_b_head", [1, 4, 256]),
                        ("w_b_o", [1, D, D])):
            self.w[nm] = self.inp(nm, shp)
        self.state_gla = self.inp("state_gla", [NS, 4, 128, 256])
        self.c_TRI4 = self.inp("c_TRI4", [64, 256])
        self.o_gla_p = self.out("gla_prompt", [4, 128, 256])
        self.o_gla_s = self.out("gla_sample", [NS, 4, 128, 256])
        self.QK_d = self.scratch("QK_d", [1024, L])
        self.LA_d = self.scratch("LA_d", [L, 512])
        self.VB_d = self.scratch("VB_d", [L, 1024], BF16)
        self.SR_d = self.scratch("SR_d", [L, 1024])
        self.O_d = self.scratch("O_d", [L, 1024])
        self.QKS_d = self.scratch("QKS_d", [1024, NS])
        self.LAS_d = self.scratch("LAS_d", [NS, 512])
        self.VBS_d = self.scratch("VBS_d", [NS, 1024], BF16)
        self.SRS_d = self.scratch("SRS_d", [NS, 1024])
        self.OS_d = self.scratch("OS_d", [NS, 1024])
        for nm, shp in (("w_c_in", [1, D, 2608]), ("b_c_gate", [1, 48]), ("c_pos_k", [1, 32, 64]), ("c_pos_v", [1, 32, 64]),
                        ("w_c_k1", [1, 2048, 128]), ("w_c_k2", [1, 128, 64]), ("w_c_v1", [1, 2048, 128]), ("w_c_v2", [1, 128, 64]),
                        ("w_c_o", [1, D, D])):
            self.w[nm] = self.inp(nm, shp)
        NQT, NCT, NSLC = L // 128, (L // 16 - 1 + 127) // 128, L // 64
        self.c_BK = self.inp("c_BK", [128, 16 * 33])
        self.c_BKC = self.inp("c_BKC", [128, 16 * NQT * NCT])
        self.c_FT = self.inp("c_FT", [128, NQT * NSLC])
        self.c_E = self.inp("c_E", [NSLC, L])
        self.c_CAUS = self.inp("c_CAUS", [128, 128])
        self.c_COVER = self.inp("c_COVER", [128, NCT * NSLC])
        self.c_SELG = self.inp("c_SELG", [48, 48 * 64])
        self.c_BSC = self.inp("c_BSC", [128, 64])
        self.c_BSSEL = self.inp("c_BSSEL", [128, 1024])
        self.c_BSWIN = self.inp("c_BSWIN", [128, 64])
        self.c_COVS = self.inp("c_COVS", [128, 512])
        self.c_E2 = self.inp("c_E2", [128, 8192])
        self.c_FTS = self.inp("c_FTS", [4, 129])
        if self.nsa_sample:
            self.pool_rows = self.inp("cache_nsa_kv", [self.n_pool * 128, 1024])
            self.page_table = self.inp("page_table", [NS, 64], I32)
            self.cache_nsa_win = self.inp("cache_nsa_win", [NS, 512, 512])
        self.C2S_d = self.scratch("C2S_d", [1024, 8192], BF16)
        self.KSS_d = self.scratch("KSS_d", [64, 4, 8192], BF16)
        self.VSS_d = self.scratch("VSS_d", [8192, 256], BF16)
        self.o_nsa_win_p = self.out("nsa_win_prompt", [min(512, L), 512])
        self.o_nsa_win_s = self.out("nsa_win_sample", [NS, 512])
        self.o_nsa_kv_p = self.out("nsa_kv_prompt", [L, 1024])
        self.o_nsa_kv_s = self.out("nsa_kv_sample", [NS, 1024])
        self.QN_d = self.scratch("QN_d", [64, 16, L], BF16)
        self.KN_d = self.scratch("KN_d", [64, 8, L], BF16)
        self.C2_d = self.scratch("C2_d", [1024, L], BF16)
        self.VN_d = self.scratch("VN_d", [L, 512], BF16)
        self.GT_d = self.scratch("GT_d", [48, L], BF16)
        self.KC_d = self.scratch("KC_d", [64, 4, 256], BF16)
        self.VC_d = self.scratch("VC_d", [256, 4, 64], BF16)
        self.OT_d = self.scratch("OT_d", [64, 16, L], BF16)
        self.QNS_d = self.scratch("QNS_d", [64, 16, NS], BF16)
        self.KNS_d = self.scratch("KNS_d", [64, 8, NS], BF16)
        self.GTS_d = self.scratch("GTS_d", [48, NS], BF16)
        self.VNS_d = self.scratch("VNS_d", [NS, 512], BF16)
        self.OTS_d = self.scratch("OTS_d", [64, 16, NS], BF16)
        self.y_prompt = self.out("y_prompt", [L, D])
        self.y_sample = self.out("y_sample", [NS, D])
        self.o_mem = self.out("mem_kv_prompt", [DEPTH, MEM, 2 * D])
        self.X = self.scratch("X", [L, D])
        self.XS = self.scratch("XS", [NS, D])

    def build(self):
        self.declare_io()
        with ExitStack() as st:
            self.P = Prog(self.nc, st)
            self.setup_consts(st)
            self.phase_init(self.x_prompt, self.x_sample)
            for l in self.layers:
                if "mix" in self.parts and l % 3 == 0:
                    self.phase_mixA(l, l // 3)
                if "mix" in self.parts and l % 3 == 1:
                    self.phase_mixB(l, l // 3)
                if "mix" in self.parts and l % 3 == 2:
                    self.phase_mixC(l, l // 3)
                    if self.nsa_sample:
                        self.phase_mixC_sample(l, l // 3)
                    self.phase_mixC_out(l, l // 3)
                if "cross" in self.parts:
                    self.phase_cross(l)
                if "ffn" in self.parts:
                    self.phase_ffn(l)
            self.phase_final(self.y_prompt, self.y_sample)
            self.P.barrier()
            self.P.emit()
        return self.nc


def host_consts(L=4096):
    sl = (2.0 ** (-8.0 * np.arange(1, 25) / 24)).astype(np.float64)
    k = np.arange(128)[:, None]
    q = np.arange(128)[None, :]
    WIN = (128, 512, 2048)
    DIL = (1, 4, 16)
    RA = np.zeros((128, 24, 128), np.float64)
    R0 = np.zeros((128, 24, 128), np.float64)
    for h in range(24):
        RA[:, h, :] = -sl[h] * (q - k)
        R0[:, h, :] = np.where(q >= k, -sl[h] * (q - k), -30000.0)
    MA = np.zeros((128, 24, 128), np.float64)
    idx = 0
    for g in range(3):
        for dl in range(WIN[g] // 128 + 1):
            dist = 128 * dl + q - k
            MA[:, idx, :] = ((dist >= 0) & (dist <= WIN[g]) & (dist % DIL[g] == 0))
            idx += 1
    CBA = np.zeros((128, 24, 17), np.float64)
    for h in range(24):
        for dl in range(17):
            CBA[:, h, dl] = -sl[h] * 128 * dl
    CBS = np.zeros((128, 24), np.float64)
    for h in range(24):
        CBS[:, h] = -sl[h] * DIL[h // 8] * (128 - np.arange(128))
    f = lambda a: np.ascontiguousarray(a.reshape(128, -1)).astype(np.float32)
    NQT, NCMP, NSLC = L // 128, L // 16 - 1, L // 64
    NCT = (NCMP + 127) // 128
    slc = 2.0 ** (-8.0 * np.arange(1, 17) / 16)
    kk = np.arange(128)
    BK = np.zeros((128, 16, 33))
    for h in range(16):
        for dl in range(33):
            BK[:, h, dl] = slc[h] * (-128 * dl + kk - 64)
    BKC = np.zeros((128, 16, NQT, NCT))
    for h in range(16):
        for qt in range(NQT):
            for ct in range(NCT):
                BKC[:, h, qt, ct] = np.minimum(slc[h] * (16 * (128 * ct + kk) + 31 - (128 * qt + 64)), 60.0)
    FT = np.zeros((128, NQT, NSLC))
    jj = np.arange(NSLC)[None, :]
    for qt in range(NQT):
        t = (128 * qt + np.arange(128))[:, None]
        valid = 64 * jj <= t
        forced = (jj == 0) | (jj == t // 64) | (jj == t // 64 - 1)
        FT[:, qt, :] = np.where(valid, np.where(forced, 1e30, 0.0), -1e30)
    E = (np.arange(L)[None, :] // 64 == np.arange(NSLC)[:, None]).astype(np.float32)
    CAUS = (k <= q).astype(np.float32)
    COVER = np.zeros((128, NCT, NSLC))
    for ct in range(NCT):
        cs = 16 * (128 * ct + np.arange(128))[:, None]
        COVER[:, ct, :] = (cs < 64 * (jj + 1)) & (cs + 32 > 64 * jj)
    SELG = np.zeros((48, 48, 64))
    for r in range(48):
        SELG[r, r, :] = 1.0
    BSC = np.zeros((128, 4, 16)); BSS = np.zeros((128, 64, 16)); BSW = np.zeros((128, 4, 16))
    for h in range(16):
        for ct in range(4):
            BSC[:, ct, h] = np.minimum(slc[h] * (16 * (128 * ct + kk) + 31 - 8192), 0.0)
            BSW[:, ct, h] = slc[h] * (128 * ct + kk - 512)
        for kt in range(64):
            BSS[:, kt, h] = slc[h] * (128 * kt + kk - 8192)
    COVS = np.zeros((128, 4, 128))
    j128 = np.arange(128)[None, :]
    for ct in range(4):
        cs = 16 * (128 * ct + np.arange(128))[:, None]
        COVS[:, ct, :] = (cs < 64 * (j128 + 1)) & (cs + 32 > 64 * j128)
    E2 = np.zeros((128, 64, 128))
    for kt in range(64):
        E2[:, kt, :] = (np.arange(128)[:, None] == (2 * kt + np.arange(128)[None, :] // 64))
    FTS = np.zeros((4, 129)); FTS[:, [0, 127, 128]] = 1e30
    nsa_s = {"c_BSC": BSC, "c_BSSEL": BSS, "c_BSWIN": BSW, "c_COVS": COVS, "c_E2": E2, "c_FTS": FTS}
    nsa = {"c_BK": BK, "c_BKC": BKC, "c_FT": FT, "c_E": E, "c_CAUS": CAUS, "c_COVER": COVER, "c_SELG": SELG}
    nsa.update(nsa_s)
    nsa = {kk_: np.ascontiguousarray(v.reshape(v.shape[0], -1)).astype(np.float32) for kk_, v in nsa.items()}
    tri = (np.arange(64)[:, None] <= np.arange(64)[None, :]).astype(np.float32)
    tri4 = np.ascontiguousarray(np.tile(tri, (1, 4)))
    return {**nsa, "c_TRI4": tri4, "c_RA": f(RA), "c_R0": f(R0), "c_MA": f(MA), "c_CBA": f(CBA), "c_CBS": f(CBS)}


_CACHE = {}


def kernel(**inputs):
    L, NS = 4096, 4
    f32 = lambda a: np.ascontiguousarray(np.asarray(a), dtype=np.float32)
    if "nc" not in _CACHE:
        B = Builder(L=L, NS=NS)
        B.nsa_sample = True
        _CACHE["nc"] = B.build()
        _CACHE["B"] = B
    B, nc = _CACHE["B"], _CACHE["nc"]
    consts = host_consts(L)
    xp = f32(inputs["x_prompt"])
    xs = f32(inputs["x_sample"]).reshape(32, D)
    memp = f32(inputs["mem_prompt"])
    cmem = f32(inputs["cache_mem_kv"]).reshape(DEPTH, 32, MEM, 2 * D)
    cd = {n: f32(inputs[n]).reshape(2, 32, -1, D) for n in ("cache_dil_w128", "cache_dil_w512", "cache_dil_w2048")}
    shared = {n: f32(inputs[n]) for n in ("g_mix", "g_cross", "g_mem", "g_ffn", "g_final", "w_ffn_in", "w_ffn_out",
                                          "w_x_q", "w_x_kv", "w_x_o", "w_a_qkv", "w_a_o", "w_b_in", "w_b_gate2", "b_b_gate",
                                          "g_b_head", "w_b_o", "w_c_in", "b_c_gate", "c_pos_k", "c_pos_v", "w_c_k1", "w_c_k2",
                                          "w_c_v1", "w_c_v2", "w_c_o")}
    shared.update(consts)
    sgla = f32(inputs["state_gla"]).reshape(32, 4, 128, 256)
    pool = f32(inputs["cache_nsa_kv"]).reshape(-1, 1024)
    ptab = np.ascontiguousarray(np.asarray(inputs["page_table"]), dtype=np.int32)
    cwin = f32(inputs["cache_nsa_win"]).reshape(32, 512, 512)
    in_maps = []
    for c in range(NCORES):
        m = dict(shared)
        m["x_prompt"] = xp[c % 4]
        m["x_sample"] = np.ascontiguousarray(xs[4 * c:4 * c + 4])
        m["mem_prompt"] = memp[c % 4]
        m["cache_mem_kv"] = np.ascontiguousarray(cmem[:, 4 * c:4 * c + 4])
        for n in cd:
            m[n] = np.ascontiguousarray(cd[n][:, 4 * c:4 * c + 4])
        m["state_gla"] = np.ascontiguousarray(sgla[4 * c:4 * c + 4])
        m["cache_nsa_kv"] = pool
        m["page_table"] = np.ascontiguousarray(ptab[4 * c:4 * c + 4])
        m["cache_nsa_win"] = np.ascontiguousarray(cwin[4 * c:4 * c + 4])
        in_maps.append({k: m[k] for k in B.ins})
    res = run_bass_kernel_spmd(nc, in_maps, core_ids=list(range(NCORES))).results
    cat = lambda name, cores: np.stack([np.asarray(res[c][name]) for c in cores])
    y_prompt = cat("y_prompt", range(4))
    y_sample = np.concatenate([np.asarray(res[c]["y_sample"]) for c in range(8)]).reshape(32, 1, D)
    outs = [y_prompt, y_sample]
    for W in (128, 512, 2048):
        p = cat("dil%d_prompt" % W, range(4))
        outs.append(np.ascontiguousarray(p.transpose(1, 0, 2, 3)).reshape(2, 4, W, 2, 8, 64))
        s = np.concatenate([np.asarray(res[c]["dil%d_sample" % W]) for c in range(8)], axis=1)
        outs.append(s.reshape(2, 32, 1, 2, 8, 64))
    outs.append(cat("gla_prompt", range(4)).reshape(1, 4, 4, 128, 256))
    outs.append(np.concatenate([np.asarray(res[c]["gla_sample"]) for c in range(8)]).reshape(1, 32, 4, 128, 256))
    outs.append(cat("nsa_win_prompt", range(4)).reshape(1, 4, 512, 2, 4, 64))
    outs.append(np.concatenate([np.asarray(res[c]["nsa_win_sample"]) for c in range(8)]).reshape(1, 32, 1, 2, 4, 64))
    outs.append(cat("nsa_kv_prompt", range(4)).reshape(1, 4, 4096, 4, 4, 64))
    outs.append(np.concatenate([np.asarray(res[c]["nsa_kv_sample"]) for c in range(8)]).reshape(1, 32, 1, 4, 4, 64))
    mem = cat("mem_kv_prompt", range(4))
    outs.append(np.ascontiguousarray(mem.transpose(1, 0, 2, 3)).reshape(DEPTH, 4, MEM, 2, 4, 256))
    return tuple(outs)
```
